# Optimizing a Trainium2 kernel written in Bass

```python
import jax
import jax.numpy as jnp
from jax import lax
import numpy as np

D_MODEL = 1024
BATCH = 8
SEQ = 8192
DEPTH = 1

D_RNN = D_MODEL
N_LRU_BLOCKS = 16
LRU_BLOCK = D_RNN // N_LRU_BLOCKS
CONV_WIDTH = 4
LRU_C = 8.0
N_HEADS = 16
HEAD_DIM = 64
N_KV_GROUPS = 4
HEADS_PER_GROUP = N_HEADS // N_KV_GROUPS
D_ATTN = N_HEADS * HEAD_DIM
D_KV = N_KV_GROUPS * HEAD_DIM
CMP_LEN = 32
CMP_STRIDE = 16
CMP_HIDDEN = 256
SEL_LEN = 64
SEL_TOPK = 16
WINDOW = 512
Q_BLOCK = 64
FORCE_SCORE = 1e4
R_SEL = SEL_LEN // CMP_STRIDE
R_CMP = CMP_LEN // CMP_STRIDE
ALIBI_MAX_BIAS = 8.0
D_FF = 2816
D_PLE = 256
NORM_EPS = 1e-6

_IN_SIZES = (D_RNN, D_RNN, D_ATTN, D_KV, D_KV, D_KV, D_KV, D_KV, D_KV, 3 * N_HEADS, D_MODEL, D_MODEL)
D_IN = sum(_IN_SIZES)
IN_SPLITS = tuple(int(c) for c in np.cumsum(_IN_SIZES)[:-1])

kernel_name = "hybrid_rglru_nsa_sandwich_block"


def rmsnorm(x, g):
    x32 = x.astype(jnp.float32)
    y = x32 * lax.rsqrt(jnp.mean(x32 * x32, axis=-1, keepdims=True) + NORM_EPS)
    return (y * g.astype(jnp.float32)).astype(x.dtype)


def alibi_slopes():
    h = jnp.arange(1, N_HEADS + 1, dtype=jnp.float32)
    return jnp.exp2(-ALIBI_MAX_BIAS * h / N_HEADS)


def masked_softmax(s, mask):
    s = jnp.where(mask, s.astype(jnp.float32), -jnp.inf)
    m = jnp.max(s, axis=-1, keepdims=True)
    m = jnp.where(jnp.isfinite(m), m, 0.0)
    e = jnp.where(mask, jnp.exp(s - m), 0.0)
    return e / jnp.maximum(jnp.sum(e, axis=-1, keepdims=True), 1e-30)


def causal_conv(x, w, b):
    S = x.shape[1]
    xp = jnp.pad(x, ((0, 0), (CONV_WIDTH - 1, 0), (0, 0)))
    y = b
    for k in range(CONV_WIDTH):
        y = y + xp[:, k:k + S] * w[k]
    return y


def _lru_combine(c1, c2):
    a1, b1 = c1
    a2, b2 = c2
    return a1 * a2, a2 * b1 + b2


def rg_lru(xc, wa, ba, wx, bx, lam):
    B, S, _ = xc.shape
    xb = xc.reshape(B, S, N_LRU_BLOCKS, LRU_BLOCK)
    r = jax.nn.sigmoid(jnp.einsum('bsnd,nde->bsne', xb, wa).reshape(B, S, D_RNN) + ba)
    i = jax.nn.sigmoid(jnp.einsum('bsnd,nde->bsne', xb, wx).reshape(B, S, D_RNN) + bx)
    log_a = (-LRU_C * r.astype(jnp.float32)) * jax.nn.softplus(-lam.astype(jnp.float32))
    a = jnp.exp(log_a)
    b = jnp.sqrt(-jnp.expm1(2.0 * log_a)) * (i * xc).astype(jnp.float32)
    _, h = lax.associative_scan(_lru_combine, (a, b), axis=1)
    return h.astype(xc.dtype)


def compress(kv, pos, w1, w2):
    B, S, G, dh = kv.shape
    c = kv.reshape(B, S // CMP_STRIDE, CMP_STRIDE, G, dh)
    blk = jnp.concatenate([c[:, :-1], c[:, 1:]], axis=2)
    blk = blk + pos[:, None, :]
    nc = blk.shape[1]
    blk = blk.transpose(0, 1, 3, 2, 4).reshape(B, nc, G, CMP_LEN * dh)
    return jax.nn.gelu(blk @ w1) @ w2


def nsa(q, kc, vc, ks, vs, kw, vw, gates, slopes):
    B, S = q.shape[0], q.shape[1]
    G, R, dh = N_KV_GROUPS, HEADS_PER_GROUP, HEAD_DIM
    NQ = S // Q_BLOCK
    NB = S // SEL_LEN
    NC = kc.shape[1]
    topk = min(SEL_TOPK, NB)
    scale = HEAD_DIM ** -0.5
    slopes_g = slopes.reshape(G, R)[None, :, :, None, None]
    cmp_end = jnp.arange(NC) * CMP_STRIDE + (CMP_LEN - 1)
    kc_t = kc.transpose(0, 2, 1, 3)
    vc_t = vc.transpose(0, 2, 1, 3)
    ks_blk = ks.reshape(B, NB, SEL_LEN, G, dh).transpose(0, 3, 1, 2, 4)
    vs_blk = vs.reshape(B, NB, SEL_LEN, G, dh).transpose(0, 3, 1, 2, 4)
    kw_pad = jnp.pad(kw, ((0, 0), (WINDOW, 0), (0, 0), (0, 0)))
    vw_pad = jnp.pad(vw, ((0, 0), (WINDOW, 0), (0, 0), (0, 0)))
    q_blocks = q.reshape(B, NQ, Q_BLOCK, G, R, dh).transpose(1, 0, 2, 3, 4, 5)
    g_blocks = gates.reshape(B, NQ, Q_BLOCK, G, R, 3).transpose(1, 0, 2, 3, 4, 5)
    gather = jax.vmap(jax.vmap(lambda kb, idx: kb[idx]))
    blk_ids = jnp.arange(NB)

    def one_block(args):
        qb, qi, gi = args
        q0 = qb * Q_BLOCK
        t = q0 + jnp.arange(Q_BLOCK)
        qf = qi * scale
        dist_c = t[:, None] - cmp_end[None, :]
        s = jnp.einsum('bqgrd,bgcd->bgrqc', qf, kc_t) - slopes_g * dist_c
        p_cmp = masked_softmax(s, dist_c >= 0)
        o_cmp = jnp.einsum('bgrqc,bgcd->bqgrd', p_cmp.astype(vc_t.dtype), vc_t)
        imp = jnp.sum(p_cmp, axis=2)
        left = R_CMP - 1
        imp = jnp.pad(imp, ((0, 0), (0, 0), (0, 0), (left, R_SEL * NB - NC)))
        imp_slc = jnp.zeros(imp.shape[:3] + (NB,), jnp.float32)
        for m in range(R_SEL):
            for n in range(R_CMP):
                st = m - n + left
                imp_slc = imp_slc + imp[..., st:st + R_SEL * NB:R_SEL]
        cur = t // SEL_LEN
        valid = blk_ids[None, :] * SEL_LEN <= t[:, None]
        forced = (blk_ids[None, :] == 0) | (blk_ids[None, :] == cur[:, None]) | (blk_ids[None, :] == cur[:, None] - 1)
        score = jnp.where(valid, imp_slc, -1.0)
        score = jnp.where(forced & valid, FORCE_SCORE, score)
        _, idx = lax.top_k(score, topk)
        k_sel = gather(ks_blk, idx).reshape(B, G, Q_BLOCK, topk * SEL_LEN, dh)
        v_sel = gather(vs_blk, idx).reshape(B, G, Q_BLOCK, topk * SEL_LEN, dh)
        pos = (idx[..., None] * SEL_LEN + jnp.arange(SEL_LEN)).reshape(B, G, Q_BLOCK, topk * SEL_LEN)
        dist_s = t[None, None, :, None] - pos
        s = jnp.einsum('bqgrd,bgqkd->bgrqk', qf, k_sel) - slopes_g * dist_s[:, :, None]
        p_sel = masked_softmax(s, (dist_s >= 0)[:, :, None])
        o_slc = jnp.einsum('bgrqk,bgqkd->bqgrd', p_sel.astype(v_sel.dtype), v_sel)
        kwin = lax.dynamic_slice_in_dim(kw_pad, q0, WINDOW + Q_BLOCK, axis=1)
        vwin = lax.dynamic_slice_in_dim(vw_pad, q0, WINDOW + Q_BLOCK, axis=1)
        pos_w = q0 - WINDOW + jnp.arange(WINDOW + Q_BLOCK)
        dist_w = t[:, None] - pos_w[None, :]
        mask_w = (dist_w >= 0) & (dist_w < WINDOW) & (pos_w[None, :] >= 0)
        s = jnp.einsum('bqgrd,bkgd->bgrqk', qf, kwin) - slopes_g * dist_w
        p_win = masked_softmax(s, mask_w)
        o_win = jnp.einsum('bgrqk,bkgd->bqgrd', p_win.astype(vwin.dtype), vwin)
        g = jax.nn.sigmoid(gi)
        return g[..., 0:1] * o_cmp + g[..., 1:2] * o_slc + g[..., 2:3] * o_win

    o = lax.map(one_block, (jnp.arange(NQ), q_blocks, g_blocks))
    return o.transpose(1, 0, 2, 3, 4, 5).reshape(B, S, D_ATTN)


def setup_inputs(seed: int = 0) -> dict:
    key = jax.random.key(seed)
    ks = jax.random.split(key, 32)
    f32 = jnp.float32

    def nrm(k, shape, fan_in):
        return jax.random.normal(k, shape, f32) * (fan_in ** -0.5)

    def gain(k):
        return 1.0 + 0.05 * jax.random.normal(k, (DEPTH, D_MODEL), f32)

    u = jax.random.uniform(ks[10], (DEPTH, D_RNN), f32, minval=0.9, maxval=0.999)
    s = u ** (1.0 / LRU_C)
    lam = jnp.log(s) - jnp.log1p(-s)
    return {
        'x': jax.random.normal(ks[0], (BATCH, SEQ, D_MODEL), f32),
        'p': jax.random.normal(ks[1], (DEPTH, BATCH, SEQ, D_PLE), f32),
        'norm_mix_pre': gain(ks[2]),
        'norm_mix_post': gain(ks[3]),
        'w_in': nrm(ks[4], (DEPTH, D_MODEL, D_IN), D_MODEL),
        'conv_w': nrm(ks[5], (DEPTH, CONV_WIDTH, D_RNN), CONV_WIDTH),
        'conv_b': 0.02 * jax.random.normal(ks[6], (DEPTH, D_RNN), f32),
        'lru_wa': nrm(ks[7], (DEPTH, N_LRU_BLOCKS, LRU_BLOCK, LRU_BLOCK), LRU_BLOCK),
        'lru_ba': 0.02 * jax.random.normal(ks[8], (DEPTH, D_RNN), f32),
        'lru_wx': nrm(ks[9], (DEPTH, N_LRU_BLOCKS, LRU_BLOCK, LRU_BLOCK), LRU_BLOCK),
        'lru_bx': 0.02 * jax.random.normal(ks[11], (DEPTH, D_RNN), f32),
        'lru_lambda': lam,
        'cmp_pos_k': 0.02 * jax.random.normal(ks[12], (DEPTH, CMP_LEN, HEAD_DIM), f32),
        'cmp_pos_v': 0.02 * jax.random.normal(ks[13], (DEPTH, CMP_LEN, HEAD_DIM), f32),
        'cmp_k_w1': nrm(ks[14], (DEPTH, CMP_LEN * HEAD_DIM, CMP_HIDDEN), CMP_LEN * HEAD_DIM),
        'cmp_k_w2': nrm(ks[15], (DEPTH, CMP_HIDDEN, HEAD_DIM), CMP_HIDDEN),
        'cmp_v_w1': nrm(ks[16], (DEPTH, CMP_LEN * HEAD_DIM, CMP_HIDDEN), CMP_LEN * HEAD_DIM),
        'cmp_v_w2': nrm(ks[17], (DEPTH, CMP_HIDDEN, HEAD_DIM), CMP_HIDDEN),
        'w_out': nrm(ks[18], (DEPTH, D_MODEL, D_MODEL), D_MODEL),
        'norm_ffn_pre': gain(ks[19]),
        'norm_ffn_post': gain(ks[20]),
        'ffn_w_gate_up': nrm(ks[21], (DEPTH, D_MODEL, 2 * D_FF), D_MODEL),
        'ffn_w_down': nrm(ks[22], (DEPTH, D_FF, D_MODEL), D_FF),
        'ple_w_proj': nrm(ks[23], (DEPTH, D_PLE, D_MODEL), D_PLE),
        'ple_w_gate': nrm(ks[24], (DEPTH, D_MODEL, D_MODEL), D_MODEL),
        'ple_b_gate': 0.02 * jax.random.normal(ks[25], (DEPTH, D_MODEL), f32),
    }


def reference(x, p, norm_mix_pre, norm_mix_post, w_in, conv_w, conv_b, lru_wa, lru_ba, lru_wx, lru_bx,
              lru_lambda, cmp_pos_k, cmp_pos_v, cmp_k_w1, cmp_k_w2, cmp_v_w1, cmp_v_w2, w_out,
              norm_ffn_pre, norm_ffn_post, ffn_w_gate_up, ffn_w_down, ple_w_proj, ple_w_gate, ple_b_gate):
    B, S, _ = x.shape
    slopes = alibi_slopes()
    for i in range(DEPTH):
        h = rmsnorm(x, norm_mix_pre[i])
        z = h @ w_in[i]
        xr, gr, q, kc, vc, ksl, vsl, kw, vw, g_nsa, gm_rnn, gm_attn = jnp.split(z, IN_SPLITS, axis=-1)
        xr = causal_conv(xr, conv_w[i], conv_b[i])
        y_rnn = rg_lru(xr, lru_wa[i], lru_ba[i], lru_wx[i], lru_bx[i], lru_lambda[i]) * jax.nn.gelu(gr)
        kv_shape = (B, S, N_KV_GROUPS, HEAD_DIM)
        kc_c = compress(kc.reshape(kv_shape), cmp_pos_k[i], cmp_k_w1[i], cmp_k_w2[i])
        vc_c = compress(vc.reshape(kv_shape), cmp_pos_v[i], cmp_v_w1[i], cmp_v_w2[i])
        y_attn = nsa(q, kc_c, vc_c, ksl.reshape(kv_shape), vsl.reshape(kv_shape),
                     kw.reshape(kv_shape), vw.reshape(kv_shape), g_nsa, slopes)
        y = jax.nn.sigmoid(gm_rnn) * y_rnn + jax.nn.sigmoid(gm_attn) * y_attn
        x = x + rmsnorm(y @ w_out[i], norm_mix_post[i])
        h = rmsnorm(x, norm_ffn_pre[i])
        gate, up = jnp.split(h @ ffn_w_gate_up[i], 2, axis=-1)
        x = x + rmsnorm((jax.nn.silu(gate) * up) @ ffn_w_down[i], norm_ffn_post[i])
        x = x + jax.nn.sigmoid(x @ ple_w_gate[i] + ple_b_gate[i]) * (p[i] @ ple_w_proj[i])
    return x
```

```python
import numpy as np
import ml_dtypes
from contextlib import ExitStack
import concourse.bass as bass
import concourse.mybir as mybir
from concourse.bass_utils import run_bass_kernel_spmd

F32 = mybir.dt.float32
BF16 = mybir.dt.bfloat16
AF = mybir.ActivationFunctionType
ALU = mybir.AluOpType

D = 1024
DIN = 6704
DFF = 2816
EPS = 1e-6
ENGS = ("pe", "act", "dve", "pool", "sp")
C_XR, C_GR, C_Q, C_KC, C_VC, C_KS, C_VS, C_KW, C_VW, C_GN, C_GMR, C_GMA = (
    0, 1024, 2048, 3072, 3328, 3584, 3840, 4096, 4352, 4608, 4656, 5680)


class Sched:
    def __init__(self, nc, n_dma_sems=8):
        self.nc = nc
        self.nd = n_dma_sems
        self.semkey = {}
        for e in ENGS:
            self.semkey[("E", e)] = nc.alloc_semaphore(name=f"S_{e}")
        self.dmaq = ("sp", "act", "pool")
        for e in self.dmaq:
            for i in range(n_dma_sems):
                self.semkey[("D", e, i)] = nc.alloc_semaphore(name=f"D_{e}{i}")
        self.eng_cnt = {e: 0 for e in ENGS}
        self.dma_cnt = {e: 0 for e in self.dmaq}
        self.known = {e: {} for e in ENGS}
        self.dma_sem_last = {}
        self.ops = []
        self.total = 0

    def op(self, eng, fn, r=(), w=(), dma=False):
        self.ops.append((eng, fn, tuple(r), tuple(w), dma))

    def full_clock(self):
        c = {}
        for e in ENGS:
            if self.eng_cnt[e] > 0:
                c[("E", e)] = self.eng_cnt[e]
        for e in self.dmaq:
            for i in range(self.nd):
                n = (self.dma_cnt[e] - i + self.nd - 1) // self.nd
                if n > 0:
                    c[("D", e, i)] = 16 * n
        return c

    def emit(self):
        nc = self.nc
        ops = self.ops
        nops = len(ops)
        self.total += nops
        last_w = {}
        readers = {}
        deps = [None] * nops
        for k, (eng, fn, rd, wr, dma) in enumerate(ops):
            d = set()
            for r in rd:
                if r in last_w:
                    d.add(last_w[r])
            for w in wr:
                if w in last_w:
                    d.add(last_w[w])
                for x in readers.get(w, ()):
                    d.add(x)
            d.discard(k)
            deps[k] = d
            for w in wr:
                last_w[w] = k
                readers[w] = []
            for r in rd:
                if r not in wr:
                    readers.setdefault(r, []).append(k)
        sig = [None] * nops
        clock = [None] * nops
        per_eng = {e: [] for e in ENGS}
        dma_sem_last = {}
        for k, (eng, fn, rd, wr, dma) in enumerate(ops):
            waits = {}
            kn = self.known[eng]
            dl = set(deps[k])
            if dma:
                i = self.dma_cnt[eng] % self.nd
                sk = ("D", eng, i)
                self.dma_cnt[eng] += 1
                if sk in dma_sem_last:
                    dl.add(dma_sem_last[sk])
                dma_sem_last[sk] = k
            for d in dl:
                if (not ops[d][4]) and ops[d][0] == eng and eng == "pe":
                    continue
                s, v = sig[d]
                if kn.get(s, 0) >= v:
                    continue
                if waits.get(s, 0) < v:
                    waits[s] = v
            for d in dl:
                if (not ops[d][4]) and ops[d][0] == eng and eng == "pe":
                    continue
                for cs, cv in clock[d].items():
                    if kn.get(cs, 0) < cv:
                        kn[cs] = cv
            if dma:
                val = 16 * ((self.dma_cnt[eng] - 1) // self.nd) + 16
                sig[k] = (sk, val)
                c = dict(kn)
                c[sk] = val
                clock[k] = c
                per_eng[eng].append((waits, fn, sk, 16))
            else:
                self.eng_cnt[eng] += 1
                sk = ("E", eng)
                sig[k] = (sk, self.eng_cnt[eng])
                c = dict(kn)
                c[sk] = self.eng_cnt[eng]
                clock[k] = c
                per_eng[eng].append((waits, fn, sk, 1))
        final = self.full_clock()
        engobj = {"pe": nc.tensor, "act": nc.scalar, "dve": nc.vector, "pool": nc.gpsimd, "sp": nc.sync}
        semkey = self.semkey

        def run(ename):
            E = engobj[ename]
            for waits, fn, sk, inc in per_eng[ename]:
                for s, v in waits.items():
                    E.wait_ge(semkey[s], v)
                fn(E).then_inc(semkey[sk], inc)
            kn = self.known[ename]
            for s, v in final.items():
                if s == ("E", ename):
                    continue
                if kn.get(s, 0) < v:
                    E.wait_ge(semkey[s], v)

        with nc.Block() as block:
            @block.tensor
            def _(e):
                run("pe")

            @block.scalar
            def _(e):
                run("act")

            @block.vector
            def _(e):
                run("dve")

            @block.gpsimd
            def _(e):
                run("pool")

            @block.sync
            def _(e):
                run("sp")
        for e in ENGS:
            self.known[e] = dict(final)
        self.ops = []


def _split3(v):
    v = np.asarray(v, np.float32)
    hi = v.astype(ml_dtypes.bfloat16)
    r1 = (v - hi.astype(np.float32)).astype(np.float32)
    mid = r1.astype(ml_dtypes.bfloat16)
    r2 = (r1 - mid.astype(np.float32)).astype(np.float32)
    lo = r2.astype(ml_dtypes.bfloat16)
    return hi, mid, lo


def _slopes():
    h = np.arange(1, 17, dtype=np.float32)
    return np.exp2(-8.0 * h / 16.0).astype(np.float32)


def make_consts(S):
    bf = ml_dtypes.bfloat16
    sl = _slopes()
    c = {}
    c["ident"] = np.eye(128, dtype=np.float32)
    qaug = np.zeros((6, 16, 512), dtype=bf)
    ql = np.arange(512, dtype=np.float32)
    for h in range(16):
        a, b, cc = _split3(np.full((512,), sl[h], np.float32))
        qaug[0, h], qaug[1, h], qaug[2, h] = a, b, cc
        a, b, cc = _split3(-sl[h] * ql)
        qaug[3, h], qaug[4, h], qaug[5, h] = a, b, cc
    c["qaug"] = qaug
    kl = np.arange(S, dtype=np.float32) % 128
    kaug = np.ones((6, S), np.float32)
    kaug[0:3] = kl
    c["kaug"] = kaug.astype(bf)
    kaugc = np.ones((6, 512), np.float32)
    kaugc[0:3] = 16.0 * (np.arange(512, dtype=np.float32) % 128)
    c["kaugc"] = kaugc.astype(bf)
    kk = np.arange(128)[:, None]
    qq = np.arange(128)[None, :]
    tri = (np.stack([(qq >= kk), (qq < kk)]).astype(np.float32) - 1.0) * 30000.0
    c["tri"] = np.ascontiguousarray(tri.transpose(1, 0, 2)).astype(bf)
    cl = np.arange(128)[:, None]
    qL = np.arange(512)[None, :]
    cm = (np.stack([(qL - 16 * cl - 31 + 512 * m >= 0) for m in range(5)]).astype(np.float32) - 1.0) * 30000.0
    c["cmask"] = np.ascontiguousarray(cm.transpose(1, 0, 2)).astype(bf)
    NC = S // 16 - 1
    NCT = (NC + 128) // 128
    A1 = np.zeros((NCT * 128, 129), np.float32)
    for cc_ in range(NC):
        j, rem = cc_ // 4, cc_ % 4
        if rem < 3:
            if j < 128:
                A1[cc_, j] = 2.0
        else:
            if j < 128:
                A1[cc_, j] = 1.0
            if j + 1 < 128:
                A1[cc_, j + 1] = 1.0
        A1[cc_, 128] = 1.0
    c["A1"] = np.ascontiguousarray(A1.reshape(NCT, 128, 129).transpose(1, 0, 2)).astype(bf)
    G = np.zeros((128, S), np.float32)
    y = np.arange(S)
    G[y // 64, y] = 1.0
    c["G"] = G.astype(bf)
    mv = np.zeros((128, 256), np.float32)
    ma = np.zeros((128, 256), np.float32)
    for p in range(128):
        cur = 0 if p < 64 else 1
        for col in range(256):
            jr = col - 126
            valid = jr <= cur
            forced = (jr == cur) or (jr == cur - 1)
            if not valid:
                ma[p, col] = -1.0
            elif forced:
                ma[p, col] = 1e4
            else:
                mv[p, col] = 1.0
    c["mvalid"] = mv
    c["madd"] = ma
    return c


CONST_SPECS = None


def build(S, debug=False):
    assert S % 2048 == 0
    NT = S // 512
    NKT = S // 128
    NC = S // 16 - 1
    NCT = (NC + 128) // 128
    sl = [float(v) for v in _slopes()]
    nc = bass.Bass("TRN2", target_bir_lowering=False)
    skind = "ExternalOutput" if debug else "Internal"

    def din(name, shape, dt=F32):
        return nc.dram_tensor(name, list(shape), dt, kind="ExternalInput").ap()

    def dscr(name, shape, dt=F32):
        return nc.dram_tensor(name, list(shape), dt, kind=skind).ap()

    x_d = din("x", [S, D])
    p_d = din("p", [S, 256])
    w_in_d = din("w_in", [D, DIN])
    w_out_d = din("w_out", [D, D])
    wgu_d = din("wgu", [D, 2 * DFF])
    wd_d = din("wd", [DFF, D])
    wpp_d = din("wpp", [256, D])
    wpg_d = din("wpg", [D, D])
    gains_d = din("gains", [5, 128, D])
    pv_d = din("pv", [128, 64])
    bda_d = din("bda", [128, 8, 128])
    bdx_d = din("bdx", [128, 8, 128])
    cw1_d = [din("ckw1", [2048, 256]), din("cvw1", [2048, 256])]
    cw2_d = [din("ckw2", [256, 64]), din("cvw2", [256, 64])]
    posT_d = [din("posTk", [128, 16]), din("posTv", [128, 16])]
    ident_d = din("ident", [128, 128])
    qaug_d = din("qaug", [6, 16, 512], BF16)
    kaug_d = din("kaug", [6, S], BF16)
    kaugc_d = din("kaugc", [6, 512], BF16)
    tri_d = din("tri", [128, 2, 128], BF16)
    cmask_d = din("cmask", [128, 5, 512], BF16)
    A1_d = din("A1", [128, NCT, 129], BF16)
    G_d = din("G", [128, S], BF16)
    mvalid_d = din("mvalid", [128, 256])
    madd_d = din("madd", [128, 256])
    out_d = nc.dram_tensor("out", [S, D], F32, kind="ExternalOutput").ap()

    qT_d = dscr("s_qT", [D, S], BF16)
    kcT_d = dscr("s_kcT", [256, S], BF16)
    vcT_d = dscr("s_vcT", [256, S], BF16)
    ksT_d = dscr("s_ksT", [256, S], BF16)
    kwT_d = dscr("s_kwT", [256, S], BF16)
    vsw_d = dscr("s_vsw", [S, 512], BF16)
    gsg_d = dscr("s_gsg", [S, 48])
    yr_d = dscr("s_yr", [D, S], BF16)
    ga_d = dscr("s_ga", [D, S], BF16)
    kcc_d = dscr("s_kcc", [256, 512], BF16)
    vcc_d = dscr("s_vcc", [NCT * 128, 256], BF16)
    ya_d = dscr("s_ya", [S, D])
    x1_d = dscr("s_x1", [S, D])
    x2_d = dscr("s_x2", [S, D])

    SC = Sched(nc)

    def MM(out, lhsT, rhs, start, stop, r, w):
        SC.op("pe", lambda e: e.matmul(out, lhsT=lhsT, rhs=rhs, start=start, stop=stop, skip_group_check=True), r, w)

    def TR(out, in_, ident, r, w):
        SC.op("pe", lambda e: e.transpose(out, in_, ident), r, w)

    def ACT(out, in_, func, r, w, bias=0.0, scale=1.0, accum=None):
        SC.op("act", lambda e: e.activation(out=out, in_=in_, func=func, bias=bias, scale=scale, accum_out=accum), r, w)

    def TS(out, in0, s1, s2, op0, op1, r, w, eng="dve", accum=None):
        if op1 is None:
            SC.op(eng, lambda e: e.tensor_scalar(out=out, in0=in0, scalar1=s1, scalar2=None, op0=op0), r, w)
        else:
            SC.op(eng, lambda e: e.tensor_scalar(out=out, in0=in0, scalar1=s1, scalar2=s2, op0=op0, op1=op1), r, w)

    def TT(out, in0, in1, op, r, w, eng="dve"):
        SC.op(eng, lambda e: e.tensor_tensor(out=out, in0=in0, in1=in1, op=op), r, w)

    def STT(out, in0, scalar, in1, op0, op1, r, w):
        SC.op("dve", lambda e: e.scalar_tensor_tensor(out=out, in0=in0, scalar=scalar, in1=in1, op0=op0, op1=op1), r, w)

    def CP(out, in_, r, w, eng="dve"):
        SC.op(eng, lambda e: e.tensor_copy(out=out, in_=in_), r, w)

    def MEMSET(ap, val, w, eng="dve"):
        SC.op(eng, lambda e: e.memset(ap, val), (), w)

    def RECIP(out, in_, r, w):
        SC.op("dve", lambda e: e.reciprocal(out=out, in_=in_), r, w)

    def DMA(q, out, in_, r, w, **kw):
        SC.op(q, lambda e: e.dma_start(out=out, in_=in_, **kw), r, w, dma=True)

    def gelu_tanh(es_tmp, out, src, nparts, ncols, keyp, rkeys, wkeys):
        t1, t2 = es_tmp
        TT(t1, src, src, ALU.mult, rkeys, [keyp + "t1"])
        TS(t1, t1, 0.044715, 1.0, ALU.mult, ALU.add, [keyp + "t1"], [keyp + "t1"])
        TT(t1, t1, src, ALU.mult, [keyp + "t1"] + list(rkeys), [keyp + "t1"])
        ACT(t2, t1, AF.Sigmoid, [keyp + "t1"], [keyp + "t2"], scale=1.5957691216057308)
        TT(out, t2, src, ALU.mult, [keyp + "t2"] + list(rkeys), wkeys)

    def rms_rstd(rstd, ssq, r, w):
        ACT(rstd, ssq, AF.Sqrt, r, w, bias=epsb[:, 0:1], scale=1.0 / D)
        RECIP(rstd, rstd, w, w)

    with ExitStack() as g_es:
        ident = g_es.enter_context(nc.sbuf_tensor("sb_ident", [128, 128], F32))
        identb = g_es.enter_context(nc.sbuf_tensor("sb_identb", [128, 128], BF16))
        epsb = g_es.enter_context(nc.sbuf_tensor("sb_epsb", [128, 1], F32))
        DMA("sp", ident[:], ident_d, (), ["ident"])
        DMA("pool", identb[:], ident_d, (), ["identb"])
        MEMSET(epsb[:], EPS, ["epsb"])
        SC.emit()

        with ExitStack() as es:
            def sb(name, shape, dt=F32):
                return es.enter_context(nc.sbuf_tensor("sb_" + name, list(shape), dt))
            Wi = sb("Wi", [128, 8, DIN], BF16)
            bda = sb("bda", [128, 8, 128], BF16)
            bdx = sb("bdx", [128, 8, 128], BF16)
            pv = sb("pv", [128, 64])
            cneg = sb("cneg", [128, 8])
            cneg2 = sb("cneg2", [128, 8])
            ctmp = sb("ctmp", [128, 8])
            ctmp2 = sb("ctmp2", [128, 8])
            gpre = sb("gpre", [128, D])
            xs = [sb(f"xs{i}", [128, D]) for i in range(2)]
            junk = sb("junk", [128, D])
            hb = [sb("hb0", [128, 4, D], BF16)]
            hT = sb("hT", [128, 8, 512], BF16)
            ssq = sb("ssq", [128, 8])
            xbuf = sb("xbuf", [128, 8, 516])
            hlast = sb("hlast", [128, 8])
            T = {n: sb("t_" + n, [128, 512]) for n in
                 ("xc0", "xc1", "r", "i", "a", "a2", "bb", "hs", "g00", "g01", "t1", "t2", "ge", "sgm0", "sgm1")}
            xcb = [sb(f"xcb{i}", [128, 512], BF16) for i in range(2)]
            ev = [sb(f"ev{i}", [128, 512], BF16) for i in range(3)]
            evf = [sb(f"evf{i}", [128, 512], BF16) for i in range(2)]
            gsb = sb("gsb", [128, 48])
            psT = [es.enter_context(nc.psum_tensor(f"psT{i}", [128, 1024], BF16)) for i in range(2)]
            psM = [es.enter_context(nc.psum_tensor(f"psM{i}", [128, 512], F32)) for i in range(4)]
            psG = [es.enter_context(nc.psum_tensor(f"psG{i}", [128, 512], F32)) for i in range(2)]

            WBLK = [(0, 1024), (1024, 2048), (4656, 5680), (5680, 6704), (2048, 3072), (3072, 4656)]
            for j, (c0_, c1_) in enumerate(WBLK):
                for k in range(8):
                    DMA("pool", Wi[:, k, c0_:c1_], w_in_d[k * 128:(k + 1) * 128, c0_:c1_], (), [f"WiB{j}"],
                        max_dma_last_dim=4096)

            def wkey(col0):
                for j, (c0_, c1_) in enumerate(WBLK):
                    if c0_ <= col0 < c1_:
                        return f"WiB{j}"
                raise ValueError(col0)
            DMA("pool", bda[:], bda_d, (), ["bda"])
            DMA("pool", bdx[:], bdx_d, (), ["bdx"])
            DMA("sp", pv[:], pv_d, (), ["pv"])
            DMA("sp", gpre[:], gains_d[0], (), ["gpre"])
            MEMSET(xbuf[:], 0.0, ["xbuf"])
            MEMSET(hlast[:], 0.0, ["hlast"])
            onepb = sb("onepb", [128, 1])
            MEMSET(onepb[:], 1.0 + 2.0 ** -23, ["onepb"])
            lam = pv[:, 56:64]
            ACT(ctmp[:], lam, AF.Exp, ["pv"], ["ctmp"], scale=-1.0)
            TS(ctmp2[:], ctmp[:], -0.25, 1.0 / 3.0, ALU.mult, ALU.add, ["ctmp"], ["ctmp2"])
            TT(ctmp2[:], ctmp2[:], ctmp[:], ALU.mult, ["ctmp", "ctmp2"], ["ctmp2"])
            TS(ctmp2[:], ctmp2[:], -0.5, None, ALU.add, None, ["ctmp2"], ["ctmp2"])
            TT(ctmp2[:], ctmp2[:], ctmp[:], ALU.mult, ["ctmp", "ctmp2"], ["ctmp2"])
            TS(ctmp2[:], ctmp2[:], 1.0, None, ALU.add, None, ["ctmp2"], ["ctmp2"])
            TT(ctmp2[:], ctmp2[:], ctmp[:], ALU.mult, ["ctmp", "ctmp2"], ["ctmp2"])
            TS(cneg[:], ctmp2[:], -8.0, None, ALU.mult, None, ["ctmp2"], ["cneg"])
            TS(cneg2[:], ctmp2[:], -16.0, None, ALU.mult, None, ["ctmp2"], ["cneg"])

            def norm_tile(b):
                hbuf = hb[0]
                for s in range(4):
                    xt = xs[s % 2]
                    xk = f"xs{s % 2}"
                    r0 = b * 512 + s * 128
                    DMA("sp", xt[:], x_d[r0:r0 + 128, :], (), [xk])
                    col = (b % 2) * 4 + s
                    ACT(junk[:], xt[:], AF.Square, [xk], ["junk", f"ssq{col}"], accum=ssq[:, col:col + 1])
                    rms_rstd(ssq[:, col:col + 1], ssq[:, col:col + 1], [f"ssq{col}"], [f"ssq{col}"])
                    STT(hbuf[:, s, :], xt[:], ssq[:, col:col + 1], gpre[:], ALU.mult, ALU.mult,
                        [xk, f"ssq{col}", "gpre"], [f"hb_{s}"])

            norm_tile(0)
            evi = [0]
            mi = [0]

            def next_ps():
                i = mi[0] % 4
                mi[0] += 1
                return psM[i], f"psM{i}"

            def featmm(col0, ps, pk):
                for k in range(8):
                    MM(ps[:], Wi[:, k, col0:col0 + 128], hT[:, k, :], k == 0, k == 7, [wkey(col0), f"hT{k}"], [pk])

            OMB = 1.0 + 2.0 ** -23

            def conv_chunk(c):
                xk = f"xbuf{c}"
                xc = T[f"xc{c % 2}"]
                xck = f"xc{c % 2}"
                TS(xc[:], xbuf[:, c, 3:515], pv[:, 24 + c:25 + c], pv[:, 32 + c:33 + c], ALU.mult, ALU.add, [xk, "pv"], [xck])
                for kk in range(3):
                    STT(xc[:], xbuf[:, c, kk:kk + 512], pv[:, kk * 8 + c:kk * 8 + c + 1], xc[:], ALU.mult, ALU.add,
                        [xk, "pv", xck], [xck])
                CP(xbuf[:, c, 0:3], xbuf[:, c, 512:515], [xk], [xk], eng="pool")

            def conv_act(c):
                ACT(xcb[c % 2][:], T[f"xc{c % 2}"][:], AF.Identity, [f"xc{c % 2}"], [f"xcb{c % 2}"])

            def other_job(ji, job, t0):
                col0, dst, row0, scl, sig = job
                ps, pk = next_ps()
                featmm(col0, ps, pk)
                if sig:
                    ef = evf[ji % 2]
                    ACT(ef[:], ps[:], AF.Sigmoid, [pk], [f"evf{ji % 2}"])
                    DMA("sp", dst[row0:row0 + 128, t0:t0 + 512], ef[:], [f"evf{ji % 2}"], [])
                else:
                    e_ = ev[evi[0] % 3]
                    ek = f"ev{evi[0] % 3}"
                    evi[0] += 1
                    if ji % 2 == 0:
                        TS(e_[:], ps[:], scl, None, ALU.mult, None, [pk], [ek])
                    else:
                        ACT(e_[:], ps[:], AF.Identity, [pk], [ek], scale=scl)
                    DMA("sp", dst[row0:row0 + 128, t0:t0 + 512], e_[:], [ek], [])

            for b in range(NT):
                t0 = b * 512
                hbuf = hb[0]
                for c in range(8):
                    pt = psT[c % 2]
                    for s in range(4):
                        TR(pt[:, s * 128:(s + 1) * 128], hbuf[:, s, c * 128:(c + 1) * 128], identb[:],
                           [f"hb_{s}", "identb"], [f"psT{c % 2}"])
                    if c % 2 == 0:
                        CP(hT[:, c, :], pt[:, 0:512], [f"psT{c % 2}"], [f"hT{c}"])
                    else:
                        ACT(hT[:, c, :], pt[:, 0:512], AF.Identity, [f"psT{c % 2}"], [f"hT{c}"])
                if b + 1 < NT:
                    norm_tile(b + 1)
                for c in range(8):
                    ps, pk = next_ps()
                    featmm(C_XR + c * 128, ps, pk)
                    ACT(xbuf[:, c, 3:515], ps[:], AF.Identity, [pk], [f"xbuf{c}"])
                jobs = []
                for c in range(8):
                    jobs.append((C_GMA + c * 128, ga_d, c * 128, 1.0, True))
                    jobs.append((C_Q + c * 128, qT_d, c * 128, 0.125, False))
                    base, dst = ((C_KC, kcT_d), (C_VC, vcT_d), (C_KS, ksT_d), (C_KW, kwT_d))[c // 2]
                    jobs.append((base + (c % 2) * 128, dst, (c % 2) * 128, 1.0, False))
                conv_chunk(0)
                conv_act(0)
                for c in range(8):
                    if c + 1 < 8:
                        conv_chunk(c + 1)
                    xc = T[f"xc{c % 2}"]
                    xck = f"xc{c % 2}"
                    xb_, xbk = xcb[c % 2], f"xcb{c % 2}"
                    g0, g0k = T[f"g0{c % 2}"], f"g0{c % 2}"
                    sgm, sgmk = T[f"sgm{c % 2}"], f"sgm{c % 2}"
                    MM(psG[0][:], bda[:, c, :], xb_[:], True, True, ["bda", xbk], ["psG0"])
                    MM(psG[1][:], bdx[:, c, :], xb_[:], True, True, ["bdx", xbk], ["psG1"])
                    psg, pkg = next_ps()
                    featmm(C_GR + c * 128, psg, pkg)
                    psm, pkm = next_ps()
                    featmm(C_GMR + c * 128, psm, pkm)
                    ACT(T["r"][:], psG[0][:], AF.Sigmoid, ["psG0", "pv"], ["r"], bias=pv[:, 40 + c:41 + c])
                    ACT(T["i"][:], psG[1][:], AF.Sigmoid, ["psG1", "pv"], ["i"], bias=pv[:, 48 + c:49 + c])
                    ACT(T["a"][:], T["r"][:], AF.Exp, ["r", "cneg"], ["a"], scale=cneg[:, c:c + 1])
                    TT(T["a2"][:], T["a"][:], T["a"][:], ALU.mult, ["a"], ["a2"], eng="pool")
                    TS(T["a2"][:], T["a2"][:], -1.0, OMB, ALU.mult, ALU.add, ["a2"], ["a2"], eng="pool")
                    if c + 1 < 8:
                        conv_act(c + 1)
                    ACT(g0[:], psg[:], AF.Identity, [pkg], [g0k])
                    ACT(T["a2"][:], T["a2"][:], AF.Ln, ["a2"], ["a2"])
                    ACT(T["a2"][:], T["a2"][:], AF.Exp, ["a2"], ["a2"], scale=0.5)
                    ACT(sgm[:], psm[:], AF.Sigmoid, [pkm], [sgmk])
                    TT(T["bb"][:], T["i"][:], xc[:], ALU.mult, ["i", xck], ["bb"], eng="pool")
                    TT(T["bb"][:], T["bb"][:], T["a2"][:], ALU.mult, ["bb", "a2"], ["bb"])
                    SC.op("dve", lambda e, c=c: e.tensor_tensor_scan(out=T["hs"][:], data0=T["a"][:], data1=T["bb"][:],
                                                                   initial=hlast[:, c:c + 1], op0=ALU.mult, op1=ALU.add),
                          ["a", "bb", "hlast"], ["hs"])
                    CP(hlast[:, c:c + 1], T["hs"][:, 511:512], ["hs"], ["hlast"])
                    t1, t2 = T["t1"], T["t2"]
                    TT(t1[:], g0[:], g0[:], ALU.mult, [g0k], ["t1"], eng="pool")
                    TS(t1[:], t1[:], 0.044715, 1.0, ALU.mult, ALU.add, ["t1"], ["t1"], eng="pool")
                    TT(t1[:], t1[:], g0[:], ALU.mult, ["t1", g0k], ["t1"], eng="pool")
                    for jj in range(3):
                        other_job(3 * c + jj, jobs[3 * c + jj], t0)
                    ACT(t2[:], t1[:], AF.Sigmoid, ["t1"], ["t2"], scale=1.5957691216057308)
                    TT(T["ge"][:], t2[:], g0[:], ALU.mult, ["t2", g0k], ["ge"])
                    TT(T["ge"][:], T["ge"][:], T["hs"][:], ALU.mult, ["ge", "hs"], ["ge"])
                    ef = evf[c % 2]
                    TT(ef[:], T["ge"][:], sgm[:], ALU.mult, ["ge", sgmk], [f"evf{c % 2}"], eng="pool")
                    DMA("sp", yr_d[c * 128:(c + 1) * 128, t0:t0 + 512], ef[:], [f"evf{c % 2}"], [])
                for s in range(4):
                    ps, pk = next_ps()
                    for k in range(8):
                        MM(ps[:, 0:256], hT[:, k, s * 128:(s + 1) * 128], Wi[:, k, C_VS:C_VS + 256], k == 0, k == 7,
                           [wkey(C_VS), f"hT{k}"], [pk])
                    for k in range(8):
                        MM(ps[:, 256:512], hT[:, k, s * 128:(s + 1) * 128], Wi[:, k, C_VW:C_VW + 256], k == 0, k == 7,
                           [wkey(C_VS), f"hT{k}"], [pk])
                    e_ = ev[evi[0] % 3]
                    ek = f"ev{evi[0] % 3}"
                    evi[0] += 1
                    CP(e_[:], ps[:], [pk], [ek])
                    DMA("sp", vsw_d[t0 + s * 128:t0 + (s + 1) * 128, :], e_[:], [ek], [])
                    ps, pk = next_ps()
                    for k in range(8):
                        MM(ps[:, 0:48], hT[:, k, s * 128:(s + 1) * 128], Wi[:, k, C_GN:C_GN + 48], k == 0, k == 7,
                           [wkey(C_VS), f"hT{k}"], [pk])
                    ACT(gsb[:], ps[:, 0:48], AF.Sigmoid, [pk], ["gsb"])
                    DMA("sp", gsg_d[t0 + s * 128:t0 + (s + 1) * 128, :], gsb[:], ["gsb"], [])
            SC.emit()

        with ExitStack() as es:
            def sb(name, shape, dt=F32):
                return es.enter_context(nc.sbuf_tensor("sb_" + name, list(shape), dt))
            w1 = [sb(f"cw1_{i}", [128, 16, 256], BF16) for i in range(2)]
            w2 = [sb(f"cw2_{i}", [128, 2, 64], BF16) for i in range(2)]
            posT = [sb(f"posT{i}", [128, 16], BF16) for i in range(2)]
            cb = [sb(f"cbias{i}", [128, 2]) for i in range(2)]
            kin = [sb(f"kin{i}", [128, S], BF16) for i in range(3)]
            kinr = [sb(f"kinr{i}", [128, 8, S // 16], BF16) for i in range(2)]
            u = sb("c_u", [128, 512])
            t1 = sb("c_t1", [128, 512])
            t2 = sb("c_t2", [128, 512])
            Hh = [sb(f"c_H{i}", [128, 512], BF16) for i in range(2)]
            okc = sb("c_okc", [64, 512], BF16)
            ovc = sb("c_ovc", [128, 64], BF16)
            psH = [es.enter_context(nc.psum_tensor(f"psH{i}", [128, 512], F32)) for i in range(2)]
            psO = [es.enter_context(nc.psum_tensor(f"psO{i}", [128, 512], F32)) for i in range(2)]
            psB = es.enter_context(nc.psum_tensor("psB", [128, 512], F32))
            for kv in range(2):
                DMA("pool", w1[kv][:], cw1_d[kv].rearrange("(lc p) h -> p lc h", p=128), (), [f"w1{kv}"])
                DMA("pool", w2[kv][:], cw2_d[kv].rearrange("(c p) d -> p c d", p=128), (), [f"w2{kv}"])
                DMA("pool", posT[kv][:], posT_d[kv], (), [f"posT{kv}"])
            MEMSET(okc[:], 0.0, ["okc"])
            MEMSET(ovc[:], 0.0, ["ovc"])
            for i3 in range(3):
                MEMSET(kin[i3][64:128, S - 16:S], 0.0, [f"kin{i3}"])
            ki = 0
            for kv in range(2):
                src_d = kcT_d if kv == 0 else vcT_d
                for hc in range(2):
                    for lc in range(16):
                        MM(psB[:, hc:hc + 1], w1[kv][:, lc, hc * 128:(hc + 1) * 128], posT[kv][:, lc:lc + 1], lc == 0, lc == 15,
                           [f"w1{kv}", f"posT{kv}"], ["psB"])
                CP(cb[kv][:], psB[:, 0:2], ["psB"], [f"cb{kv}"])
                for g in range(4):
                    kt_ = kin[ki % 3]
                    kk_ = f"kin{ki % 3}"
                    kr_ = kinr[ki % 2]
                    krk = f"kinr{ki % 2}"
                    ki += 1
                    DMA("sp", kt_[0:64, :], src_d[g * 64:(g + 1) * 64, :], (), [kk_])
                    DMA("sp", kt_[64:128, 0:S - 1], src_d[g * 64:(g + 1) * 64, 1:S], (), [kk_])
                    kv4 = kt_[:].rearrange("p (i j t) -> p j t i", j=8, t=2)
                    CP(kr_[:, 0:4, :], kv4[:, 0:4, 0, :], [kk_], [krk + "a"])
                    CP(kr_[:, 4:8, :], kv4[:, 4:8, 0, :], [kk_], [krk + "b"], eng="pool")
                    for hc in range(2):
                        ph = psH[hc]
                        for lc in range(16):
                            j = lc % 8
                            rhs_ = kr_[:, j, 0:NC] if lc < 8 else kr_[:, j, 1:NC + 1]
                            MM(ph[:, 0:NC], w1[kv][:, lc, hc * 128:(hc + 1) * 128], rhs_,
                               lc == 0, lc == 15, [f"w1{kv}", krk + ("a" if j < 4 else "b")], [f"psH{hc}"])
                        ACT(u[:, 0:NC], ph[:, 0:NC], AF.Identity, [f"psH{hc}", f"cb{kv}"], ["c_u"], bias=cb[kv][:, hc:hc + 1])
                        gelu_tanh((t1[:, 0:NC], t2[:, 0:NC]), Hh[hc][:, 0:NC], u[:, 0:NC], 128, NC, "p2", ["c_u"], [f"H{hc}"])
                    if kv == 0:
                        po = psO[0]
                        for hc in range(2):
                            MM(po[0:64, 0:NC], w2[kv][:, hc, :], Hh[hc][:, 0:NC], hc == 0, hc == 1,
                               [f"w2{kv}", f"H{hc}"], ["psO0"])
                        CP(okc[:, 0:NC], po[0:64, 0:NC], ["psO0"], ["okc"])
                        DMA("sp", kcc_d[g * 64:(g + 1) * 64, :], okc[:], ["okc"], [])
                    else:
                        for ct in range(NCT):
                            n = min(128, NC - ct * 128)
                            po = psO[ct % 2]
                            for hc in range(2):
                                MM(po[0:n, 0:64], Hh[hc][:, ct * 128:ct * 128 + n], w2[kv][:, hc, :], hc == 0, hc == 1,
                                   [f"w2{kv}", f"H{hc}"], [f"psO{ct % 2}"])
                            CP(ovc[0:n, :], po[0:n, 0:64], [f"psO{ct % 2}"], ["ovc"])
                            DMA("sp", vcc_d[ct * 128:ct * 128 + 128, g * 64:(g + 1) * 64], ovc[:, :], ["ovc"], [])
            SC.emit()

        with ExitStack() as es:
            def sb(name, shape, dt=F32):
                return es.enter_context(nc.sbuf_tensor("sb_" + name, list(shape), dt))
            DEPTH = 3
            NPB = 6
            KsA = sb("KsA", [128, S], BF16)
            KwA = sb("KwA", [128, S], BF16)
            KcA = sb("KcA", [128, 512], BF16)
            Vs1 = sb("Vs1", [128, NKT, 65], BF16)
            Vw1 = sb("Vw1", [128, NKT, 65], BF16)
            CA1 = sb("CA1", [128, NCT, 193], BF16)
            Gm = sb("Gm", [128, S], BF16)
            tri = sb("tri", [128, 2, 128], BF16)
            cmask = sb("cmask", [128, 5, 512], BF16)
            mvalid = sb("mvalid", [128, 256])
            madd = sb("madd", [128, 256])
            maskS = sb("maskS", [128, NKT, 512], BF16)
            QA = [sb(f"QA{i}", [128, 4, 512], BF16) for i in range(2)]
            gsig = [sb(f"gsig{i}", [128, 4, 48]) for i in range(2)]
            Pb = [sb(f"Pb{i}", [128, 512], BF16) for i in range(NPB)]
            P2 = [sb(f"P2{i}", [128, 512], BF16) for i in range(NPB)]
            IMP = sb("IMP", [128, 4, 128])
            impt = sb("impt", [128, 2, 128])
            sc1 = sb("sc1", [128, 128])
            sc2 = sb("sc2", [128, 128])
            sc3 = sb("sc3", [128, 128])
            m8 = sb("m8", [128, 8])
            selq = [sb(f"selq{i}", [128, 128]) for i in range(4)]
            selT = sb("selT", [128, 512], BF16)
            Y = [sb(f"Y{i}", [128, 4, 256]) for i in range(2)]
            ytmp = [sb(f"ytmp{i}", [128, 4, 64]) for i in range(2)]
            rcs = [sb(f"rcs{i}", [128, 8]) for i in range(4)]
            psS = [es.enter_context(nc.psum_tensor(f"psS{i}", [128, 512], F32)) for i in range(4)]
            psA = [es.enter_context(nc.psum_tensor(f"psA{i}", [128, 512], F32)) for i in range(4)]

            DMA("sp", Gm[:], G_d, (), ["Gm"])
            DMA("sp", tri[:], tri_d, (), ["tri"])
            DMA("sp", cmask[:], cmask_d, (), ["cmask"])
            DMA("sp", mvalid[:], mvalid_d, (), ["mvalid"])
            DMA("sp", madd[:], madd_d, (), ["madd"])
            MEMSET(KsA[64:128, :], 0.0, ["KsA"])
            MEMSET(KwA[64:128, :], 0.0, ["KwA"], eng="pool")
            MEMSET(KcA[64:128, :], 0.0, ["KcA"])
            MEMSET(QA[0][64:128, :, :], 0.0, ["QA0"])
            MEMSET(QA[1][64:128, :, :], 0.0, ["QA1"], eng="pool")
            DMA("sp", KsA[64:70, :], kaug_d, (), ["KsA"])
            DMA("sp", KwA[64:70, :], kaug_d, (), ["KwA"])
            DMA("sp", KcA[64:70, :], kaugc_d, (), ["KcA"])
            DMA("sp", CA1[:, :, 64:193], A1_d, (), ["CA1"])
            MEMSET(Vs1[:, :, 64:65], 1.0, ["Vs1"])
            MEMSET(Vw1[:, :, 64:65], 1.0, ["Vw1"])

            cnt = {"s": 0, "p": 0, "qa": 0, "m": 0}
            negb = sb("negb", [128, 1])
            MEMSET(negb[:], -30000.0, ["negb"])

            def next_s():
                si = cnt["s"] % 4
                cnt["s"] += 1
                return psS[si], f"psS{si}"

            def run_pipeline(steps):
                def stageA(st):
                    if st.get("before") is not None:
                        st["before"]()
                    ps, pk = next_s()
                    c0, c1 = st["cols"]
                    n = st["n"]
                    addmask = (st.get("mk") is not None) and (st["mk"] % 3 != 2)
                    has_add = (st.get("cm") is not None) or (st.get("tri") is not None) or addmask
                    MM(ps[0:n, c0:c1], st["lhsT"], st["rhsQ"][:, c0:c1], True, not has_add, st["rk"], [pk])
                    if st.get("cm") is not None:
                        MM(ps[0:n, c0:c1], identb[0:n, 0:n], cmask[0:n, st["cm"], c0:c1], False, True, ["identb", "cmask"], [pk])
                    if st.get("tri") is not None:
                        tt, sub = st["tri"]
                        MM(ps[0:n, sub * 128:(sub + 1) * 128], identb[0:n, 0:n], tri[0:n, tt, :], False, True,
                           ["identb", "tri"], [pk])
                    if addmask:
                        MM(ps[0:n, c0:c1], identb[0:n, 0:n], maskS[0:n, st["mk"], c0:c1], False, True,
                           ["identb", f"mS{st['mk']}"], [pk])
                    pi = cnt["p"] % NPB
                    cnt["p"] += 1
                    pb = Pb[pi]
                    ACT(pb[0:n, c0:c1], ps[0:n, c0:c1], AF.Exp, [pk], [f"Pb{pi}"], bias=st["bias"])
                    src, srck = pb, f"Pb{pi}"
                    if st.get("mk") is not None and not addmask:
                        kt = st["mk"]
                        p2 = P2[pi]
                        eng = "dve" if cnt["m"] % 2 == 0 else "pool"
                        cnt["m"] += 1
                        TT(p2[0:n, c0:c1], pb[0:n, c0:c1], maskS[0:n, kt, c0:c1], ALU.mult, [srck, f"mS{kt}"], [f"P2{pi}"], eng=eng)
                        src, srck = p2, f"P2{pi}"
                    st["src"] = (src, srck)

                def stageB(st):
                    src, srck = st["src"]
                    n = st["n"]
                    for sub in st["subs"]:
                        acc, ak = st["acc"](sub)
                        MM(acc, src[0:n, sub * 128:(sub + 1) * 128], st["rhsV"], st["first"](sub), st["last"], [srck] + st["vk"], [ak])
                    if st.get("after") is not None:
                        st["after"]()

                for i in range(len(steps) + DEPTH):
                    if i < len(steps):
                        stageA(steps[i])
                    if i >= DEPTH:
                        stageB(steps[i - DEPTH])

            def bc(ap2, shape):
                return ap2.unsqueeze(2).to_broadcast(shape)

            CUT = 150.0

            def far(h, q_first, k_last):
                return sl[h] * float(q_first - k_last) > CUT

            all_steps = []
            bufsel = {"qa": 0}

            def loads_gb(g, b, qi):
                q0 = b * 512
                qa = QA[qi]
                qk = f"QA{qi}"
                gs = gsig[qi]
                gk = f"gsig{qi}"
                DMA("sp", qa[0:64, :, :], qT_d[g * 256:(g + 1) * 256, q0:q0 + 512].rearrange("(r d) s -> d r s", d=64),
                    (), [qk])
                DMA("sp", qa[64:70, :, :], qaug_d[:, 4 * g:4 * g + 4, :], (), [qk])
                DMA("sp", gs[:], gsg_d[q0:q0 + 512, :].rearrange("(s p) c -> p s c", p=128), (), [gk])

            def loads_group(g):
                DMA("sp", KsA[0:64, :], ksT_d[g * 64:(g + 1) * 64, :], (), ["KsA"])
                DMA("sp", KwA[0:64, :], kwT_d[g * 64:(g + 1) * 64, :], (), ["KwA"])
                DMA("sp", KcA[0:64, :], kcc_d[g * 64:(g + 1) * 64, :], (), ["KcA"])
                DMA("act", Vs1[:, :, 0:64], vsw_d[:, g * 64:(g + 1) * 64].rearrange("(t p) d -> p t d", p=128), (), ["Vs1"])
                DMA("act", Vw1[:, :, 0:64], vsw_d[:, 256 + g * 64:256 + (g + 1) * 64].rearrange("(t p) d -> p t d", p=128),
                    (), ["Vw1"])
                DMA("act", CA1[:, :, 0:64], vcc_d[:, g * 64:(g + 1) * 64].rearrange("(t p) d -> p t d", p=128), (), ["CA1"])

            def build_gb(g, b, qi, nxt):
                q0 = b * 512
                qa = QA[qi]
                qk = f"QA{qi}"
                gs = gsig[qi]
                gk = f"gsig{qi}"
                Yt = Y[qi]
                yk = f"Y{qi}"
                gb_steps = []
                n_ct = min(NCT, (32 * b + 30) // 128 + 1)

                def selection_chain():
                    for sub in range(4):
                        i_ = 4 * b + sub
                        c_lo = 126 - 2 * i_
                        TT(sc1[:], IMP[:, sub, :], mvalid[:, c_lo:c_lo + 128], ALU.mult, [f"IMP{sub // 2}", "mvalid"], ["sc1"])
                        TT(sc1[:], sc1[:], madd[:, c_lo:c_lo + 128], ALU.add, ["sc1", "madd"], ["sc1"])
                        MEMSET(sc1[:, 0:1], 1e4, ["sc1"])
                        SC.op("dve", lambda e: e.max(out=m8[:], in_=sc1[:]), ["sc1"], ["m8"])
                        SC.op("dve", lambda e: e.match_replace(out=sc2[:], in_to_replace=m8[:], in_values=sc1[:], imm_value=-5.0),
                              ["sc1", "m8"], ["sc2"])
                        SC.op("dve", lambda e: e.max(out=m8[:], in_=sc2[:]), ["sc2"], ["m8"])
                        SC.op("dve", lambda e: e.match_replace(out=sc3[:], in_to_replace=m8[:], in_values=sc2[:], imm_value=-5.0),
                              ["sc2", "m8"], ["sc3"])
                        TT(selq[sub][:], sc3[:], sc1[:], ALU.not_equal, ["sc3", "sc1"], [f"selq{sub}"])

                for r in range(4):
                    h = 4 * g + r
                    accs = [psA[0], psA[1]] if r % 2 == 0 else [psA[2], psA[3]]
                    acck = ["psA0", "psA1"] if r % 2 == 0 else ["psA2", "psA3"]

                    def post_cmp(r=r, h=h, accs=accs, acck=acck):
                        rc = rcs[r]
                        rk_ = f"rcs{r}"
                        for j in range(2):
                            a2 = accs[j][:, 0:386].rearrange("p (s c) -> p s c", c=193)
                            ak = acck[j]
                            TS(rc[:, 0:2], a2[:, :, 192], 1e-30, None, ALU.max, None, [ak], [rk_])
                            RECIP(rc[:, 0:2], rc[:, 0:2], [rk_], [rk_])
                            TT(rc[:, 2:4], rc[:, 0:2], gs[:, 2 * j:2 * j + 2, 3 * h], ALU.mult, [rk_, gk], [rk_])
                            if r == 0:
                                TT(IMP[:, 2 * j:2 * j + 2, :], a2[:, :, 64:192], bc(rc[:, 0:2], [128, 2, 128]), ALU.mult,
                                   [ak, rk_], [f"IMP{j}"])
                            else:
                                TT(impt[:], a2[:, :, 64:192], bc(rc[:, 0:2], [128, 2, 128]), ALU.mult, [ak, rk_], ["impt"])
                                TT(IMP[:, 2 * j:2 * j + 2, :], IMP[:, 2 * j:2 * j + 2, :], impt[:], ALU.add,
                                   ["impt", f"IMP{j}"], [f"IMP{j}"], eng="pool")
                            TT(Yt[:, 2 * j:2 * j + 2, r * 64:(r + 1) * 64], a2[:, :, 0:64], bc(rc[:, 2:4], [128, 2, 64]), ALU.mult,
                               [ak, rk_], [yk + f"_{r}"])
                        if r == 3:
                            selection_chain()
                    cts = [ct for ct in range(n_ct) if not far(h, q0, 16 * (min(NC, ct * 128 + 128) - 1) + 31)]
                    if not cts:
                        cts = [n_ct - 1]
                    for ct in cts:
                        n = min(128, NC - ct * 128)
                        m = b - 4 * ct
                        st = dict(kt=ct, n=n, lhsT=KcA[0:128, ct * 128:ct * 128 + n], rhsQ=qa[0:128, r, :], cols=(0, 512),
                                  rk=["KcA", qk], bias=sl[h] * (16.0 * 128 * ct + 31.0 - q0),
                                  cm=(m if m <= 4 else None), subs=[0, 1, 2, 3], rhsV=CA1[0:n, ct, :], vk=["CA1"],
                                  last=(ct == cts[-1]))
                        st["acc"] = (lambda sub, accs=accs, acck=acck:
                                     (accs[sub // 2][:, (sub % 2) * 193:(sub % 2) * 193 + 193], acck[sub // 2]))
                        st["first"] = (lambda sub, ct=ct, c0_=cts[0]: ct == c0_ and sub % 2 == 0)
                        if ct == cts[-1]:
                            st["after"] = post_cmp
                        gb_steps.append(st)

                def post_sw(r, bi):
                    h = 4 * g + r
                    rc = rcs[r]
                    rk_ = f"rcs{r}"
                    a = psA[r]
                    ak = f"psA{r}"
                    av = a[:, 0:260].rearrange("p (s c) -> p s c", c=65)
                    TS(rc[:, 0:4], av[:, :, 64], 1e-30, None, ALU.max, None, [ak], [rk_])
                    RECIP(rc[:, 0:4], rc[:, 0:4], [rk_], [rk_])
                    TT(rc[:, 4:8], rc[:, 0:4], gs[:, :, 3 * h + bi], ALU.mult, [rk_, gk], [rk_])
                    yt_ = ytmp[r % 2]
                    TT(yt_[:], av[:, :, 0:64], bc(rc[:, 4:8], [128, 4, 64]), ALU.mult, [ak, rk_], [f"ytmp{r % 2}"])
                    TT(Yt[:, :, r * 64:(r + 1) * 64], Yt[:, :, r * 64:(r + 1) * 64], yt_[:], ALU.add,
                       [f"ytmp{r % 2}", yk + f"_{r}"], [yk + f"_{r}"], eng="pool")

                def mask_one(kt):
                    d = kt - 4 * b
                    c0 = 128 * d if d > 0 else 0
                    ps, pk = next_s()
                    MM(ps[:, c0:512], Gm[:, kt * 128:(kt + 1) * 128], selT[:, c0:512], True, True, ["Gm", "selT"], [pk])
                    if kt % 3 != 2:
                        TS(maskS[:, kt, c0:512], ps[:, c0:512], 30000.0, -30000.0, ALU.mult, ALU.add, [pk], [f"mS{kt}"])
                    else:
                        CP(maskS[:, kt, c0:512], ps[:, c0:512], [pk], [f"mS{kt}"])

                MLOOK = 2

                def mask_build(kts):
                    ps, pk = next_s()
                    for sub in range(4):
                        TR(ps[:, sub * 128:(sub + 1) * 128], selq[sub][:], ident[:], [f"selq{sub}", "ident"], [pk])
                    CP(selT[:], ps[:], [pk], ["selT"])
                    for kt in kts[:MLOOK]:
                        mask_one(kt)

                for kind in ("win", "sel"):
                    bi = 1 if kind == "sel" else 2
                    KA, KAk = (KsA, "KsA") if kind == "sel" else (KwA, "KwA")
                    VA, VAk = (Vs1, "Vs1") if kind == "sel" else (Vw1, "Vw1")
                    kts = list(range(0, 4 * b + 4)) if kind == "sel" else [kt for kt in range(4 * b - 4, 4 * b + 4) if kt >= 0]
                    started = set()
                    seen_kt = set()
                    first_of_kind = True
                    used_kts = [kt for kt in kts if any(not far(4 * g + r, q0, 128 * kt + 127) for r in range(4))]
                    for kt in kts:
                        d = kt - 4 * b
                        if kind == "sel":
                            subs = [0, 1, 2, 3] if d < 0 else list(range(d, 4))
                            trim = (0, d) if d >= 0 else None
                        else:
                            subs = list(range(max(d, 0), min(d + 4, 3) + 1))
                            trim = None
                            if d >= 0:
                                trim = (0, d)
                            elif d + 4 <= 3:
                                trim = (1, d + 4)
                        c0, c1 = subs[0] * 128, (subs[-1] + 1) * 128
                        for r in range(4):
                            h = 4 * g + r
                            if far(h, q0, 128 * kt + 127):
                                continue
                            st = dict(kt=kt, n=128, lhsT=KA[0:128, kt * 128:(kt + 1) * 128], rhsQ=qa[0:128, r, :],
                                      cols=(c0, c1), rk=[KAk, qk], bias=sl[h] * (128.0 * kt - q0), subs=subs,
                                      rhsV=VA[:, kt, :], vk=[VAk], last=(kt == kts[-1]), tri=trim,
                                      mk=(kt if kind == "sel" else None))
                            st["acc"] = (lambda sub, r=r: (psA[r][:, sub * 65:sub * 65 + 65], f"psA{r}"))

                            def first(sub, r=r, started=started):
                                key = (r, sub)
                                if key in started:
                                    return False
                                isf = not any(k_[0] == r for k_ in started)
                                started.add(key)
                                return isf
                            st["first"] = first
                            if first_of_kind:
                                first_of_kind = False
                                if kind == "win":
                                    if nxt is not None:
                                        st["before"] = (lambda nxt=nxt: loads_gb(*nxt))
                                else:
                                    def bf0(used_kts=used_kts):
                                        mask_build(used_kts)
                                        if len(used_kts) > MLOOK:
                                            mask_one(used_kts[MLOOK])
                                    st["before"] = bf0
                                    seen_kt.add(kt)
                            elif kind == "sel" and kt not in seen_kt:
                                seen_kt.add(kt)
                                j_ = used_kts.index(kt)
                                if j_ + MLOOK < len(used_kts):
                                    st["before"] = (lambda ktn=used_kts[j_ + MLOOK]: mask_one(ktn))
                            if kt == kts[-1]:
                                if kind == "sel" and r == 3:
                                    def fin(r=r, bi=bi):
                                        post_sw(r, bi)
                                        DMA("sp", ya_d[q0:q0 + 512, g * 256:(g + 1) * 256].rearrange("(s p) c -> p s c", p=128),
                                            Yt[:], [yk + f"_{rr}" for rr in range(4)], [])
                                    st["after"] = fin
                                else:
                                    st["after"] = (lambda r=r, bi=bi: post_sw(r, bi))
                            gb_steps.append(st)
                return gb_steps

            idx = 0
            for g in range(4):
                loads_group(g)
                loads_gb(g, 0, idx % 2)
                g_steps = []
                for b in range(NT):
                    qi = idx % 2
                    nxt = (g, b + 1, 1 - qi) if b + 1 < NT else None
                    g_steps.extend(build_gb(g, b, qi, nxt))
                    idx += 1
                run_pipeline(g_steps)
            SC.emit()

        def load_bf16_w(dst, src, nk, keyp):
            for k in range(nk):
                DMA("pool", dst[:, k, :], src[k * 128:(k + 1) * 128, :], (), [f"{keyp}{k}"], max_dma_last_dim=4096)

        def post_norm_residual(ps_pair, pk_pair, res, resk, gtile, gk, outt, outk, tmp, ssq2, sfx):
            sfx = sfx or ""
            kt_, k0, k1, kr = "pn_tmp" + sfx, "pn_ssq0" + sfx, "pn_ssq1" + sfx, "pn_rstd" + sfx
            for hf in range(2):
                ACT(tmp[:, hf * 512:(hf + 1) * 512], ps_pair[hf][:], AF.Square, [pk_pair[hf]], [kt_, (k0, k1)[hf]],
                    accum=ssq2[:, hf:hf + 1])
            TT(ssq2[:, 2:3], ssq2[:, 0:1], ssq2[:, 1:2], ALU.add, [k0, k1], [kr])
            rms_rstd(ssq2[:, 2:3], ssq2[:, 2:3], [kr], [kr])
            for hf in range(2):
                STT(tmp[:, hf * 512:(hf + 1) * 512], ps_pair[hf][:], ssq2[:, 2:3], gtile[:, hf * 512:(hf + 1) * 512],
                    ALU.mult, ALU.mult, [pk_pair[hf], kr, gk], [kt_])
            TT(outt, tmp[:], res, ALU.add, [kt_, resk], [outk], eng="pool")

        ffn_es = ExitStack()
        Wgu = ffn_es.enter_context(nc.sbuf_tensor("sb_Wgu", [128, 8, 2 * DFF], BF16))
        Wd = ffn_es.enter_context(nc.sbuf_tensor("sb_Wd", [128, 22, D], BF16))
        with ExitStack() as es:
            def sb(name, shape, dt=F32):
                return es.enter_context(nc.sbuf_tensor("sb_" + name, list(shape), dt))
            Wo = sb("Wo", [128, 8, D], BF16)
            gpost = sb("gpost", [128, D])
            yat = [sb(f"yat{i}", [128, 512]) for i in range(2)]
            gat = [sb(f"gat{i}", [128, 512], BF16) for i in range(4)]
            yrt = [sb(f"yrt{i}", [128, 512], BF16) for i in range(4)]
            yTs = [sb(f"yT{i}", [128, 8, 512], BF16) for i in range(2)]
            tmpfs = [sb(f"tmpf{i}", [128, 512]) for i in range(2)]
            xres = [sb(f"xres{i}", [128, D]) for i in range(3)]
            tmps = [sb(f"pn_tmp{i}", [128, D]) for i in range(2)]
            ssq2s = [sb(f"pn_ssq{i}", [128, 3]) for i in range(2)]
            psT4 = [es.enter_context(nc.psum_tensor(f"psT4{i}", [128, 512], F32)) for i in range(4)]
            psU = [es.enter_context(nc.psum_tensor(f"psU{i}", [128, 1024], BF16)) for i in range(4)]
            yab = [sb(f"yab{i}", [128, 512], BF16) for i in range(2)]
            load_bf16_w(Wo, w_out_d, 8, "Wo")
            DMA("sp", gpost[:], gains_d[1], (), ["gpost"])
            load_bf16_w(Wgu, wgu_d, 8, "Wgu")
            load_bf16_w(Wd, wd_d, 22, "Wd")
            xcnt = [0]

            def merge4(b):
                t0 = b * 512
                yT = yTs[b % 2]
                ytk_ = f"yT{b % 2}_"
                for half in range(2):
                    for s in range(4):
                        yt_ = yat[s % 2]
                        ytk = f"yat{s % 2}"
                        r0 = t0 + s * 128
                        DMA("sp", yt_[:, 0:512], ya_d[r0:r0 + 128, half * 512:(half + 1) * 512], (), [ytk])
                        yb_ = yab[s % 2]
                        ybk = f"yab{s % 2}"
                        ACT(yb_[:], yt_[:, 0:512], AF.Identity, [ytk], [ybk])
                        for cc in range(4):
                            TR(psU[cc][:, s * 128:(s + 1) * 128], yb_[:, cc * 128:(cc + 1) * 128], identb[:], [ybk, "identb"],
                               [f"psU{cc}"])
                    for cc in range(4):
                        c = half * 4 + cc
                        ga_ = gat[c % 4]
                        yr_ = yrt[c % 4]
                        tf = tmpfs[c % 2]
                        DMA("sp", ga_[:], ga_d[c * 128:(c + 1) * 128, t0:t0 + 512], (), [f"gat{c % 4}"])
                        DMA("sp", yr_[:], yr_d[c * 128:(c + 1) * 128, t0:t0 + 512], (), [f"yrt{c % 4}"])
                        TT(tf[:], ga_[:], psU[cc][:, 0:512], ALU.mult, [f"gat{c % 4}", f"psU{cc}"], [f"tmpf{c % 2}"])
                        TT(yT[:, c, :], tf[:], yr_[:], ALU.add, [f"tmpf{c % 2}", f"yrt{c % 4}"], [ytk_ + str(c)],
                           eng=("pool" if c % 2 == 0 else "dve"))

            def proj4(b):
                t0 = b * 512
                yT = yTs[b % 2]
                ytk_ = f"yT{b % 2}_"
                for s in range(4):
                    r0 = t0 + s * 128
                    xi = xcnt[0] % 3
                    xcnt[0] += 1
                    xr_ = xres[xi]
                    xrk = f"xres{xi}"
                    DMA("sp", xr_[:], x_d[r0:r0 + 128, :], (), [xrk])
                    pp = [psT4[(s % 2) * 2], psT4[(s % 2) * 2 + 1]]
                    ppk = [f"psT4{(s % 2) * 2}", f"psT4{(s % 2) * 2 + 1}"]
                    for hf in range(2):
                        for k in range(8):
                            MM(pp[hf][:], yT[:, k, s * 128:(s + 1) * 128], Wo[:, k, hf * 512:(hf + 1) * 512], k == 0, k == 7,
                               [ytk_ + str(k), f"Wo{k}"], [ppk[hf]])
                    post_norm_residual(pp, ppk, xr_[:], xrk, gpost, "gpost", xr_[:], xrk, tmps[s % 2], ssq2s[s % 2], f"a{s % 2}")
                    DMA("pool", x1_d[r0:r0 + 128, :], xr_[:], [xrk], [])

            merge4(0)
            for b in range(NT):
                if b + 1 < NT:
                    merge4(b + 1)
                proj4(b)
            SC.emit()

        with ExitStack() as es:
            def sb(name, shape, dt=F32):
                return es.enter_context(nc.sbuf_tensor("sb_" + name, list(shape), dt))
            TW = 256
            gfpre = sb("gfpre", [128, D])
            gfpost = sb("gfpost", [128, D])
            x1t = [sb(f"x1t{i}", [128, D]) for i in range(4)]
            h2 = sb("h2", [128, D], BF16)
            h2Ts = [sb(f"h2T{i}", [128, 8, TW], BF16) for i in range(2)]
            aT = sb("aT", [128, 22, TW], BF16)
            sgt = [sb(f"sgt{i}", [128, TW]) for i in range(2)]
            junk = sb("junk4", [128, D])
            ssq = sb("ssq4", [128, 2])
            tmp = sb("pn_tmp4", [128, D])
            ssq2 = sb("pn_ssq4", [128, 3])
            xo = [sb(f"xo4{i}", [128, D]) for i in range(2)]
            psTt = [es.enter_context(nc.psum_tensor(f"psTt{i}", [128, 1024], BF16)) for i in range(2)]
            psGU = [es.enter_context(nc.psum_tensor(f"psGU{i}", [128, 512], F32)) for i in range(4)]
            psD = [es.enter_context(nc.psum_tensor(f"psD{i}", [128, 512], F32)) for i in range(2)]
            DMA("sp", gfpre[:], gains_d[2], (), ["gfpre"])
            DMA("sp", gfpost[:], gains_d[3], (), ["gfpost"])
            NS = TW // 128
            gi = [0]
            NB4 = S // TW

            def prep4(b):
                t0 = b * TW
                hT_ = h2Ts[b % 2]
                for s in range(NS):
                    r0 = t0 + s * 128
                    xi = (b % 2) * NS + s
                    xt = x1t[xi]
                    xk = f"x1t{xi}"
                    DMA("sp", xt[:], x1_d[r0:r0 + 128, :], (), [xk])
                    ACT(junk[:], xt[:], AF.Square, [xk], ["junk4", "ssq4"], accum=ssq[:, 0:1])
                    rms_rstd(ssq[:, 0:1], ssq[:, 0:1], ["ssq4"], ["ssq4"])
                    STT(h2[:], xt[:], ssq[:, 0:1], gfpre[:], ALU.mult, ALU.mult, [xk, "ssq4", "gfpre"], ["h2"])
                    for c in range(8):
                        TR(psTt[c // 4][:, (c % 4) * 128:(c % 4 + 1) * 128], h2[:, c * 128:(c + 1) * 128], identb[:], ["h2", "identb"],
                           [f"psTt{c // 4}"])
                    for hh in range(2):
                        src = psTt[hh][:, 0:512].rearrange("p (c t) -> p c t", c=4)
                        if hh == 0:
                            CP(hT_[:, hh * 4:(hh + 1) * 4, s * 128:(s + 1) * 128], src, [f"psTt{hh}"], [f"h2T{b % 2}_{s}"])
                        else:
                            ACT(hT_[:, hh * 4:(hh + 1) * 4, s * 128:(s + 1) * 128], src, AF.Identity, [f"psTt{hh}"],
                                [f"h2T{b % 2}_{s}"])

            def gu4(b):
                hT_ = h2Ts[b % 2]
                hk = [f"h2T{b % 2}_{s}" for s in range(NS)]
                for f in range(22):
                    pg = psGU[gi[0] % 4]
                    pgk = f"psGU{gi[0] % 4}"
                    gi[0] += 1
                    for k in range(8):
                        MM(pg[:, 0:TW], Wgu[:, k, f * 128:(f + 1) * 128], hT_[:, k, :], k == 0, k == 7, [f"Wgu{k}"] + hk, [pgk])
                    for k in range(8):
                        MM(pg[:, TW:2 * TW], Wgu[:, k, DFF + f * 128:DFF + (f + 1) * 128], hT_[:, k, :], k == 0, k == 7,
                           [f"Wgu{k}"] + hk, [pgk])
                    sg = sgt[f % 2]
                    ACT(sg[:], pg[:, 0:TW], AF.Silu, [pgk], [f"sgt{f % 2}"])
                    TT(aT[:, f, :], sg[:], pg[:, TW:2 * TW], ALU.mult, [f"sgt{f % 2}", pgk], [f"aT{f}"])

            def down4(b):
                t0 = b * TW
                for s in range(NS):
                    r0 = t0 + s * 128
                    xi = (b % 2) * NS + s
                    xt = x1t[xi]
                    xk = f"x1t{xi}"
                    for hf in range(2):
                        for f in range(22):
                            MM(psD[hf][:], aT[:, f, s * 128:(s + 1) * 128], Wd[:, f, hf * 512:(hf + 1) * 512], f == 0, f == 21,
                               [f"aT{f}", f"Wd{f}"], [f"psD{hf}"])
                    xo_ = xo[s % 2]
                    xok = f"xo4{s % 2}"
                    post_norm_residual(psD, ["psD0", "psD1"], xt[:], xk, gfpost, "gfpost", xo_[:], xok, tmp, ssq2, None)
                    DMA("pool", x2_d[r0:r0 + 128, :], xo_[:], [xok], [])

            prep4(0)
            for b in range(NB4):
                gu4(b)
                if b + 1 < NB4:
                    prep4(b + 1)
                down4(b)
            SC.emit()

        ffn_es.close()

        with ExitStack() as es:
            def sb(name, shape, dt=F32):
                return es.enter_context(nc.sbuf_tensor("sb_" + name, list(shape), dt))
            Wpg = sb("Wpg", [128, 8, D], BF16)
            Wpp = sb("Wpp", [128, 2, D], BF16)
            bple = sb("bple", [128, D])
            x2t = [sb(f"x2t{i}", [128, D]) for i in range(3)]
            pt_ = [sb(f"ptl{i}", [128, 256]) for i in range(3)]
            x2Ts = [sb(f"x2T{i}", [128, 8, 128], BF16) for i in range(2)]
            pTs = [sb(f"pT{i}", [128, 2, 128], BF16) for i in range(2)]
            tgs = [sb(f"tg{i}", [128, D]) for i in range(2)]
            og = [sb(f"og{i}", [128, D]) for i in range(2)]
            psTt = [es.enter_context(nc.psum_tensor(f"psTc{i}", [128, 1024], BF16)) for i in range(3)]
            x2b = [sb(f"x2b{i}", [128, D], BF16) for i in range(2)]
            plb = [sb(f"plb{i}", [128, 256], BF16) for i in range(2)]
            psGp = [es.enter_context(nc.psum_tensor(f"psGp{i}", [128, 512], F32)) for i in range(2)]
            psPp = [es.enter_context(nc.psum_tensor(f"psPp{i}", [128, 512], F32)) for i in range(2)]
            load_bf16_w(Wpg, wpg_d, 8, "Wpg")
            load_bf16_w(Wpp, wpp_d, 2, "Wpp")
            DMA("sp", bple[:], gains_d[4], (), ["bple"])
            def prep_c(i_):
                r0 = i_ * 128
                xt = x2t[i_ % 3]
                xk = f"x2t{i_ % 3}"
                pl = pt_[i_ % 3]
                plk = f"ptl{i_ % 3}"
                x2T, x2Tk = x2Ts[i_ % 2], f"x2T{i_ % 2}"
                pT, pTk = pTs[i_ % 2], f"pT{i_ % 2}"
                DMA("sp", xt[:], x2_d[r0:r0 + 128, :], (), [xk])
                DMA("sp", pl[:], p_d[r0:r0 + 128, :], (), [plk])
                xb_, xbk = x2b[i_ % 2], f"x2b{i_ % 2}"
                pb_, pbk = plb[i_ % 2], f"plb{i_ % 2}"
                ACT(xb_[:], xt[:], AF.Identity, [xk], [xbk])
                CP(pb_[:], pl[:], [plk], [pbk], eng="pool")
                for c in range(8):
                    TR(psTt[c // 4][:, (c % 4) * 128:(c % 4 + 1) * 128], xb_[:, c * 128:(c + 1) * 128], identb[:], [xbk, "identb"],
                       [f"psTc{c // 4}"])
                for c in range(2):
                    TR(psTt[2][:, c * 128:(c + 1) * 128], pb_[:, c * 128:(c + 1) * 128], identb[:], [pbk, "identb"], ["psTc2"])
                for hh in range(2):
                    CP(x2T[:, hh * 4:(hh + 1) * 4, :], psTt[hh][:, 0:512].rearrange("p (c t) -> p c t", c=4), [f"psTc{hh}"], [x2Tk])
                ACT(pT[:, :, :], psTt[2][:, 0:256].rearrange("p (c t) -> p c t", c=2), AF.Identity, ["psTc2"], [pTk])

            def main_c(i_):
                r0 = i_ * 128
                xt = x2t[i_ % 3]
                xk = f"x2t{i_ % 3}"
                x2T, x2Tk = x2Ts[i_ % 2], f"x2T{i_ % 2}"
                pT, pTk = pTs[i_ % 2], f"pT{i_ % 2}"
                tg, tgk = tgs[i_ % 2], f"tg{i_ % 2}_"
                o_ = og[i_ % 2]
                ok_ = f"og{i_ % 2}"
                for hf in range(2):
                    for k in range(8):
                        MM(psGp[hf][:], x2T[:, k, :], Wpg[:, k, hf * 512:(hf + 1) * 512], k == 0, k == 7, [x2Tk, f"Wpg{k}"],
                           [f"psGp{hf}"])
                    for k in range(2):
                        MM(psPp[hf][:], pT[:, k, :], Wpp[:, k, hf * 512:(hf + 1) * 512], k == 0, k == 1, [pTk, f"Wpp{k}"],
                           [f"psPp{hf}"])
                    cs = slice(hf * 512, (hf + 1) * 512)
                    TT(tg[:, cs], psGp[hf][:], bple[:, cs], ALU.add, [f"psGp{hf}", "bple"], [tgk + str(hf)])
                    ACT(tg[:, cs], tg[:, cs], AF.Sigmoid, [tgk + str(hf)], [tgk + str(hf)])
                    TT(tg[:, cs], tg[:, cs], psPp[hf][:], ALU.mult, [tgk + str(hf), f"psPp{hf}"], [tgk + str(hf)])
                    TT(o_[:, cs], tg[:, cs], xt[:, cs], ALU.add, [tgk + str(hf), xk], [ok_ + str(hf)], eng="pool")
                DMA("pool", out_d[r0:r0 + 128, :], o_[:], [ok_ + "0", ok_ + "1"], [])

            NI = S // 128
            prep_c(0)
            for i_ in range(NI):
                if i_ + 1 < NI:
                    prep_c(i_ + 1)
                main_c(i_)
            SC.emit()
    nc._n_ops = SC.total
    return nc


def prep_shared(inp, S):
    f = lambda a: np.ascontiguousarray(np.asarray(a, dtype=np.float32))
    m = {}
    m["w_in"] = f(inp["w_in"][0])
    m["w_out"] = f(inp["w_out"][0])
    m["wgu"] = f(inp["ffn_w_gate_up"][0])
    m["wd"] = f(inp["ffn_w_down"][0])
    m["wpp"] = f(inp["ple_w_proj"][0])
    m["wpg"] = f(inp["ple_w_gate"][0])
    gains = np.stack([np.broadcast_to(np.asarray(inp[k][0], np.float32)[None, :], (128, D)) for k in
                      ("norm_mix_pre", "norm_mix_post", "norm_ffn_pre", "norm_ffn_post", "ple_b_gate")])
    m["gains"] = f(gains)
    cols = []
    cw = np.asarray(inp["conv_w"][0], np.float32)
    for k in range(4):
        cols.append(cw[k].reshape(8, 128).T)
    for key in ("conv_b", "lru_ba", "lru_bx", "lru_lambda"):
        cols.append(np.asarray(inp[key][0], np.float32).reshape(8, 128).T)
    m["pv"] = f(np.concatenate(cols, axis=1))
    for nm, key in (("bda", "lru_wa"), ("bdx", "lru_wx")):
        wsrc = np.asarray(inp[key][0], np.float32)
        bd = np.zeros((128, 8, 128), np.float32)
        for c in range(8):
            bd[0:64, c, 0:64] = wsrc[2 * c]
            bd[64:128, c, 64:128] = wsrc[2 * c + 1]
        m[nm] = bd
    m["ckw1"] = f(inp["cmp_k_w1"][0])
    m["cvw1"] = f(inp["cmp_v_w1"][0])
    m["ckw2"] = f(inp["cmp_k_w2"][0])
    m["cvw2"] = f(inp["cmp_v_w2"][0])
    m["posTk"] = f(np.asarray(inp["cmp_pos_k"][0], np.float32).reshape(16, 128).T)
    m["posTv"] = f(np.asarray(inp["cmp_pos_v"][0], np.float32).reshape(16, 128).T)
    m.update(make_consts(S))
    return m


_NC_CACHE = {}


def kernel(**inputs):
    S = 8192
    x = np.asarray(inputs["x"], np.float32)
    p = np.asarray(inputs["p"], np.float32)
    B = x.shape[0]
    if S not in _NC_CACHE:
        _NC_CACHE[S] = build(S)
    nc = _NC_CACHE[S]
    shared = prep_shared(inputs, S)
    in_maps = []
    for b in range(B):
        m = dict(shared)
        m["x"] = np.ascontiguousarray(x[b])
        m["p"] = np.ascontiguousarray(p[0, b])
        in_maps.append(m)
    res = run_bass_kernel_spmd(nc, in_maps, core_ids=list(range(B)))
    return np.stack([np.asarray(r["out"], np.float32) for r in res.results], axis=0)
```

```python
import numpy as np
import ml_dtypes
from contextlib import ExitStack
import concourse.bass as bass
import concourse.mybir as mybir
from concourse.bass_utils import run_bass_kernel_spmd

F32 = mybir.dt.float32
BF16 = mybir.dt.bfloat16
AF = mybir.ActivationFunctionType
ALU = mybir.AluOpType

D = 1024
DIN = 6704
DFF = 2816
EPS = 1e-6
ENGS = ("pe", "act", "dve", "pool", "sp")
C_XR, C_GR, C_Q, C_KC, C_VC, C_KS, C_VS, C_KW, C_VW, C_GN, C_GMR, C_GMA = (
    0, 1024, 2048, 3072, 3328, 3584, 3840, 4096, 4352, 4608, 4656, 5680)


class Sched:
    def __init__(self, nc, n_dma_sems=8):
        self.nc = nc
        self.nd = n_dma_sems
        self.semkey = {}
        for e in ENGS:
            self.semkey[("E", e)] = nc.alloc_semaphore(name=f"S_{e}")
        self.dmaq = ("sp", "act", "pool")
        for e in self.dmaq:
            for i in range(n_dma_sems):
                self.semkey[("D", e, i)] = nc.alloc_semaphore(name=f"D_{e}{i}")
        self.eng_cnt = {e: 0 for e in ENGS}
        self.dma_cnt = {e: 0 for e in self.dmaq}
        self.known = {e: {} for e in ENGS}
        self.dma_sem_last = {}
        self.ops = []
        self.total = 0

    def op(self, eng, fn, r=(), w=(), dma=False):
        self.ops.append((eng, fn, tuple(r), tuple(w), dma))

    def full_clock(self):
        c = {}
        for e in ENGS:
            if self.eng_cnt[e] > 0:
                c[("E", e)] = self.eng_cnt[e]
        for e in self.dmaq:
            for i in range(self.nd):
                n = (self.dma_cnt[e] - i + self.nd - 1) // self.nd
                if n > 0:
                    c[("D", e, i)] = 16 * n
        return c

    def emit(self):
        nc = self.nc
        ops = self.ops
        nops = len(ops)
        self.total += nops
        last_w = {}
        readers = {}
        deps = [None] * nops
        for k, (eng, fn, rd, wr, dma) in enumerate(ops):
            d = set()
            for r in rd:
                if r in last_w:
                    d.add(last_w[r])
            for w in wr:
                if w in last_w:
                    d.add(last_w[w])
                for x in readers.get(w, ()):
                    d.add(x)
            d.discard(k)
            deps[k] = d
            for w in wr:
                last_w[w] = k
                readers[w] = []
            for r in rd:
                if r not in wr:
                    readers.setdefault(r, []).append(k)
        sig = [None] * nops
        clock = [None] * nops
        per_eng = {e: [] for e in ENGS}
        dma_sem_last = {}
        for k, (eng, fn, rd, wr, dma) in enumerate(ops):
            waits = {}
            kn = self.known[eng]
            dl = set(deps[k])
            if dma:
                i = self.dma_cnt[eng] % self.nd
                sk = ("D", eng, i)
                self.dma_cnt[eng] += 1
                if sk in dma_sem_last:
                    dl.add(dma_sem_last[sk])
                dma_sem_last[sk] = k
            for d in dl:
                if (not ops[d][4]) and ops[d][0] == eng and eng == "pe":
                    continue
                s, v = sig[d]
                if kn.get(s, 0) >= v:
                    continue
                if waits.get(s, 0) < v:
                    waits[s] = v
            for d in dl:
                if (not ops[d][4]) and ops[d][0] == eng and eng == "pe":
                    continue
                for cs, cv in clock[d].items():
                    if kn.get(cs, 0) < cv:
                        kn[cs] = cv
            if dma:
                val = 16 * ((self.dma_cnt[eng] - 1) // self.nd) + 16
                sig[k] = (sk, val)
                c = dict(kn)
                c[sk] = val
                clock[k] = c
                per_eng[eng].append((waits, fn, sk, 16))
            else:
                self.eng_cnt[eng] += 1
                sk = ("E", eng)
                sig[k] = (sk, self.eng_cnt[eng])
                c = dict(kn)
                c[sk] = self.eng_cnt[eng]
                clock[k] = c
                per_eng[eng].append((waits, fn, sk, 1))
        final = self.full_clock()
        engobj = {"pe": nc.tensor, "act": nc.scalar, "dve": nc.vector, "pool": nc.gpsimd, "sp": nc.sync}
        semkey = self.semkey

        def run(ename):
            E = engobj[ename]
            for waits, fn, sk, inc in per_eng[ename]:
                for s, v in waits.items():
                    E.wait_ge(semkey[s], v)
                fn(E).then_inc(semkey[sk], inc)
            kn = self.known[ename]
            for s, v in final.items():
                if s == ("E", ename):
                    continue
                if kn.get(s, 0) < v:
                    E.wait_ge(semkey[s], v)

        with nc.Block() as block:
            @block.tensor
            def _(e):
                run("pe")

            @block.scalar
            def _(e):
                run("act")

            @block.vector
            def _(e):
                run("dve")

            @block.gpsimd
            def _(e):
                run("pool")

            @block.sync
            def _(e):
                run("sp")
        for e in ENGS:
            self.known[e] = dict(final)
        self.ops = []


def _split3(v):
    v = np.asarray(v, np.float32)
    hi = v.astype(ml_dtypes.bfloat16)
    r1 = (v - hi.astype(np.float32)).astype(np.float32)
    mid = r1.astype(ml_dtypes.bfloat16)
    r2 = (r1 - mid.astype(np.float32)).astype(np.float32)
    lo = r2.astype(ml_dtypes.bfloat16)
    return hi, mid, lo


def _slopes():
    h = np.arange(1, 17, dtype=np.float32)
    return np.exp2(-8.0 * h / 16.0).astype(np.float32)


def make_consts(S):
    bf = ml_dtypes.bfloat16
    sl = _slopes()
    c = {}
    c["ident"] = np.eye(128, dtype=np.float32)
    qaug = np.zeros((6, 16, 512), dtype=bf)
    ql = np.arange(512, dtype=np.float32)
    for h in range(16):
        a, b, cc = _split3(np.full((512,), sl[h], np.float32))
        qaug[0, h], qaug[1, h], qaug[2, h] = a, b, cc
        a, b, cc = _split3(-sl[h] * ql)
        qaug[3, h], qaug[4, h], qaug[5, h] = a, b, cc
    c["qaug"] = qaug
    kl = np.arange(S, dtype=np.float32) % 128
    kaug = np.ones((6, S), np.float32)
    kaug[0:3] = kl
    c["kaug"] = kaug.astype(bf)
    kaugc = np.ones((6, 512), np.float32)
    kaugc[0:3] = 16.0 * (np.arange(512, dtype=np.float32) % 128)
    c["kaugc"] = kaugc.astype(bf)
    kk = np.arange(128)[:, None]
    qq = np.arange(128)[None, :]
    tri = (np.stack([(qq >= kk), (qq < kk)]).astype(np.float32) - 1.0) * 30000.0
    c["tri"] = np.ascontiguousarray(tri.transpose(1, 0, 2)).astype(bf)
    cl = np.arange(128)[:, None]
    qL = np.arange(512)[None, :]
    cm = (np.stack([(qL - 16 * cl - 31 + 512 * m >= 0) for m in range(5)]).astype(np.float32) - 1.0) * 30000.0
    c["cmask"] = np.ascontiguousarray(cm.transpose(1, 0, 2)).astype(bf)
    NC = S // 16 - 1
    NCT = (NC + 128) // 128
    A1 = np.zeros((NCT * 128, 129), np.float32)
    for cc_ in range(NC):
        j, rem = cc_ // 4, cc_ % 4
        if rem < 3:
            if j < 128:
                A1[cc_, j] = 2.0
        else:
            if j < 128:
                A1[cc_, j] = 1.0
            if j + 1 < 128:
                A1[cc_, j + 1] = 1.0
        A1[cc_, 128] = 1.0
    c["A1"] = np.ascontiguousarray(A1.reshape(NCT, 128, 129).transpose(1, 0, 2)).astype(bf)
    G = np.zeros((128, S), np.float32)
    y = np.arange(S)
    G[y // 64, y] = 1.0
    c["G"] = G.astype(bf)
    mv = np.zeros((128, 256), np.float32)
    ma = np.zeros((128, 256), np.float32)
    for p in range(128):
        cur = 0 if p < 64 else 1
        for col in range(256):
            jr = col - 126
            valid = jr <= cur
            forced = (jr == cur) or (jr == cur - 1)
            if not valid:
                ma[p, col] = -1.0
            elif forced:
                ma[p, col] = 1e4
            else:
                mv[p, col] = 1.0
    c["mvalid"] = mv
    c["madd"] = ma
    return c


CONST_SPECS = None


def build(S, debug=False):
    assert S % 2048 == 0
    NT = S // 512
    NKT = S // 128
    NC = S // 16 - 1
    NCT = (NC + 128) // 128
    sl = [float(v) for v in _slopes()]
    nc = bass.Bass("TRN2", target_bir_lowering=False)
    skind = "ExternalOutput" if debug else "Internal"

    def din(name, shape, dt=F32):
        return nc.dram_tensor(name, list(shape), dt, kind="ExternalInput").ap()

    def dscr(name, shape, dt=F32):
        return nc.dram_tensor(name, list(shape), dt, kind=skind).ap()

    x_d = din("x", [S, D])
    p_d = din("p", [S, 256])
    w_in_d = din("w_in", [D, DIN])
    w_out_d = din("w_out", [D, D])
    wgu_d = din("wgu", [D, 2 * DFF])
    wd_d = din("wd", [DFF, D])
    wpp_d = din("wpp", [256, D])
    wpg_d = din("wpg", [D, D])
    gains_d = din("gains", [5, 128, D])
    pv_d = din("pv", [128, 64])
    bda_d = din("bda", [128, 8, 128])
    bdx_d = din("bdx", [128, 8, 128])
    cw1_d = [din("ckw1", [2048, 256]), din("cvw1", [2048, 256])]
    cw2_d = [din("ckw2", [256, 64]), din("cvw2", [256, 64])]
    posT_d = [din("posTk", [128, 16]), din("posTv", [128, 16])]
    ident_d = din("ident", [128, 128])
    qaug_d = din("qaug", [6, 16, 512], BF16)
    kaug_d = din("kaug", [6, S], BF16)
    kaugc_d = din("kaugc", [6, 512], BF16)
    tri_d = din("tri", [128, 2, 128], BF16)
    cmask_d = din("cmask", [128, 5, 512], BF16)
    A1_d = din("A1", [128, NCT, 129], BF16)
    G_d = din("G", [128, S], BF16)
    mvalid_d = din("mvalid", [128, 256])
    madd_d = din("madd", [128, 256])
    out_d = nc.dram_tensor("out", [S, D], F32, kind="ExternalOutput").ap()

    qT_d = dscr("s_qT", [D, S], BF16)
    kcT_d = dscr("s_kcT", [256, S], BF16)
    vcT_d = dscr("s_vcT", [256, S], BF16)
    ksT_d = dscr("s_ksT", [256, S], BF16)
    kwT_d = dscr("s_kwT", [256, S], BF16)
    vsw_d = dscr("s_vsw", [S, 512], BF16)
    gsg_d = dscr("s_gsg", [S, 48])
    yr_d = dscr("s_yr", [D, S], BF16)
    ga_d = dscr("s_ga", [D, S], BF16)
    kcc_d = dscr("s_kcc", [256, 512], BF16)
    vcc_d = dscr("s_vcc", [NCT * 128, 256], BF16)
    ya_d = dscr("s_ya", [S, D])
    x1_d = dscr("s_x1", [S, D])
    x2_d = dscr("s_x2", [S, D])

    SC = Sched(nc)

    def MM(out, lhsT, rhs, start, stop, r, w):
        SC.op("pe", lambda e: e.matmul(out, lhsT=lhsT, rhs=rhs, start=start, stop=stop, skip_group_check=True), r, w)

    def TR(out, in_, ident, r, w):
        SC.op("pe", lambda e: e.transpose(out, in_, ident), r, w)

    def ACT(out, in_, func, r, w, bias=0.0, scale=1.0, accum=None):
        SC.op("act", lambda e: e.activation(out=out, in_=in_, func=func, bias=bias, scale=scale, accum_out=accum), r, w)

    def TS(out, in0, s1, s2, op0, op1, r, w, eng="dve", accum=None):
        if op1 is None:
            SC.op(eng, lambda e: e.tensor_scalar(out=out, in0=in0, scalar1=s1, scalar2=None, op0=op0), r, w)
        else:
            SC.op(eng, lambda e: e.tensor_scalar(out=out, in0=in0, scalar1=s1, scalar2=s2, op0=op0, op1=op1), r, w)

    def TT(out, in0, in1, op, r, w, eng="dve"):
        SC.op(eng, lambda e: e.tensor_tensor(out=out, in0=in0, in1=in1, op=op), r, w)

    def STT(out, in0, scalar, in1, op0, op1, r, w):
        SC.op("dve", lambda e: e.scalar_tensor_tensor(out=out, in0=in0, scalar=scalar, in1=in1, op0=op0, op1=op1), r, w)

    def CP(out, in_, r, w, eng="dve"):
        SC.op(eng, lambda e: e.tensor_copy(out=out, in_=in_), r, w)

    def MEMSET(ap, val, w, eng="dve"):
        SC.op(eng, lambda e: e.memset(ap, val), (), w)

    def RECIP(out, in_, r, w):
        SC.op("dve", lambda e: e.reciprocal(out=out, in_=in_), r, w)

    def DMA(q, out, in_, r, w, **kw):
        SC.op(q, lambda e: e.dma_start(out=out, in_=in_, **kw), r, w, dma=True)

    def gelu_tanh(es_tmp, out, src, nparts, ncols, keyp, rkeys, wkeys):
        t1, t2 = es_tmp
        TT(t1, src, src, ALU.mult, rkeys, [keyp + "t1"])
        TS(t1, t1, 0.044715, 1.0, ALU.mult, ALU.add, [keyp + "t1"], [keyp + "t1"])
        TT(t1, t1, src, ALU.mult, [keyp + "t1"] + list(rkeys), [keyp + "t1"])
        ACT(t2, t1, AF.Sigmoid, [keyp + "t1"], [keyp + "t2"], scale=1.5957691216057308)
        TT(out, t2, src, ALU.mult, [keyp + "t2"] + list(rkeys), wkeys)

    def rms_rstd(rstd, ssq, r, w):
        ACT(rstd, ssq, AF.Sqrt, r, w, bias=epsb[:, 0:1], scale=1.0 / D)
        RECIP(rstd, rstd, w, w)

    with ExitStack() as g_es:
        ident = g_es.enter_context(nc.sbuf_tensor("sb_ident", [128, 128], F32))
        identb = g_es.enter_context(nc.sbuf_tensor("sb_identb", [128, 128], BF16))
        epsb = g_es.enter_context(nc.sbuf_tensor("sb_epsb", [128, 1], F32))
        DMA("sp", ident[:], ident_d, (), ["ident"])
        DMA("pool", identb[:], ident_d, (), ["identb"])
        MEMSET(epsb[:], EPS, ["epsb"])
        SC.emit()

        with ExitStack() as es:
            def sb(name, shape, dt=F32):
                return es.enter_context(nc.sbuf_tensor("sb_" + name, list(shape), dt))
            Wi = sb("Wi", [128, 8, DIN], BF16)
            bda = sb("bda", [128, 8, 128], BF16)
            bdx = sb("bdx", [128, 8, 128], BF16)
            pv = sb("pv", [128, 64])
            cneg = sb("cneg", [128, 8])
            cneg2 = sb("cneg2", [128, 8])
            ctmp = sb("ctmp", [128, 8])
            ctmp2 = sb("ctmp2", [128, 8])
            gpre = sb("gpre", [128, D])
            xs = [sb(f"xs{i}", [128, D]) for i in range(2)]
            junk = sb("junk", [128, D])
            hb = [sb("hb0", [128, 4, D], BF16)]
            hT = sb("hT", [128, 8, 512], BF16)
            ssq = sb("ssq", [128, 8])
            xbuf = sb("xbuf", [128, 8, 516])
            hlast = sb("hlast", [128, 8])
            T = {n: sb("t_" + n, [128, 512]) for n in
                 ("xc0", "xc1", "r", "i", "a", "a2", "bb", "hs", "g00", "g01", "t1", "t2", "ge", "sgm0", "sgm1")}
            xcb = [sb(f"xcb{i}", [128, 512], BF16) for i in range(2)]
            ev = [sb(f"ev{i}", [128, 512], BF16) for i in range(3)]
            evf = [sb(f"evf{i}", [128, 512], BF16) for i in range(2)]
            gsb = sb("gsb", [128, 48])
            psT = [es.enter_context(nc.psum_tensor(f"psT{i}", [128, 1024], BF16)) for i in range(2)]
            psM = [es.enter_context(nc.psum_tensor(f"psM{i}", [128, 512], F32)) for i in range(4)]
            psG = [es.enter_context(nc.psum_tensor(f"psG{i}", [128, 512], F32)) for i in range(2)]

            WBLK = [(0, 1024), (1024, 2048), (4656, 5680), (5680, 6704), (2048, 3072), (3072, 4656)]
            for j, (c0_, c1_) in enumerate(WBLK):
                for k in range(8):
                    DMA("pool", Wi[:, k, c0_:c1_], w_in_d[k * 128:(k + 1) * 128, c0_:c1_], (), [f"WiB{j}"],
                        max_dma_last_dim=4096)

            def wkey(col0):
                for j, (c0_, c1_) in enumerate(WBLK):
                    if c0_ <= col0 < c1_:
                        return f"WiB{j}"
                raise ValueError(col0)
            DMA("pool", bda[:], bda_d, (), ["bda"])
            DMA("pool", bdx[:], bdx_d, (), ["bdx"])
            DMA("sp", pv[:], pv_d, (), ["pv"])
            DMA("sp", gpre[:], gains_d[0], (), ["gpre"])
            MEMSET(xbuf[:], 0.0, ["xbuf"])
            MEMSET(hlast[:], 0.0, ["hlast"])
            onepb = sb("onepb", [128, 1])
            MEMSET(onepb[:], 1.0 + 2.0 ** -23, ["onepb"])
            lam = pv[:, 56:64]
            ACT(ctmp[:], lam, AF.Exp, ["pv"], ["ctmp"], scale=-1.0)
            TS(ctmp2[:], ctmp[:], -0.25, 1.0 / 3.0, ALU.mult, ALU.add, ["ctmp"], ["ctmp2"])
            TT(ctmp2[:], ctmp2[:], ctmp[:], ALU.mult, ["ctmp", "ctmp2"], ["ctmp2"])
            TS(ctmp2[:], ctmp2[:], -0.5, None, ALU.add, None, ["ctmp2"], ["ctmp2"])
            TT(ctmp2[:], ctmp2[:], ctmp[:], ALU.mult, ["ctmp", "ctmp2"], ["ctmp2"])
            TS(ctmp2[:], ctmp2[:], 1.0, None, ALU.add, None, ["ctmp2"], ["ctmp2"])
            TT(ctmp2[:], ctmp2[:], ctmp[:], ALU.mult, ["ctmp", "ctmp2"], ["ctmp2"])
            TS(cneg[:], ctmp2[:], -8.0, None, ALU.mult, None, ["ctmp2"], ["cneg"])
            TS(cneg2[:], ctmp2[:], -16.0, None, ALU.mult, None, ["ctmp2"], ["cneg"])

            def norm_tile(b):
                hbuf = hb[0]
                for s in range(4):
                    xt = xs[s % 2]
                    xk = f"xs{s % 2}"
                    r0 = b * 512 + s * 128
                    DMA("sp", xt[:], x_d[r0:r0 + 128, :], (), [xk])
                    col = (b % 2) * 4 + s
                    ACT(junk[:], xt[:], AF.Square, [xk], ["junk", f"ssq{col}"], accum=ssq[:, col:col + 1])
                    rms_rstd(ssq[:, col:col + 1], ssq[:, col:col + 1], [f"ssq{col}"], [f"ssq{col}"])
                    STT(hbuf[:, s, :], xt[:], ssq[:, col:col + 1], gpre[:], ALU.mult, ALU.mult,
                        [xk, f"ssq{col}", "gpre"], [f"hb_{s}"])

            norm_tile(0)
            evi = [0]
            mi = [0]

            def next_ps():
                i = mi[0] % 4
                mi[0] += 1
                return psM[i], f"psM{i}"

            def featmm(col0, ps, pk):
                for k in range(8):
                    MM(ps[:], Wi[:, k, col0:col0 + 128], hT[:, k, :], k == 0, k == 7, [wkey(col0), f"hT{k}"], [pk])

            OMB = 1.0 + 2.0 ** -23

            def conv_chunk(c):
                xk = f"xbuf{c}"
                xc = T[f"xc{c % 2}"]
                xck = f"xc{c % 2}"
                TS(xc[:], xbuf[:, c, 3:515], pv[:, 24 + c:25 + c], pv[:, 32 + c:33 + c], ALU.mult, ALU.add, [xk, "pv"], [xck])
                for kk in range(3):
                    STT(xc[:], xbuf[:, c, kk:kk + 512], pv[:, kk * 8 + c:kk * 8 + c + 1], xc[:], ALU.mult, ALU.add,
                        [xk, "pv", xck], [xck])
                CP(xbuf[:, c, 0:3], xbuf[:, c, 512:515], [xk], [xk], eng="pool")

            def conv_act(c):
                ACT(xcb[c % 2][:], T[f"xc{c % 2}"][:], AF.Identity, [f"xc{c % 2}"], [f"xcb{c % 2}"])

            def other_job(ji, job, t0):
                col0, dst, row0, scl, sig = job
                ps, pk = next_ps()
                featmm(col0, ps, pk)
                if sig:
                    ef = evf[ji % 2]
                    ACT(ef[:], ps[:], AF.Sigmoid, [pk], [f"evf{ji % 2}"])
                    DMA("sp", dst[row0:row0 + 128, t0:t0 + 512], ef[:], [f"evf{ji % 2}"], [])
                else:
                    e_ = ev[evi[0] % 3]
                    ek = f"ev{evi[0] % 3}"
                    evi[0] += 1
                    if ji % 2 == 0:
                        TS(e_[:], ps[:], scl, None, ALU.mult, None, [pk], [ek])
                    else:
                        ACT(e_[:], ps[:], AF.Identity, [pk], [ek], scale=scl)
                    DMA("sp", dst[row0:row0 + 128, t0:t0 + 512], e_[:], [ek], [])

            for b in range(NT):
                t0 = b * 512
                hbuf = hb[0]
                for c in range(8):
                    pt = psT[c % 2]
                    for s in range(4):
                        TR(pt[:, s * 128:(s + 1) * 128], hbuf[:, s, c * 128:(c + 1) * 128], identb[:],
                           [f"hb_{s}", "identb"], [f"psT{c % 2}"])
                    if c % 2 == 0:
                        CP(hT[:, c, :], pt[:, 0:512], [f"psT{c % 2}"], [f"hT{c}"])
                    else:
                        ACT(hT[:, c, :], pt[:, 0:512], AF.Identity, [f"psT{c % 2}"], [f"hT{c}"])
                if b + 1 < NT:
                    norm_tile(b + 1)
                for c in range(8):
                    ps, pk = next_ps()
                    featmm(C_XR + c * 128, ps, pk)
                    ACT(xbuf[:, c, 3:515], ps[:], AF.Identity, [pk], [f"xbuf{c}"])
                jobs = []
                for c in range(8):
                    jobs.append((C_GMA + c * 128, ga_d, c * 128, 1.0, True))
                    jobs.append((C_Q + c * 128, qT_d, c * 128, 0.125, False))
                    base, dst = ((C_KC, kcT_d), (C_VC, vcT_d), (C_KS, ksT_d), (C_KW, kwT_d))[c // 2]
                    jobs.append((base + (c % 2) * 128, dst, (c % 2) * 128, 1.0, False))
                conv_chunk(0)
                conv_act(0)
                for c in range(8):
                    if c + 1 < 8:
                        conv_chunk(c + 1)
                    xc = T[f"xc{c % 2}"]
                    xck = f"xc{c % 2}"
                    xb_, xbk = xcb[c % 2], f"xcb{c % 2}"
                    g0, g0k = T[f"g0{c % 2}"], f"g0{c % 2}"
                    sgm, sgmk = T[f"sgm{c % 2}"], f"sgm{c % 2}"
                    MM(psG[0][:], bda[:, c, :], xb_[:], True, True, ["bda", xbk], ["psG0"])
                    MM(psG[1][:], bdx[:, c, :], xb_[:], True, True, ["bdx", xbk], ["psG1"])
                    psg, pkg = next_ps()
                    featmm(C_GR + c * 128, psg, pkg)
                    psm, pkm = next_ps()
                    featmm(C_GMR + c * 128, psm, pkm)
                    ACT(T["r"][:], psG[0][:], AF.Sigmoid, ["psG0", "pv"], ["r"], bias=pv[:, 40 + c:41 + c])
                    ACT(T["i"][:], psG[1][:], AF.Sigmoid, ["psG1", "pv"], ["i"], bias=pv[:, 48 + c:49 + c])
                    ACT(T["a"][:], T["r"][:], AF.Exp, ["r", "cneg"], ["a"], scale=cneg[:, c:c + 1])
                    TT(T["a2"][:], T["a"][:], T["a"][:], ALU.mult, ["a"], ["a2"], eng="pool")
                    TS(T["a2"][:], T["a2"][:], -1.0, OMB, ALU.mult, ALU.add, ["a2"], ["a2"], eng="pool")
                    if c + 1 < 8:
                        conv_act(c + 1)
                    ACT(g0[:], psg[:], AF.Identity, [pkg], [g0k])
                    ACT(T["a2"][:], T["a2"][:], AF.Ln, ["a2"], ["a2"])
                    ACT(T["a2"][:], T["a2"][:], AF.Exp, ["a2"], ["a2"], scale=0.5)
                    ACT(sgm[:], psm[:], AF.Sigmoid, [pkm], [sgmk])
                    TT(T["bb"][:], T["i"][:], xc[:], ALU.mult, ["i", xck], ["bb"], eng="pool")
                    TT(T["bb"][:], T["bb"][:], T["a2"][:], ALU.mult, ["bb", "a2"], ["bb"])
                    SC.op("dve", lambda e, c=c: e.tensor_tensor_scan(out=T["hs"][:], data0=T["a"][:], data1=T["bb"][:],
                                                                   initial=hlast[:, c:c + 1], op0=ALU.mult, op1=ALU.add),
                          ["a", "bb", "hlast"], ["hs"])
                    CP(hlast[:, c:c + 1], T["hs"][:, 511:512], ["hs"], ["hlast"])
                    t1, t2 = T["t1"], T["t2"]
                    TT(t1[:], g0[:], g0[:], ALU.mult, [g0k], ["t1"], eng="pool")
                    TS(t1[:], t1[:], 0.044715, 1.0, ALU.mult, ALU.add, ["t1"], ["t1"], eng="pool")
                    TT(t1[:], t1[:], g0[:], ALU.mult, ["t1", g0k], ["t1"], eng="pool")
                    for jj in range(3):
                        other_job(3 * c + jj, jobs[3 * c + jj], t0)
                    ACT(t2[:], t1[:], AF.Sigmoid, ["t1"], ["t2"], scale=1.5957691216057308)
                    TT(T["ge"][:], t2[:], g0[:], ALU.mult, ["t2", g0k], ["ge"])
                    TT(T["ge"][:], T["ge"][:], T["hs"][:], ALU.mult, ["ge", "hs"], ["ge"])
                    ef = evf[c % 2]
                    TT(ef[:], T["ge"][:], sgm[:], ALU.mult, ["ge", sgmk], [f"evf{c % 2}"], eng="pool")
                    DMA("sp", yr_d[c * 128:(c + 1) * 128, t0:t0 + 512], ef[:], [f"evf{c % 2}"], [])
                for s in range(4):
                    ps, pk = next_ps()
                    for k in range(8):
                        MM(ps[:, 0:256], hT[:, k, s * 128:(s + 1) * 128], Wi[:, k, C_VS:C_VS + 256], k == 0, k == 7,
                           [wkey(C_VS), f"hT{k}"], [pk])
                    for k in range(8):
                        MM(ps[:, 256:512], hT[:, k, s * 128:(s + 1) * 128], Wi[:, k, C_VW:C_VW + 256], k == 0, k == 7,
                           [wkey(C_VS), f"hT{k}"], [pk])
                    e_ = ev[evi[0] % 3]
                    ek = f"ev{evi[0] % 3}"
                    evi[0] += 1
                    CP(e_[:], ps[:], [pk], [ek])
                    DMA("sp", vsw_d[t0 + s * 128:t0 + (s + 1) * 128, :], e_[:], [ek], [])
                    ps, pk = next_ps()
                    for k in range(8):
                        MM(ps[:, 0:48], hT[:, k, s * 128:(s + 1) * 128], Wi[:, k, C_GN:C_GN + 48], k == 0, k == 7,
                           [wkey(C_VS), f"hT{k}"], [pk])
                    ACT(gsb[:], ps[:, 0:48], AF.Sigmoid, [pk], ["gsb"])
                    DMA("sp", gsg_d[t0 + s * 128:t0 + (s + 1) * 128, :], gsb[:], ["gsb"], [])
            SC.emit()

        with ExitStack() as es:
            def sb(name, shape, dt=F32):
                return es.enter_context(nc.sbuf_tensor("sb_" + name, list(shape), dt))
            w1 = [sb(f"cw1_{i}", [128, 16, 256], BF16) for i in range(2)]
            w2 = [sb(f"cw2_{i}", [128, 2, 64], BF16) for i in range(2)]
            posT = [sb(f"posT{i}", [128, 16], BF16) for i in range(2)]
            cb = [sb(f"cbias{i}", [128, 2]) for i in range(2)]
            kin = [sb(f"kin{i}", [128, S], BF16) for i in range(3)]
            kinr = [sb(f"kinr{i}", [128, 8, S // 16], BF16) for i in range(2)]
            u = sb("c_u", [128, 512])
            t1 = sb("c_t1", [128, 512])
            t2 = sb("c_t2", [128, 512])
            Hh = [sb(f"c_H{i}", [128, 512], BF16) for i in range(2)]
            okc = sb("c_okc", [64, 512], BF16)
            ovc = sb("c_ovc", [128, 64], BF16)
            psH = [es.enter_context(nc.psum_tensor(f"psH{i}", [128, 512], F32)) for i in range(2)]
            psO = [es.enter_context(nc.psum_tensor(f"psO{i}", [128, 512], F32)) for i in range(2)]
            psB = es.enter_context(nc.psum_tensor("psB", [128, 512], F32))
            for kv in range(2):
                DMA("pool", w1[kv][:], cw1_d[kv].rearrange("(lc p) h -> p lc h", p=128), (), [f"w1{kv}"])
                DMA("pool", w2[kv][:], cw2_d[kv].rearrange("(c p) d -> p c d", p=128), (), [f"w2{kv}"])
                DMA("pool", posT[kv][:], posT_d[kv], (), [f"posT{kv}"])
            MEMSET(okc[:], 0.0, ["okc"])
            MEMSET(ovc[:], 0.0, ["ovc"])
            for i3 in range(3):
                MEMSET(kin[i3][64:128, S - 16:S], 0.0, [f"kin{i3}"])
            def p2_load(it):
                kv_, g_ = it // 4, it % 4
                src_ = kcT_d if kv_ == 0 else vcT_d
                kt__ = kin[it % 3]
                kk__ = f"kin{it % 3}"
                DMA("sp", kt__[0:64, :], src_[g_ * 64:(g_ + 1) * 64, :], (), [kk__])
                DMA("sp", kt__[64:128, 0:S - 1], src_[g_ * 64:(g_ + 1) * 64, 1:S], (), [kk__])

            p2_load(0)
            ki = 0
            for kv in range(2):
                src_d = kcT_d if kv == 0 else vcT_d
                for hc in range(2):
                    for lc in range(16):
                        MM(psB[:, hc:hc + 1], w1[kv][:, lc, hc * 128:(hc + 1) * 128], posT[kv][:, lc:lc + 1], lc == 0, lc == 15,
                           [f"w1{kv}", f"posT{kv}"], ["psB"])
                CP(cb[kv][:], psB[:, 0:2], ["psB"], [f"cb{kv}"])
                for g in range(4):
                    kt_ = kin[ki % 3]
                    kk_ = f"kin{ki % 3}"
                    kr_ = kinr[ki % 2]
                    krk = f"kinr{ki % 2}"
                    ki += 1
                    if ki < 8:
                        p2_load(ki)
                    kv4 = kt_[:].rearrange("p (i j t) -> p j t i", j=8, t=2)
                    CP(kr_[:, 0:4, :], kv4[:, 0:4, 0, :], [kk_], [krk + "a"])
                    CP(kr_[:, 4:8, :], kv4[:, 4:8, 0, :], [kk_], [krk + "b"], eng="pool")
                    for hc in range(2):
                        ph = psH[hc]
                        for lc in range(16):
                            j = lc % 8
                            rhs_ = kr_[:, j, 0:NC] if lc < 8 else kr_[:, j, 1:NC + 1]
                            MM(ph[:, 0:NC], w1[kv][:, lc, hc * 128:(hc + 1) * 128], rhs_,
                               lc == 0, lc == 15, [f"w1{kv}", krk + ("a" if j < 4 else "b")], [f"psH{hc}"])
                        ACT(u[:, 0:NC], ph[:, 0:NC], AF.Identity, [f"psH{hc}", f"cb{kv}"], ["c_u"], bias=cb[kv][:, hc:hc + 1])
                        gelu_tanh((t1[:, 0:NC], t2[:, 0:NC]), Hh[hc][:, 0:NC], u[:, 0:NC], 128, NC, "p2", ["c_u"], [f"H{hc}"])
                    if kv == 0:
                        po = psO[0]
                        for hc in range(2):
                            MM(po[0:64, 0:NC], w2[kv][:, hc, :], Hh[hc][:, 0:NC], hc == 0, hc == 1,
                               [f"w2{kv}", f"H{hc}"], ["psO0"])
                        CP(okc[:, 0:NC], po[0:64, 0:NC], ["psO0"], ["okc"])
                        DMA("act", kcc_d[g * 64:(g + 1) * 64, :], okc[:], ["okc"], [])
                    else:
                        for ct in range(NCT):
                            n = min(128, NC - ct * 128)
                            po = psO[ct % 2]
                            for hc in range(2):
                                MM(po[0:n, 0:64], Hh[hc][:, ct * 128:ct * 128 + n], w2[kv][:, hc, :], hc == 0, hc == 1,
                                   [f"w2{kv}", f"H{hc}"], [f"psO{ct % 2}"])
                            CP(ovc[0:n, :], po[0:n, 0:64], [f"psO{ct % 2}"], ["ovc"])
                            DMA("act", vcc_d[ct * 128:ct * 128 + 128, g * 64:(g + 1) * 64], ovc[:, :], ["ovc"], [])
            SC.emit()

        with ExitStack() as es:
            def sb(name, shape, dt=F32):
                return es.enter_context(nc.sbuf_tensor("sb_" + name, list(shape), dt))
            DEPTH = 3
            NPB = 6
            KsA = sb("KsA", [128, S], BF16)
            KwA = sb("KwA", [128, S], BF16)
            KcA = sb("KcA", [128, 512], BF16)
            Vs1 = sb("Vs1", [128, NKT, 65], BF16)
            Vw1 = sb("Vw1", [128, NKT, 65], BF16)
            CA1 = sb("CA1", [128, NCT, 193], BF16)
            Gm = sb("Gm", [128, S], BF16)
            tri = sb("tri", [128, 2, 128], BF16)
            cmask = sb("cmask", [128, 5, 512], BF16)
            mvalid = sb("mvalid", [128, 256])
            madd = sb("madd", [128, 256])
            maskS = sb("maskS", [128, NKT, 512], BF16)
            QA = [sb(f"QA{i}", [128, 4, 512], BF16) for i in range(2)]
            gsig = [sb(f"gsig{i}", [128, 4, 48]) for i in range(2)]
            Pb = [sb(f"Pb{i}", [128, 512], BF16) for i in range(NPB)]
            P2 = [sb(f"P2{i}", [128, 512], BF16) for i in range(NPB)]
            IMP = sb("IMP", [128, 4, 128])
            impt = sb("impt", [128, 2, 128])
            sc1 = sb("sc1", [128, 128])
            sc2 = sb("sc2", [128, 128])
            sc3 = sb("sc3", [128, 128])
            m8 = sb("m8", [128, 8])
            selq = [sb(f"selq{i}", [128, 128]) for i in range(4)]
            selT = sb("selT", [128, 512], BF16)
            Y = [sb(f"Y{i}", [128, 4, 256]) for i in range(2)]
            ytmp = [sb(f"ytmp{i}", [128, 4, 64]) for i in range(2)]
            rcs = [sb(f"rcs{i}", [128, 8]) for i in range(4)]
            psS = [es.enter_context(nc.psum_tensor(f"psS{i}", [128, 512], F32)) for i in range(4)]
            psA = [es.enter_context(nc.psum_tensor(f"psA{i}", [128, 512], F32)) for i in range(4)]

            DMA("sp", Gm[:], G_d, (), ["Gm"])
            DMA("sp", tri[:], tri_d, (), ["tri"])
            DMA("sp", cmask[:], cmask_d, (), ["cmask"])
            DMA("sp", mvalid[:], mvalid_d, (), ["mvalid"])
            DMA("sp", madd[:], madd_d, (), ["madd"])
            MEMSET(KsA[64:128, :], 0.0, ["KsA"])
            MEMSET(KwA[64:128, :], 0.0, ["KwA"], eng="pool")
            MEMSET(KcA[64:128, :], 0.0, ["KcA"])
            MEMSET(QA[0][64:128, :, :], 0.0, ["QA0"])
            MEMSET(QA[1][64:128, :, :], 0.0, ["QA1"], eng="pool")
            DMA("sp", KsA[64:70, :], kaug_d, (), ["KsA"])
            DMA("sp", KwA[64:70, :], kaug_d, (), ["KwA"])
            DMA("sp", KcA[64:70, :], kaugc_d, (), ["KcA"])
            DMA("sp", CA1[:, :, 64:193], A1_d, (), ["CA1"])
            MEMSET(Vs1[:, :, 64:65], 1.0, ["Vs1"])
            MEMSET(Vw1[:, :, 64:65], 1.0, ["Vw1"])

            cnt = {"s": 0, "p": 0, "qa": 0, "m": 0}
            negb = sb("negb", [128, 1])
            MEMSET(negb[:], -30000.0, ["negb"])

            def next_s():
                si = cnt["s"] % 4
                cnt["s"] += 1
                return psS[si], f"psS{si}"

            def run_pipeline(steps):
                def stageA(st):
                    if st.get("before") is not None:
                        st["before"]()
                    ps, pk = next_s()
                    c0, c1 = st["cols"]
                    n = st["n"]
                    addmask = (st.get("mk") is not None) and (st["mk"] % 3 != 2)
                    has_add = (st.get("cm") is not None) or (st.get("tri") is not None) or addmask
                    MM(ps[0:n, c0:c1], st["lhsT"], st["rhsQ"][:, c0:c1], True, not has_add, st["rk"], [pk])
                    if st.get("cm") is not None:
                        MM(ps[0:n, c0:c1], identb[0:n, 0:n], cmask[0:n, st["cm"], c0:c1], False, True, ["identb", "cmask"], [pk])
                    if st.get("tri") is not None:
                        tt, sub = st["tri"]
                        MM(ps[0:n, sub * 128:(sub + 1) * 128], identb[0:n, 0:n], tri[0:n, tt, :], False, True,
                           ["identb", "tri"], [pk])
                    if addmask:
                        MM(ps[0:n, c0:c1], identb[0:n, 0:n], maskS[0:n, st["mk"], c0:c1], False, True,
                           ["identb", f"mS{st['mk']}"], [pk])
                    pi = cnt["p"] % NPB
                    cnt["p"] += 1
                    pb = Pb[pi]
                    ACT(pb[0:n, c0:c1], ps[0:n, c0:c1], AF.Exp, [pk], [f"Pb{pi}"], bias=st["bias"])
                    src, srck = pb, f"Pb{pi}"
                    if st.get("mk") is not None and not addmask:
                        kt = st["mk"]
                        p2 = P2[pi]
                        eng = "dve" if cnt["m"] % 2 == 0 else "pool"
                        cnt["m"] += 1
                        TT(p2[0:n, c0:c1], pb[0:n, c0:c1], maskS[0:n, kt, c0:c1], ALU.mult, [srck, f"mS{kt}"], [f"P2{pi}"], eng=eng)
                        src, srck = p2, f"P2{pi}"
                    st["src"] = (src, srck)

                def stageB(st):
                    src, srck = st["src"]
                    n = st["n"]
                    for sub in st["subs"]:
                        acc, ak = st["acc"](sub)
                        MM(acc, src[0:n, sub * 128:(sub + 1) * 128], st["rhsV"], st["first"](sub), st["last"], [srck] + st["vk"], [ak])
                    if st.get("after") is not None:
                        st["after"]()

                for i in range(len(steps) + DEPTH):
                    if i < len(steps):
                        stageA(steps[i])
                    if i >= DEPTH:
                        stageB(steps[i - DEPTH])

            def bc(ap2, shape):
                return ap2.unsqueeze(2).to_broadcast(shape)

            CUT = 150.0

            def far(h, q_first, k_last):
                return sl[h] * float(q_first - k_last) > CUT

            all_steps = []
            bufsel = {"qa": 0}

            def loads_gb(g, b, qi):
                q0 = b * 512
                qa = QA[qi]
                qk = f"QA{qi}"
                gs = gsig[qi]
                gk = f"gsig{qi}"
                DMA("sp", qa[0:64, :, :], qT_d[g * 256:(g + 1) * 256, q0:q0 + 512].rearrange("(r d) s -> d r s", d=64),
                    (), [qk])
                DMA("sp", qa[64:70, :, :], qaug_d[:, 4 * g:4 * g + 4, :], (), [qk])
                DMA("sp", gs[:], gsg_d[q0:q0 + 512, :].rearrange("(s p) c -> p s c", p=128), (), [gk])

            def loads_group(g):
                DMA("sp", KsA[0:64, :], ksT_d[g * 64:(g + 1) * 64, :], (), ["KsA"])
                DMA("sp", KwA[0:64, :], kwT_d[g * 64:(g + 1) * 64, :], (), ["KwA"])
                DMA("sp", KcA[0:64, :], kcc_d[g * 64:(g + 1) * 64, :], (), ["KcA"])
                DMA("act", Vs1[:, :, 0:64], vsw_d[:, g * 64:(g + 1) * 64].rearrange("(t p) d -> p t d", p=128), (), ["Vs1"])
                DMA("act", Vw1[:, :, 0:64], vsw_d[:, 256 + g * 64:256 + (g + 1) * 64].rearrange("(t p) d -> p t d", p=128),
                    (), ["Vw1"])
                DMA("act", CA1[:, :, 0:64], vcc_d[:, g * 64:(g + 1) * 64].rearrange("(t p) d -> p t d", p=128), (), ["CA1"])

            def build_gb(g, b, qi, nxt):
                q0 = b * 512
                qa = QA[qi]
                qk = f"QA{qi}"
                gs = gsig[qi]
                gk = f"gsig{qi}"
                Yt = Y[qi]
                yk = f"Y{qi}"
                gb_steps = []
                n_ct = min(NCT, (32 * b + 30) // 128 + 1)

                def selection_chain():
                    for sub in range(4):
                        i_ = 4 * b + sub
                        c_lo = 126 - 2 * i_
                        TT(sc1[:], IMP[:, sub, :], mvalid[:, c_lo:c_lo + 128], ALU.mult, [f"IMP{sub // 2}", "mvalid"], ["sc1"])
                        TT(sc1[:], sc1[:], madd[:, c_lo:c_lo + 128], ALU.add, ["sc1", "madd"], ["sc1"])
                        MEMSET(sc1[:, 0:1], 1e4, ["sc1"])
                        SC.op("dve", lambda e: e.max(out=m8[:], in_=sc1[:]), ["sc1"], ["m8"])
                        SC.op("dve", lambda e: e.match_replace(out=sc2[:], in_to_replace=m8[:], in_values=sc1[:], imm_value=-5.0),
                              ["sc1", "m8"], ["sc2"])
                        SC.op("dve", lambda e: e.max(out=m8[:], in_=sc2[:]), ["sc2"], ["m8"])
                        SC.op("dve", lambda e: e.match_replace(out=sc3[:], in_to_replace=m8[:], in_values=sc2[:], imm_value=-5.0),
                              ["sc2", "m8"], ["sc3"])
                        TT(selq[sub][:], sc3[:], sc1[:], ALU.not_equal, ["sc3", "sc1"], [f"selq{sub}"])

                for r in range(4):
                    h = 4 * g + r
                    accs = [psA[0], psA[1]] if r % 2 == 0 else [psA[2], psA[3]]
                    acck = ["psA0", "psA1"] if r % 2 == 0 else ["psA2", "psA3"]

                    def post_cmp(r=r, h=h, accs=accs, acck=acck):
                        rc = rcs[r]
                        rk_ = f"rcs{r}"
                        for j in range(2):
                            a2 = accs[j][:, 0:386].rearrange("p (s c) -> p s c", c=193)
                            ak = acck[j]
                            TS(rc[:, 0:2], a2[:, :, 192], 1e-30, None, ALU.max, None, [ak], [rk_])
                            RECIP(rc[:, 0:2], rc[:, 0:2], [rk_], [rk_])
                            TT(rc[:, 2:4], rc[:, 0:2], gs[:, 2 * j:2 * j + 2, 3 * h], ALU.mult, [rk_, gk], [rk_])
                            if r == 0:
                                TT(IMP[:, 2 * j:2 * j + 2, :], a2[:, :, 64:192], bc(rc[:, 0:2], [128, 2, 128]), ALU.mult,
                                   [ak, rk_], [f"IMP{j}"])
                            else:
                                TT(impt[:], a2[:, :, 64:192], bc(rc[:, 0:2], [128, 2, 128]), ALU.mult, [ak, rk_], ["impt"])
                                TT(IMP[:, 2 * j:2 * j + 2, :], IMP[:, 2 * j:2 * j + 2, :], impt[:], ALU.add,
                                   ["impt", f"IMP{j}"], [f"IMP{j}"], eng="pool")
                            TT(Yt[:, 2 * j:2 * j + 2, r * 64:(r + 1) * 64], a2[:, :, 0:64], bc(rc[:, 2:4], [128, 2, 64]), ALU.mult,
                               [ak, rk_], [yk + f"_{r}"])
                        if r == 3:
                            selection_chain()
                    cts = [ct for ct in range(n_ct) if not far(h, q0, 16 * (min(NC, ct * 128 + 128) - 1) + 31)]
                    if not cts:
                        cts = [n_ct - 1]
                    for ct in cts:
                        n = min(128, NC - ct * 128)
                        m = b - 4 * ct
                        st = dict(kt=ct, n=n, lhsT=KcA[0:128, ct * 128:ct * 128 + n], rhsQ=qa[0:128, r, :], cols=(0, 512),
                                  rk=["KcA", qk], bias=sl[h] * (16.0 * 128 * ct + 31.0 - q0),
                                  cm=(m if m <= 4 else None), subs=[0, 1, 2, 3], rhsV=CA1[0:n, ct, :], vk=["CA1"],
                                  last=(ct == cts[-1]))
                        st["acc"] = (lambda sub, accs=accs, acck=acck:
                                     (accs[sub // 2][:, (sub % 2) * 193:(sub % 2) * 193 + 193], acck[sub // 2]))
                        st["first"] = (lambda sub, ct=ct, c0_=cts[0]: ct == c0_ and sub % 2 == 0)
                        if ct == cts[-1]:
                            st["after"] = post_cmp
                        gb_steps.append(st)

                def post_sw(r, bi):
                    h = 4 * g + r
                    rc = rcs[r]
                    rk_ = f"rcs{r}"
                    a = psA[r]
                    ak = f"psA{r}"
                    av = a[:, 0:260].rearrange("p (s c) -> p s c", c=65)
                    TS(rc[:, 0:4], av[:, :, 64], 1e-30, None, ALU.max, None, [ak], [rk_])
                    RECIP(rc[:, 0:4], rc[:, 0:4], [rk_], [rk_])
                    TT(rc[:, 4:8], rc[:, 0:4], gs[:, :, 3 * h + bi], ALU.mult, [rk_, gk], [rk_])
                    yt_ = ytmp[r % 2]
                    TT(yt_[:], av[:, :, 0:64], bc(rc[:, 4:8], [128, 4, 64]), ALU.mult, [ak, rk_], [f"ytmp{r % 2}"])
                    TT(Yt[:, :, r * 64:(r + 1) * 64], Yt[:, :, r * 64:(r + 1) * 64], yt_[:], ALU.add,
                       [f"ytmp{r % 2}", yk + f"_{r}"], [yk + f"_{r}"], eng="pool")

                def mask_one(kt):
                    d = kt - 4 * b
                    c0 = 128 * d if d > 0 else 0
                    ps, pk = next_s()
                    MM(ps[:, c0:512], Gm[:, kt * 128:(kt + 1) * 128], selT[:, c0:512], True, True, ["Gm", "selT"], [pk])
                    if kt % 3 != 2:
                        TS(maskS[:, kt, c0:512], ps[:, c0:512], 30000.0, -30000.0, ALU.mult, ALU.add, [pk], [f"mS{kt}"])
                    else:
                        CP(maskS[:, kt, c0:512], ps[:, c0:512], [pk], [f"mS{kt}"])

                MLOOK = 2

                def mask_build(kts):
                    ps, pk = next_s()
                    for sub in range(4):
                        TR(ps[:, sub * 128:(sub + 1) * 128], selq[sub][:], ident[:], [f"selq{sub}", "ident"], [pk])
                    CP(selT[:], ps[:], [pk], ["selT"])
                    for kt in kts[:MLOOK]:
                        mask_one(kt)

                for kind in ("win", "sel"):
                    bi = 1 if kind == "sel" else 2
                    KA, KAk = (KsA, "KsA") if kind == "sel" else (KwA, "KwA")
                    VA, VAk = (Vs1, "Vs1") if kind == "sel" else (Vw1, "Vw1")
                    kts = list(range(0, 4 * b + 4)) if kind == "sel" else [kt for kt in range(4 * b - 4, 4 * b + 4) if kt >= 0]
                    started = set()
                    seen_kt = set()
                    first_of_kind = True
                    used_kts = [kt for kt in kts if any(not far(4 * g + r, q0, 128 * kt + 127) for r in range(4))]
                    for kt in kts:
                        d = kt - 4 * b
                        if kind == "sel":
                            subs = [0, 1, 2, 3] if d < 0 else list(range(d, 4))
                            trim = (0, d) if d >= 0 else None
                        else:
                            subs = list(range(max(d, 0), min(d + 4, 3) + 1))
                            trim = None
                            if d >= 0:
                                trim = (0, d)
                            elif d + 4 <= 3:
                                trim = (1, d + 4)
                        c0, c1 = subs[0] * 128, (subs[-1] + 1) * 128
                        for r in range(4):
                            h = 4 * g + r
                            if far(h, q0, 128 * kt + 127):
                                continue
                            st = dict(kt=kt, n=128, lhsT=KA[0:128, kt * 128:(kt + 1) * 128], rhsQ=qa[0:128, r, :],
                                      cols=(c0, c1), rk=[KAk, qk], bias=sl[h] * (128.0 * kt - q0), subs=subs,
                                      rhsV=VA[:, kt, :], vk=[VAk], last=(kt == kts[-1]), tri=trim,
                                      mk=(kt if kind == "sel" else None))
                            st["acc"] = (lambda sub, r=r: (psA[r][:, sub * 65:sub * 65 + 65], f"psA{r}"))

                            def first(sub, r=r, started=started):
                                key = (r, sub)
                                if key in started:
                                    return False
                                isf = not any(k_[0] == r for k_ in started)
                                started.add(key)
                                return isf
                            st["first"] = first
                            if first_of_kind:
                                first_of_kind = False
                                if kind == "win":
                                    if nxt is not None:
                                        st["before"] = (lambda nxt=nxt: loads_gb(*nxt))
                                else:
                                    def bf0(used_kts=used_kts):
                                        mask_build(used_kts)
                                        if len(used_kts) > MLOOK:
                                            mask_one(used_kts[MLOOK])
                                    st["before"] = bf0
                                    seen_kt.add(kt)
                            elif kind == "sel" and kt not in seen_kt:
                                seen_kt.add(kt)
                                j_ = used_kts.index(kt)
                                if j_ + MLOOK < len(used_kts):
                                    st["before"] = (lambda ktn=used_kts[j_ + MLOOK]: mask_one(ktn))
                            if kt == kts[-1]:
                                if kind == "sel" and r == 3:
                                    def fin(r=r, bi=bi):
                                        post_sw(r, bi)
                                        DMA("sp", ya_d[q0:q0 + 512, g * 256:(g + 1) * 256].rearrange("(s p) c -> p s c", p=128),
                                            Yt[:], [yk + f"_{rr}" for rr in range(4)], [])
                                    st["after"] = fin
                                else:
                                    st["after"] = (lambda r=r, bi=bi: post_sw(r, bi))
                            gb_steps.append(st)
                return gb_steps

            idx = 0
            for g in range(4):
                loads_group(g)
                loads_gb(g, 0, idx % 2)
                g_steps = []
                for b in range(NT):
                    qi = idx % 2
                    nxt = (g, b + 1, 1 - qi) if b + 1 < NT else None
                    g_steps.extend(build_gb(g, b, qi, nxt))
                    idx += 1
                run_pipeline(g_steps)
            SC.emit()

        def load_bf16_w(dst, src, nk, keyp):
            for k in range(nk):
                DMA("pool", dst[:, k, :], src[k * 128:(k + 1) * 128, :], (), [f"{keyp}{k}"], max_dma_last_dim=4096)

        def post_norm_residual(ps_pair, pk_pair, res, resk, gtile, gk, outt, outk, tmp, ssq2, sfx):
            sfx = sfx or ""
            kt_, k0, k1, kr = "pn_tmp" + sfx, "pn_ssq0" + sfx, "pn_ssq1" + sfx, "pn_rstd" + sfx
            for hf in range(2):
                ACT(tmp[:, hf * 512:(hf + 1) * 512], ps_pair[hf][:], AF.Square, [pk_pair[hf]], [kt_, (k0, k1)[hf]],
                    accum=ssq2[:, hf:hf + 1])
            TT(ssq2[:, 2:3], ssq2[:, 0:1], ssq2[:, 1:2], ALU.add, [k0, k1], [kr])
            rms_rstd(ssq2[:, 2:3], ssq2[:, 2:3], [kr], [kr])
            for hf in range(2):
                STT(tmp[:, hf * 512:(hf + 1) * 512], ps_pair[hf][:], ssq2[:, 2:3], gtile[:, hf * 512:(hf + 1) * 512],
                    ALU.mult, ALU.mult, [pk_pair[hf], kr, gk], [kt_])
            TT(outt, tmp[:], res, ALU.add, [kt_, resk], [outk], eng="pool")

        ffn_es = ExitStack()
        Wgu = ffn_es.enter_context(nc.sbuf_tensor("sb_Wgu", [128, 8, 2 * DFF], BF16))
        Wd = ffn_es.enter_context(nc.sbuf_tensor("sb_Wd", [128, 22, D], BF16))
        with ExitStack() as es:
            def sb(name, shape, dt=F32):
                return es.enter_context(nc.sbuf_tensor("sb_" + name, list(shape), dt))
            Wo = sb("Wo", [128, 8, D], BF16)
            gpost = sb("gpost", [128, D])
            yat = [sb(f"yat{i}", [128, 512]) for i in range(2)]
            gat = [sb(f"gat{i}", [128, 512], BF16) for i in range(4)]
            yrt = [sb(f"yrt{i}", [128, 512], BF16) for i in range(4)]
            yTs = [sb(f"yT{i}", [128, 8, 512], BF16) for i in range(2)]
            tmpfs = [sb(f"tmpf{i}", [128, 512]) for i in range(2)]
            xres = [sb(f"xres{i}", [128, D]) for i in range(3)]
            tmps = [sb(f"pn_tmp{i}", [128, D]) for i in range(2)]
            ssq2s = [sb(f"pn_ssq{i}", [128, 3]) for i in range(2)]
            psT4 = [es.enter_context(nc.psum_tensor(f"psT4{i}", [128, 512], F32)) for i in range(4)]
            psU = [es.enter_context(nc.psum_tensor(f"psU{i}", [128, 1024], BF16)) for i in range(4)]
            yab = [sb(f"yab{i}", [128, 512], BF16) for i in range(2)]
            load_bf16_w(Wo, w_out_d, 8, "Wo")
            DMA("sp", gpost[:], gains_d[1], (), ["gpost"])
            load_bf16_w(Wgu, wgu_d, 8, "Wgu")
            load_bf16_w(Wd, wd_d, 22, "Wd")
            xcnt = [0]

            def merge4(b):
                t0 = b * 512
                yT = yTs[b % 2]
                ytk_ = f"yT{b % 2}_"
                for half in range(2):
                    for s in range(4):
                        yt_ = yat[s % 2]
                        ytk = f"yat{s % 2}"
                        r0 = t0 + s * 128
                        DMA("sp", yt_[:, 0:512], ya_d[r0:r0 + 128, half * 512:(half + 1) * 512], (), [ytk])
                        yb_ = yab[s % 2]
                        ybk = f"yab{s % 2}"
                        ACT(yb_[:], yt_[:, 0:512], AF.Identity, [ytk], [ybk])
                        for cc in range(4):
                            TR(psU[cc][:, s * 128:(s + 1) * 128], yb_[:, cc * 128:(cc + 1) * 128], identb[:], [ybk, "identb"],
                               [f"psU{cc}"])
                    for cc in range(4):
                        c = half * 4 + cc
                        ga_ = gat[c % 4]
                        yr_ = yrt[c % 4]
                        tf = tmpfs[c % 2]
                        DMA("sp", ga_[:], ga_d[c * 128:(c + 1) * 128, t0:t0 + 512], (), [f"gat{c % 4}"])
                        DMA("sp", yr_[:], yr_d[c * 128:(c + 1) * 128, t0:t0 + 512], (), [f"yrt{c % 4}"])
                        TT(tf[:], ga_[:], psU[cc][:, 0:512], ALU.mult, [f"gat{c % 4}", f"psU{cc}"], [f"tmpf{c % 2}"])
                        TT(yT[:, c, :], tf[:], yr_[:], ALU.add, [f"tmpf{c % 2}", f"yrt{c % 4}"], [ytk_ + str(c)],
                           eng=("pool" if c % 2 == 0 else "dve"))

            def proj4(b):
                t0 = b * 512
                yT = yTs[b % 2]
                ytk_ = f"yT{b % 2}_"
                for s in range(4):
                    r0 = t0 + s * 128
                    xi = xcnt[0] % 3
                    xcnt[0] += 1
                    xr_ = xres[xi]
                    xrk = f"xres{xi}"
                    DMA("sp", xr_[:], x_d[r0:r0 + 128, :], (), [xrk])
                    pp = [psT4[(s % 2) * 2], psT4[(s % 2) * 2 + 1]]
                    ppk = [f"psT4{(s % 2) * 2}", f"psT4{(s % 2) * 2 + 1}"]
                    for hf in range(2):
                        for k in range(8):
                            MM(pp[hf][:], yT[:, k, s * 128:(s + 1) * 128], Wo[:, k, hf * 512:(hf + 1) * 512], k == 0, k == 7,
                               [ytk_ + str(k), f"Wo{k}"], [ppk[hf]])
                    post_norm_residual(pp, ppk, xr_[:], xrk, gpost, "gpost", xr_[:], xrk, tmps[s % 2], ssq2s[s % 2], f"a{s % 2}")
                    DMA("pool", x1_d[r0:r0 + 128, :], xr_[:], [xrk], [])

            merge4(0)
            for b in range(NT):
                if b + 1 < NT:
                    merge4(b + 1)
                proj4(b)
            SC.emit()

        with ExitStack() as es:
            def sb(name, shape, dt=F32):
                return es.enter_context(nc.sbuf_tensor("sb_" + name, list(shape), dt))
            TW = 256
            gfpre = sb("gfpre", [128, D])
            gfpost = sb("gfpost", [128, D])
            x1t = [sb(f"x1t{i}", [128, D]) for i in range(4)]
            h2 = sb("h2", [128, D], BF16)
            h2Ts = [sb(f"h2T{i}", [128, 8, TW], BF16) for i in range(2)]
            aT = sb("aT", [128, 22, TW], BF16)
            sgt = [sb(f"sgt{i}", [128, TW]) for i in range(2)]
            junk = sb("junk4", [128, D])
            ssq = sb("ssq4", [128, 2])
            tmp = sb("pn_tmp4", [128, D])
            ssq2 = sb("pn_ssq4", [128, 3])
            xo = [sb(f"xo4{i}", [128, D]) for i in range(2)]
            psTt = [es.enter_context(nc.psum_tensor(f"psTt{i}", [128, 1024], BF16)) for i in range(2)]
            psGU = [es.enter_context(nc.psum_tensor(f"psGU{i}", [128, 512], F32)) for i in range(4)]
            psD = [es.enter_context(nc.psum_tensor(f"psD{i}", [128, 512], F32)) for i in range(2)]
            DMA("sp", gfpre[:], gains_d[2], (), ["gfpre"])
            DMA("sp", gfpost[:], gains_d[3], (), ["gfpost"])
            NS = TW // 128
            gi = [0]
            NB4 = S // TW

            def prep4(b):
                t0 = b * TW
                hT_ = h2Ts[b % 2]
                for s in range(NS):
                    r0 = t0 + s * 128
                    xi = (b % 2) * NS + s
                    xt = x1t[xi]
                    xk = f"x1t{xi}"
                    DMA("sp", xt[:], x1_d[r0:r0 + 128, :], (), [xk])
                    ACT(junk[:], xt[:], AF.Square, [xk], ["junk4", "ssq4"], accum=ssq[:, 0:1])
                    rms_rstd(ssq[:, 0:1], ssq[:, 0:1], ["ssq4"], ["ssq4"])
                    STT(h2[:], xt[:], ssq[:, 0:1], gfpre[:], ALU.mult, ALU.mult, [xk, "ssq4", "gfpre"], ["h2"])
                    for c in range(8):
                        TR(psTt[c // 4][:, (c % 4) * 128:(c % 4 + 1) * 128], h2[:, c * 128:(c + 1) * 128], identb[:], ["h2", "identb"],
                           [f"psTt{c // 4}"])
                    for hh in range(2):
                        src = psTt[hh][:, 0:512].rearrange("p (c t) -> p c t", c=4)
                        if hh == 0:
                            CP(hT_[:, hh * 4:(hh + 1) * 4, s * 128:(s + 1) * 128], src, [f"psTt{hh}"], [f"h2T{b % 2}_{s}"])
                        else:
                            ACT(hT_[:, hh * 4:(hh + 1) * 4, s * 128:(s + 1) * 128], src, AF.Identity, [f"psTt{hh}"],
                                [f"h2T{b % 2}_{s}"])

            def gu4(b):
                hT_ = h2Ts[b % 2]
                hk = [f"h2T{b % 2}_{s}" for s in range(NS)]
                for f in range(22):
                    pg = psGU[gi[0] % 4]
                    pgk = f"psGU{gi[0] % 4}"
                    gi[0] += 1
                    for k in range(8):
                        MM(pg[:, 0:TW], Wgu[:, k, f * 128:(f + 1) * 128], hT_[:, k, :], k == 0, k == 7, [f"Wgu{k}"] + hk, [pgk])
                    for k in range(8):
                        MM(pg[:, TW:2 * TW], Wgu[:, k, DFF + f * 128:DFF + (f + 1) * 128], hT_[:, k, :], k == 0, k == 7,
                           [f"Wgu{k}"] + hk, [pgk])
                    sg = sgt[f % 2]
                    ACT(sg[:], pg[:, 0:TW], AF.Silu, [pgk], [f"sgt{f % 2}"])
                    TT(aT[:, f, :], sg[:], pg[:, TW:2 * TW], ALU.mult, [f"sgt{f % 2}", pgk], [f"aT{f}"])

            def down4(b):
                t0 = b * TW
                for s in range(NS):
                    r0 = t0 + s * 128
                    xi = (b % 2) * NS + s
                    xt = x1t[xi]
                    xk = f"x1t{xi}"
                    for hf in range(2):
                        for f in range(22):
                            MM(psD[hf][:], aT[:, f, s * 128:(s + 1) * 128], Wd[:, f, hf * 512:(hf + 1) * 512], f == 0, f == 21,
                               [f"aT{f}", f"Wd{f}"], [f"psD{hf}"])
                    xo_ = xo[s % 2]
                    xok = f"xo4{s % 2}"
                    post_norm_residual(psD, ["psD0", "psD1"], xt[:], xk, gfpost, "gfpost", xo_[:], xok, tmp, ssq2, None)
                    DMA("pool", x2_d[r0:r0 + 128, :], xo_[:], [xok], [])

            prep4(0)
            for b in range(NB4):
                gu4(b)
                if b + 1 < NB4:
                    prep4(b + 1)
                down4(b)
            SC.emit()

        ffn_es.close()

        with ExitStack() as es:
            def sb(name, shape, dt=F32):
                return es.enter_context(nc.sbuf_tensor("sb_" + name, list(shape), dt))
            Wpg = sb("Wpg", [128, 8, D], BF16)
            Wpp = sb("Wpp", [128, 2, D], BF16)
            bple = sb("bple", [128, D])
            x2t = [sb(f"x2t{i}", [128, D]) for i in range(3)]
            pt_ = [sb(f"ptl{i}", [128, 256]) for i in range(3)]
            x2Ts = [sb(f"x2T{i}", [128, 8, 128], BF16) for i in range(2)]
            pTs = [sb(f"pT{i}", [128, 2, 128], BF16) for i in range(2)]
            tgs = [sb(f"tg{i}", [128, D]) for i in range(2)]
            og = [sb(f"og{i}", [128, D]) for i in range(2)]
            psTt = [es.enter_context(nc.psum_tensor(f"psTc{i}", [128, 512], F32)) for i in range(3)]
            psGp = [es.enter_context(nc.psum_tensor(f"psGp{i}", [128, 512], F32)) for i in range(2)]
            psPp = [es.enter_context(nc.psum_tensor(f"psPp{i}", [128, 512], F32)) for i in range(2)]
            load_bf16_w(Wpg, wpg_d, 8, "Wpg")
            load_bf16_w(Wpp, wpp_d, 2, "Wpp")
            DMA("sp", bple[:], gains_d[4], (), ["bple"])
            def prep_c(i_):
                r0 = i_ * 128
                xt = x2t[i_ % 3]
                xk = f"x2t{i_ % 3}"
                pl = pt_[i_ % 3]
                plk = f"ptl{i_ % 3}"
                x2T, x2Tk = x2Ts[i_ % 2], f"x2T{i_ % 2}"
                pT, pTk = pTs[i_ % 2], f"pT{i_ % 2}"
                DMA("sp", xt[:], x2_d[r0:r0 + 128, :], (), [xk])
                DMA("sp", pl[:], p_d[r0:r0 + 128, :], (), [plk])
                for c in range(8):
                    TR(psTt[c // 4][:, (c % 4) * 128:(c % 4 + 1) * 128], xt[:, c * 128:(c + 1) * 128], ident[:], [xk, "ident"],
                       [f"psTc{c // 4}"])
                for c in range(2):
                    TR(psTt[2][:, c * 128:(c + 1) * 128], pl[:, c * 128:(c + 1) * 128], ident[:], [plk, "ident"], ["psTc2"])
                for hh in range(2):
                    CP(x2T[:, hh * 4:(hh + 1) * 4, :], psTt[hh][:].rearrange("p (c t) -> p c t", c=4), [f"psTc{hh}"], [x2Tk])
                ACT(pT[:, :, :], psTt[2][:, 0:256].rearrange("p (c t) -> p c t", c=2), AF.Identity, ["psTc2"], [pTk])

            def main_c(i_):
                r0 = i_ * 128
                xt = x2t[i_ % 3]
                xk = f"x2t{i_ % 3}"
                x2T, x2Tk = x2Ts[i_ % 2], f"x2T{i_ % 2}"
                pT, pTk = pTs[i_ % 2], f"pT{i_ % 2}"
                tg, tgk = tgs[i_ % 2], f"tg{i_ % 2}_"
                o_ = og[i_ % 2]
                ok_ = f"og{i_ % 2}"
                for hf in range(2):
                    for k in range(8):
                        MM(psGp[hf][:], x2T[:, k, :], Wpg[:, k, hf * 512:(hf + 1) * 512], k == 0, k == 7, [x2Tk, f"Wpg{k}"],
                           [f"psGp{hf}"])
                    for k in range(2):
                        MM(psPp[hf][:], pT[:, k, :], Wpp[:, k, hf * 512:(hf + 1) * 512], k == 0, k == 1, [pTk, f"Wpp{k}"],
                           [f"psPp{hf}"])
                    cs = slice(hf * 512, (hf + 1) * 512)
                    TT(tg[:, cs], psGp[hf][:], bple[:, cs], ALU.add, [f"psGp{hf}", "bple"], [tgk + str(hf)])
                    ACT(tg[:, cs], tg[:, cs], AF.Sigmoid, [tgk + str(hf)], [tgk + str(hf)])
                    TT(tg[:, cs], tg[:, cs], psPp[hf][:], ALU.mult, [tgk + str(hf), f"psPp{hf}"], [tgk + str(hf)])
                    TT(o_[:, cs], tg[:, cs], xt[:, cs], ALU.add, [tgk + str(hf), xk], [ok_ + str(hf)], eng="pool")
                DMA("pool", out_d[r0:r0 + 128, :], o_[:], [ok_ + "0", ok_ + "1"], [])

            NI = S // 128
            prep_c(0)
            for i_ in range(NI):
                if i_ + 1 < NI:
                    prep_c(i_ + 1)
                main_c(i_)
            SC.emit()
    nc._n_ops = SC.total
    return nc


def prep_shared(inp, S):
    f = lambda a: np.ascontiguousarray(np.asarray(a, dtype=np.float32))
    m = {}
    m["w_in"] = f(inp["w_in"][0])
    m["w_out"] = f(inp["w_out"][0])
    m["wgu"] = f(inp["ffn_w_gate_up"][0])
    m["wd"] = f(inp["ffn_w_down"][0])
    m["wpp"] = f(inp["ple_w_proj"][0])
    m["wpg"] = f(inp["ple_w_gate"][0])
    gains = np.stack([np.broadcast_to(np.asarray(inp[k][0], np.float32)[None, :], (128, D)) for k in
                      ("norm_mix_pre", "norm_mix_post", "norm_ffn_pre", "norm_ffn_post", "ple_b_gate")])
    m["gains"] = f(gains)
    cols = []
    cw = np.asarray(inp["conv_w"][0], np.float32)
    for k in range(4):
        cols.append(cw[k].reshape(8, 128).T)
    for key in ("conv_b", "lru_ba", "lru_bx", "lru_lambda"):
        cols.append(np.asarray(inp[key][0], np.float32).reshape(8, 128).T)
    m["pv"] = f(np.concatenate(cols, axis=1))
    for nm, key in (("bda", "lru_wa"), ("bdx", "lru_wx")):
        wsrc = np.asarray(inp[key][0], np.float32)
        bd = np.zeros((128, 8, 128), np.float32)
        for c in range(8):
            bd[0:64, c, 0:64] = wsrc[2 * c]
            bd[64:128, c, 64:128] = wsrc[2 * c + 1]
        m[nm] = bd
    m["ckw1"] = f(inp["cmp_k_w1"][0])
    m["cvw1"] = f(inp["cmp_v_w1"][0])
    m["ckw2"] = f(inp["cmp_k_w2"][0])
    m["cvw2"] = f(inp["cmp_v_w2"][0])
    m["posTk"] = f(np.asarray(inp["cmp_pos_k"][0], np.float32).reshape(16, 128).T)
    m["posTv"] = f(np.asarray(inp["cmp_pos_v"][0], np.float32).reshape(16, 128).T)
    m.update(make_consts(S))
    return m


_NC_CACHE = {}


def kernel(**inputs):
    S = 8192
    x = np.asarray(inputs["x"], np.float32)
    p = np.asarray(inputs["p"], np.float32)
    B = x.shape[0]
    if S not in _NC_CACHE:
        _NC_CACHE[S] = build(S)
    nc = _NC_CACHE[S]
    shared = prep_shared(inputs, S)
    in_maps = []
    for b in range(B):
        m = dict(shared)
        m["x"] = np.ascontiguousarray(x[b])
        m["p"] = np.ascontiguousarray(p[0, b])
        in_maps.append(m)
    res = run_bass_kernel_spmd(nc, in_maps, core_ids=list(range(B)))
    return np.stack([np.asarray(r["out"], np.float32) for r in res.results], axis=0)
```

```python
import numpy as np
import ml_dtypes
from contextlib import ExitStack
import concourse.bass as bass
import concourse.mybir as mybir
from concourse.bass_utils import run_bass_kernel_spmd

F32 = mybir.dt.float32
BF16 = mybir.dt.bfloat16
AF = mybir.ActivationFunctionType
ALU = mybir.AluOpType

D = 1024
DIN = 6704
DFF = 2816
EPS = 1e-6
ENGS = ("pe", "act", "dve", "pool", "sp")
C_XR, C_GR, C_Q, C_KC, C_VC, C_KS, C_VS, C_KW, C_VW, C_GN, C_GMR, C_GMA = (
    0, 1024, 2048, 3072, 3328, 3584, 3840, 4096, 4352, 4608, 4656, 5680)


class Sched:
    def __init__(self, nc, n_dma_sems=8):
        self.nc = nc
        self.nd = n_dma_sems
        self.semkey = {}
        for e in ENGS:
            self.semkey[("E", e)] = nc.alloc_semaphore(name=f"S_{e}")
        self.dmaq = ("sp", "act", "pool")
        for e in self.dmaq:
            for i in range(n_dma_sems):
                self.semkey[("D", e, i)] = nc.alloc_semaphore(name=f"D_{e}{i}")
        self.eng_cnt = {e: 0 for e in ENGS}
        self.dma_cnt = {e: 0 for e in self.dmaq}
        self.known = {e: {} for e in ENGS}
        self.dma_sem_last = {}
        self.ops = []
        self.total = 0

    def op(self, eng, fn, r=(), w=(), dma=False):
        self.ops.append((eng, fn, tuple(r), tuple(w), dma))

    def full_clock(self):
        c = {}
        for e in ENGS:
            if self.eng_cnt[e] > 0:
                c[("E", e)] = self.eng_cnt[e]
        for e in self.dmaq:
            for i in range(self.nd):
                n = (self.dma_cnt[e] - i + self.nd - 1) // self.nd
                if n > 0:
                    c[("D", e, i)] = 16 * n
        return c

    def emit(self):
        nc = self.nc
        ops = self.ops
        nops = len(ops)
        self.total += nops
        last_w = {}
        readers = {}
        deps = [None] * nops
        for k, (eng, fn, rd, wr, dma) in enumerate(ops):
            d = set()
            for r in rd:
                if r in last_w:
                    d.add(last_w[r])
            for w in wr:
                if w in last_w:
                    d.add(last_w[w])
                for x in readers.get(w, ()):
                    d.add(x)
            d.discard(k)
            deps[k] = d
            for w in wr:
                last_w[w] = k
                readers[w] = []
            for r in rd:
                if r not in wr:
                    readers.setdefault(r, []).append(k)
        sig = [None] * nops
        clock = [None] * nops
        per_eng = {e: [] for e in ENGS}
        dma_sem_last = {}
        for k, (eng, fn, rd, wr, dma) in enumerate(ops):
            waits = {}
            kn = self.known[eng]
            dl = set(deps[k])
            if dma:
                i = self.dma_cnt[eng] % self.nd
                sk = ("D", eng, i)
                self.dma_cnt[eng] += 1
                if sk in dma_sem_last:
                    dl.add(dma_sem_last[sk])
                dma_sem_last[sk] = k
            for d in dl:
                if (not ops[d][4]) and ops[d][0] == eng and eng == "pe":
                    continue
                s, v = sig[d]
                if kn.get(s, 0) >= v:
                    continue
                if waits.get(s, 0) < v:
                    waits[s] = v
            for d in dl:
                if (not ops[d][4]) and ops[d][0] == eng and eng == "pe":
                    continue
                for cs, cv in clock[d].items():
                    if kn.get(cs, 0) < cv:
                        kn[cs] = cv
            if dma:
                val = 16 * ((self.dma_cnt[eng] - 1) // self.nd) + 16
                sig[k] = (sk, val)
                c = dict(kn)
                c[sk] = val
                clock[k] = c
                per_eng[eng].append((waits, fn, sk, 16))
            else:
                self.eng_cnt[eng] += 1
                sk = ("E", eng)
                sig[k] = (sk, self.eng_cnt[eng])
                c = dict(kn)
                c[sk] = self.eng_cnt[eng]
                clock[k] = c
                per_eng[eng].append((waits, fn, sk, 1))
        final = self.full_clock()
        engobj = {"pe": nc.tensor, "act": nc.scalar, "dve": nc.vector, "pool": nc.gpsimd, "sp": nc.sync}
        semkey = self.semkey

        def run(ename):
            E = engobj[ename]
            for waits, fn, sk, inc in per_eng[ename]:
                for s, v in waits.items():
                    E.wait_ge(semkey[s], v)
                fn(E).then_inc(semkey[sk], inc)
            kn = self.known[ename]
            for s, v in final.items():
                if s == ("E", ename):
                    continue
                if kn.get(s, 0) < v:
                    E.wait_ge(semkey[s], v)

        with nc.Block() as block:
            @block.tensor
            def _(e):
                run("pe")

            @block.scalar
            def _(e):
                run("act")

            @block.vector
            def _(e):
                run("dve")

            @block.gpsimd
            def _(e):
                run("pool")

            @block.sync
            def _(e):
                run("sp")
        for e in ENGS:
            self.known[e] = dict(final)
        self.ops = []


def _split3(v):
    v = np.asarray(v, np.float32)
    hi = v.astype(ml_dtypes.bfloat16)
    r1 = (v - hi.astype(np.float32)).astype(np.float32)
    mid = r1.astype(ml_dtypes.bfloat16)
    r2 = (r1 - mid.astype(np.float32)).astype(np.float32)
    lo = r2.astype(ml_dtypes.bfloat16)
    return hi, mid, lo


def _slopes():
    h = np.arange(1, 17, dtype=np.float32)
    return np.exp2(-8.0 * h / 16.0).astype(np.float32)


def make_consts(S):
    bf = ml_dtypes.bfloat16
    sl = _slopes()
    c = {}
    c["ident"] = np.eye(128, dtype=np.float32)
    qaug = np.zeros((6, 16, 512), dtype=bf)
    ql = np.arange(512, dtype=np.float32)
    for h in range(16):
        a, b, cc = _split3(np.full((512,), sl[h], np.float32))
        qaug[0, h], qaug[1, h], qaug[2, h] = a, b, cc
        a, b, cc = _split3(-sl[h] * ql)
        qaug[3, h], qaug[4, h], qaug[5, h] = a, b, cc
    c["qaug"] = qaug
    kl = np.arange(S, dtype=np.float32) % 128
    kaug = np.ones((6, S), np.float32)
    kaug[0:3] = kl
    c["kaug"] = kaug.astype(bf)
    kaugc = np.ones((6, 512), np.float32)
    kaugc[0:3] = 16.0 * (np.arange(512, dtype=np.float32) % 128)
    c["kaugc"] = kaugc.astype(bf)
    kk = np.arange(128)[:, None]
    qq = np.arange(128)[None, :]
    tri = (np.stack([(qq >= kk), (qq < kk)]).astype(np.float32) - 1.0) * 30000.0
    c["tri"] = np.ascontiguousarray(tri.transpose(1, 0, 2)).astype(bf)
    cl = np.arange(128)[:, None]
    qL = np.arange(512)[None, :]
    cm = (np.stack([(qL - 16 * cl - 31 + 512 * m >= 0) for m in range(5)]).astype(np.float32) - 1.0) * 30000.0
    c["cmask"] = np.ascontiguousarray(cm.transpose(1, 0, 2)).astype(bf)
    NC = S // 16 - 1
    NCT = (NC + 128) // 128
    A1 = np.zeros((NCT * 128, 129), np.float32)
    for cc_ in range(NC):
        j, rem = cc_ // 4, cc_ % 4
        if rem < 3:
            if j < 128:
                A1[cc_, j] = 2.0
        else:
            if j < 128:
                A1[cc_, j] = 1.0
            if j + 1 < 128:
                A1[cc_, j + 1] = 1.0
        A1[cc_, 128] = 1.0
    c["A1"] = np.ascontiguousarray(A1.reshape(NCT, 128, 129).transpose(1, 0, 2)).astype(bf)
    G = np.zeros((128, S), np.float32)
    y = np.arange(S)
    G[y // 64, y] = 1.0
    c["G"] = G.astype(bf)
    mv = np.zeros((128, 256), np.float32)
    ma = np.zeros((128, 256), np.float32)
    for p in range(128):
        cur = 0 if p < 64 else 1
        for col in range(256):
            jr = col - 126
            valid = jr <= cur
            forced = (jr == cur) or (jr == cur - 1)
            if not valid:
                ma[p, col] = -1.0
            elif forced:
                ma[p, col] = 1e4
            else:
                mv[p, col] = 1.0
    c["mvalid"] = mv
    c["madd"] = ma
    return c


CONST_SPECS = None


def build(S, debug=False):
    assert S % 2048 == 0
    NT = S // 512
    NKT = S // 128
    NC = S // 16 - 1
    NCT = (NC + 128) // 128
    sl = [float(v) for v in _slopes()]
    nc = bass.Bass("TRN2", target_bir_lowering=False)
    skind = "ExternalOutput" if debug else "Internal"

    def din(name, shape, dt=F32):
        return nc.dram_tensor(name, list(shape), dt, kind="ExternalInput").ap()

    def dscr(name, shape, dt=F32):
        return nc.dram_tensor(name, list(shape), dt, kind=skind).ap()

    x_d = din("x", [S, D])
    p_d = din("p", [S, 256])
    w_in_d = din("w_in", [D, DIN])
    w_out_d = din("w_out", [D, D])
    wgu_d = din("wgu", [D, 2 * DFF])
    wd_d = din("wd", [DFF, D])
    wpp_d = din("wpp", [256, D])
    wpg_d = din("wpg", [D, D])
    gains_d = din("gains", [5, 128, D])
    pv_d = din("pv", [128, 64])
    bda_d = din("bda", [128, 8, 128])
    bdx_d = din("bdx", [128, 8, 128])
    cw1_d = [din("ckw1", [2048, 256]), din("cvw1", [2048, 256])]
    cw2_d = [din("ckw2", [256, 64]), din("cvw2", [256, 64])]
    posT_d = [din("posTk", [128, 16]), din("posTv", [128, 16])]
    ident_d = din("ident", [128, 128])
    qaug_d = din("qaug", [6, 16, 512], BF16)
    kaug_d = din("kaug", [6, S], BF16)
    kaugc_d = din("kaugc", [6, 512], BF16)
    tri_d = din("tri", [128, 2, 128], BF16)
    cmask_d = din("cmask", [128, 5, 512], BF16)
    A1_d = din("A1", [128, NCT, 129], BF16)
    G_d = din("G", [128, S], BF16)
    mvalid_d = din("mvalid", [128, 256])
    madd_d = din("madd", [128, 256])
    out_d = nc.dram_tensor("out", [S, D], F32, kind="ExternalOutput").ap()

    qT_d = dscr("s_qT", [D, S], BF16)
    kcT_d = dscr("s_kcT", [256, S], BF16)
    vcT_d = dscr("s_vcT", [256, S], BF16)
    ksT_d = dscr("s_ksT", [256, S], BF16)
    kwT_d = dscr("s_kwT", [256, S], BF16)
    vsw_d = dscr("s_vsw", [S, 512], BF16)
    gsg_d = dscr("s_gsg", [S, 48])
    yr_d = dscr("s_yr", [D, S], BF16)
    ga_d = dscr("s_ga", [D, S], BF16)
    kcc_d = dscr("s_kcc", [256, 512], BF16)
    vcc_d = dscr("s_vcc", [NCT * 128, 256], BF16)
    ya_d = dscr("s_ya", [S, D], BF16)
    x1_d = dscr("s_x1", [S, D])
    x2_d = dscr("s_x2", [S, D])

    SC = Sched(nc)

    def MM(out, lhsT, rhs, start, stop, r, w):
        SC.op("pe", lambda e: e.matmul(out, lhsT=lhsT, rhs=rhs, start=start, stop=stop, skip_group_check=True), r, w)

    def TR(out, in_, ident, r, w):
        SC.op("pe", lambda e: e.transpose(out, in_, ident), r, w)

    def ACT(out, in_, func, r, w, bias=0.0, scale=1.0, accum=None):
        SC.op("act", lambda e: e.activation(out=out, in_=in_, func=func, bias=bias, scale=scale, accum_out=accum), r, w)

    def TS(out, in0, s1, s2, op0, op1, r, w, eng="dve", accum=None):
        if op1 is None:
            SC.op(eng, lambda e: e.tensor_scalar(out=out, in0=in0, scalar1=s1, scalar2=None, op0=op0), r, w)
        else:
            SC.op(eng, lambda e: e.tensor_scalar(out=out, in0=in0, scalar1=s1, scalar2=s2, op0=op0, op1=op1), r, w)

    def TT(out, in0, in1, op, r, w, eng="dve"):
        SC.op(eng, lambda e: e.tensor_tensor(out=out, in0=in0, in1=in1, op=op), r, w)

    def STT(out, in0, scalar, in1, op0, op1, r, w):
        SC.op("dve", lambda e: e.scalar_tensor_tensor(out=out, in0=in0, scalar=scalar, in1=in1, op0=op0, op1=op1), r, w)

    def CP(out, in_, r, w, eng="dve"):
        SC.op(eng, lambda e: e.tensor_copy(out=out, in_=in_), r, w)

    def MEMSET(ap, val, w, eng="dve"):
        SC.op(eng, lambda e: e.memset(ap, val), (), w)

    def RECIP(out, in_, r, w):
        SC.op("dve", lambda e: e.reciprocal(out=out, in_=in_), r, w)

    def DMA(q, out, in_, r, w, **kw):
        SC.op(q, lambda e: e.dma_start(out=out, in_=in_, **kw), r, w, dma=True)

    def gelu_tanh(es_tmp, out, src, nparts, ncols, keyp, rkeys, wkeys):
        t1, t2 = es_tmp
        TT(t1, src, src, ALU.mult, rkeys, [keyp + "t1"])
        TS(t1, t1, 0.044715, 1.0, ALU.mult, ALU.add, [keyp + "t1"], [keyp + "t1"])
        TT(t1, t1, src, ALU.mult, [keyp + "t1"] + list(rkeys), [keyp + "t1"])
        ACT(t2, t1, AF.Sigmoid, [keyp + "t1"], [keyp + "t2"], scale=1.5957691216057308)
        TT(out, t2, src, ALU.mult, [keyp + "t2"] + list(rkeys), wkeys)

    def rms_rstd(rstd, ssq, r, w):
        ACT(rstd, ssq, AF.Sqrt, r, w, bias=epsb[:, 0:1], scale=1.0 / D)
        RECIP(rstd, rstd, w, w)

    with ExitStack() as g_es:
        ident = g_es.enter_context(nc.sbuf_tensor("sb_ident", [128, 128], F32))
        identb = g_es.enter_context(nc.sbuf_tensor("sb_identb", [128, 128], BF16))
        epsb = g_es.enter_context(nc.sbuf_tensor("sb_epsb", [128, 1], F32))
        DMA("sp", ident[:], ident_d, (), ["ident"])
        DMA("pool", identb[:], ident_d, (), ["identb"])
        MEMSET(epsb[:], EPS, ["epsb"])
        SC.emit()

        with ExitStack() as es:
            def sb(name, shape, dt=F32):
                return es.enter_context(nc.sbuf_tensor("sb_" + name, list(shape), dt))
            Wi = sb("Wi", [128, 8, DIN], BF16)
            bda = sb("bda", [128, 8, 128], BF16)
            bdx = sb("bdx", [128, 8, 128], BF16)
            pv = sb("pv", [128, 64])
            cneg = sb("cneg", [128, 8])
            cneg2 = sb("cneg2", [128, 8])
            ctmp = sb("ctmp", [128, 8])
            ctmp2 = sb("ctmp2", [128, 8])
            gpre = sb("gpre", [128, D])
            xs = [sb(f"xs{i}", [128, D]) for i in range(2)]
            junk = sb("junk", [128, D])
            hb = [sb("hb0", [128, 4, D], BF16)]
            hT = sb("hT", [128, 8, 512], BF16)
            ssq = sb("ssq", [128, 8])
            xbuf = sb("xbuf", [128, 8, 516])
            hlast = sb("hlast", [128, 8])
            T = {n: sb("t_" + n, [128, 512]) for n in
                 ("xc0", "xc1", "r", "i", "a", "a2", "bb", "hs", "g00", "g01", "t1", "t2", "ge", "sgm0", "sgm1")}
            xcb = [sb(f"xcb{i}", [128, 512], BF16) for i in range(2)]
            ev = [sb(f"ev{i}", [128, 512], BF16) for i in range(3)]
            evf = [sb(f"evf{i}", [128, 512], BF16) for i in range(2)]
            gsb = sb("gsb", [128, 48])
            psT = [es.enter_context(nc.psum_tensor(f"psT{i}", [128, 1024], BF16)) for i in range(2)]
            psM = [es.enter_context(nc.psum_tensor(f"psM{i}", [128, 512], F32)) for i in range(4)]
            psG = [es.enter_context(nc.psum_tensor(f"psG{i}", [128, 512], F32)) for i in range(2)]

            WBLK = [(0, 1024), (1024, 2048), (4656, 5680), (5680, 6704), (2048, 3072), (3072, 4656)]
            for j, (c0_, c1_) in enumerate(WBLK):
                for k in range(8):
                    DMA("pool", Wi[:, k, c0_:c1_], w_in_d[k * 128:(k + 1) * 128, c0_:c1_], (), [f"WiB{j}"],
                        max_dma_last_dim=4096)

            def wkey(col0):
                for j, (c0_, c1_) in enumerate(WBLK):
                    if c0_ <= col0 < c1_:
                        return f"WiB{j}"
                raise ValueError(col0)
            DMA("pool", bda[:], bda_d, (), ["bda"])
            DMA("pool", bdx[:], bdx_d, (), ["bdx"])
            DMA("sp", pv[:], pv_d, (), ["pv"])
            DMA("sp", gpre[:], gains_d[0], (), ["gpre"])
            MEMSET(xbuf[:], 0.0, ["xbuf"])
            MEMSET(hlast[:], 0.0, ["hlast"])
            onepb = sb("onepb", [128, 1])
            MEMSET(onepb[:], 1.0 + 2.0 ** -23, ["onepb"])
            lam = pv[:, 56:64]
            ACT(ctmp[:], lam, AF.Exp, ["pv"], ["ctmp"], scale=-1.0)
            TS(ctmp2[:], ctmp[:], -0.25, 1.0 / 3.0, ALU.mult, ALU.add, ["ctmp"], ["ctmp2"])
            TT(ctmp2[:], ctmp2[:], ctmp[:], ALU.mult, ["ctmp", "ctmp2"], ["ctmp2"])
            TS(ctmp2[:], ctmp2[:], -0.5, None, ALU.add, None, ["ctmp2"], ["ctmp2"])
            TT(ctmp2[:], ctmp2[:], ctmp[:], ALU.mult, ["ctmp", "ctmp2"], ["ctmp2"])
            TS(ctmp2[:], ctmp2[:], 1.0, None, ALU.add, None, ["ctmp2"], ["ctmp2"])
            TT(ctmp2[:], ctmp2[:], ctmp[:], ALU.mult, ["ctmp", "ctmp2"], ["ctmp2"])
            TS(cneg[:], ctmp2[:], -8.0, None, ALU.mult, None, ["ctmp2"], ["cneg"])
            TS(cneg2[:], ctmp2[:], -16.0, None, ALU.mult, None, ["ctmp2"], ["cneg"])

            def norm_tile(b):
                hbuf = hb[0]
                for s in range(4):
                    xt = xs[s % 2]
                    xk = f"xs{s % 2}"
                    r0 = b * 512 + s * 128
                    DMA("sp", xt[:], x_d[r0:r0 + 128, :], (), [xk])
                    col = (b % 2) * 4 + s
                    ACT(junk[:], xt[:], AF.Square, [xk], ["junk", f"ssq{col}"], accum=ssq[:, col:col + 1])
                    rms_rstd(ssq[:, col:col + 1], ssq[:, col:col + 1], [f"ssq{col}"], [f"ssq{col}"])
                    STT(hbuf[:, s, :], xt[:], ssq[:, col:col + 1], gpre[:], ALU.mult, ALU.mult,
                        [xk, f"ssq{col}", "gpre"], [f"hb_{s}"])

            norm_tile(0)
            evi = [0]
            mi = [0]

            def next_ps():
                i = mi[0] % 4
                mi[0] += 1
                return psM[i], f"psM{i}"

            def featmm(col0, ps, pk):
                for k in range(8):
                    MM(ps[:], Wi[:, k, col0:col0 + 128], hT[:, k, :], k == 0, k == 7, [wkey(col0), f"hT{k}"], [pk])

            OMB = 1.0 + 2.0 ** -23

            def conv_chunk(c):
                xk = f"xbuf{c}"
                xc = T[f"xc{c % 2}"]
                xck = f"xc{c % 2}"
                TS(xc[:], xbuf[:, c, 3:515], pv[:, 24 + c:25 + c], pv[:, 32 + c:33 + c], ALU.mult, ALU.add, [xk, "pv"], [xck])
                for kk in range(3):
                    STT(xc[:], xbuf[:, c, kk:kk + 512], pv[:, kk * 8 + c:kk * 8 + c + 1], xc[:], ALU.mult, ALU.add,
                        [xk, "pv", xck], [xck])
                CP(xbuf[:, c, 0:3], xbuf[:, c, 512:515], [xk], [xk], eng="pool")

            def conv_act(c):
                ACT(xcb[c % 2][:], T[f"xc{c % 2}"][:], AF.Identity, [f"xc{c % 2}"], [f"xcb{c % 2}"])

            def other_job(ji, job, t0):
                col0, dst, row0, scl, sig = job
                ps, pk = next_ps()
                featmm(col0, ps, pk)
                if sig:
                    ef = evf[ji % 2]
                    ACT(ef[:], ps[:], AF.Sigmoid, [pk], [f"evf{ji % 2}"])
                    DMA("sp", dst[row0:row0 + 128, t0:t0 + 512], ef[:], [f"evf{ji % 2}"], [])
                else:
                    e_ = ev[evi[0] % 3]
                    ek = f"ev{evi[0] % 3}"
                    evi[0] += 1
                    if ji % 2 == 0:
                        TS(e_[:], ps[:], scl, None, ALU.mult, None, [pk], [ek])
                    else:
                        ACT(e_[:], ps[:], AF.Identity, [pk], [ek], scale=scl)
                    DMA("sp", dst[row0:row0 + 128, t0:t0 + 512], e_[:], [ek], [])

            for b in range(NT):
                t0 = b * 512
                hbuf = hb[0]
                for c in range(8):
                    pt = psT[c % 2]
                    for s in range(4):
                        TR(pt[:, s * 128:(s + 1) * 128], hbuf[:, s, c * 128:(c + 1) * 128], identb[:],
                           [f"hb_{s}", "identb"], [f"psT{c % 2}"])
                    if c % 2 == 0:
                        CP(hT[:, c, :], pt[:, 0:512], [f"psT{c % 2}"], [f"hT{c}"])
                    else:
                        ACT(hT[:, c, :], pt[:, 0:512], AF.Identity, [f"psT{c % 2}"], [f"hT{c}"])
                if b + 1 < NT:
                    norm_tile(b + 1)
                for c in range(8):
                    ps, pk = next_ps()
                    featmm(C_XR + c * 128, ps, pk)
                    ACT(xbuf[:, c, 3:515], ps[:], AF.Identity, [pk], [f"xbuf{c}"])
                jobs = []
                for c in range(8):
                    jobs.append((C_GMA + c * 128, ga_d, c * 128, 1.0, True))
                    jobs.append((C_Q + c * 128, qT_d, c * 128, 0.125, False))
                    base, dst = ((C_KC, kcT_d), (C_VC, vcT_d), (C_KS, ksT_d), (C_KW, kwT_d))[c // 2]
                    jobs.append((base + (c % 2) * 128, dst, (c % 2) * 128, 1.0, False))
                conv_chunk(0)
                conv_act(0)
                for c in range(8):
                    if c + 1 < 8:
                        conv_chunk(c + 1)
                    xc = T[f"xc{c % 2}"]
                    xck = f"xc{c % 2}"
                    xb_, xbk = xcb[c % 2], f"xcb{c % 2}"
                    g0, g0k = T[f"g0{c % 2}"], f"g0{c % 2}"
                    sgm, sgmk = T[f"sgm{c % 2}"], f"sgm{c % 2}"
                    MM(psG[0][:], bda[:, c, :], xb_[:], True, True, ["bda", xbk], ["psG0"])
                    MM(psG[1][:], bdx[:, c, :], xb_[:], True, True, ["bdx", xbk], ["psG1"])
                    psg, pkg = next_ps()
                    featmm(C_GR + c * 128, psg, pkg)
                    psm, pkm = next_ps()
                    featmm(C_GMR + c * 128, psm, pkm)
                    ACT(T["r"][:], psG[0][:], AF.Sigmoid, ["psG0", "pv"], ["r"], bias=pv[:, 40 + c:41 + c])
                    ACT(T["i"][:], psG[1][:], AF.Sigmoid, ["psG1", "pv"], ["i"], bias=pv[:, 48 + c:49 + c])
                    ACT(T["a"][:], T["r"][:], AF.Exp, ["r", "cneg"], ["a"], scale=cneg[:, c:c + 1])
                    TT(T["a2"][:], T["a"][:], T["a"][:], ALU.mult, ["a"], ["a2"], eng="pool")
                    TS(T["a2"][:], T["a2"][:], -1.0, OMB, ALU.mult, ALU.add, ["a2"], ["a2"], eng="pool")
                    if c + 1 < 8:
                        conv_act(c + 1)
                    ACT(g0[:], psg[:], AF.Identity, [pkg], [g0k])
                    ACT(T["a2"][:], T["a2"][:], AF.Ln, ["a2"], ["a2"])
                    ACT(T["a2"][:], T["a2"][:], AF.Exp, ["a2"], ["a2"], scale=0.5)
                    ACT(sgm[:], psm[:], AF.Sigmoid, [pkm], [sgmk])
                    TT(T["bb"][:], T["i"][:], xc[:], ALU.mult, ["i", xck], ["bb"], eng="pool")
                    TT(T["bb"][:], T["bb"][:], T["a2"][:], ALU.mult, ["bb", "a2"], ["bb"])
                    SC.op("dve", lambda e, c=c: e.tensor_tensor_scan(out=T["hs"][:], data0=T["a"][:], data1=T["bb"][:],
                                                                   initial=hlast[:, c:c + 1], op0=ALU.mult, op1=ALU.add),
                          ["a", "bb", "hlast"], ["hs"])
                    CP(hlast[:, c:c + 1], T["hs"][:, 511:512], ["hs"], ["hlast"])
                    t1, t2 = T["t1"], T["t2"]
                    TT(t1[:], g0[:], g0[:], ALU.mult, [g0k], ["t1"], eng="pool")
                    TS(t1[:], t1[:], 0.044715, 1.0, ALU.mult, ALU.add, ["t1"], ["t1"], eng="pool")
                    TT(t1[:], t1[:], g0[:], ALU.mult, ["t1", g0k], ["t1"], eng="pool")
                    for jj in range(3):
                        other_job(3 * c + jj, jobs[3 * c + jj], t0)
                    ACT(t2[:], t1[:], AF.Sigmoid, ["t1"], ["t2"], scale=1.5957691216057308)
                    TT(T["ge"][:], t2[:], g0[:], ALU.mult, ["t2", g0k], ["ge"])
                    TT(T["ge"][:], T["ge"][:], T["hs"][:], ALU.mult, ["ge", "hs"], ["ge"])
                    ef = evf[c % 2]
                    TT(ef[:], T["ge"][:], sgm[:], ALU.mult, ["ge", sgmk], [f"evf{c % 2}"], eng="pool")
                    DMA("sp", yr_d[c * 128:(c + 1) * 128, t0:t0 + 512], ef[:], [f"evf{c % 2}"], [])
                for s in range(4):
                    ps, pk = next_ps()
                    for k in range(8):
                        MM(ps[:, 0:256], hT[:, k, s * 128:(s + 1) * 128], Wi[:, k, C_VS:C_VS + 256], k == 0, k == 7,
                           [wkey(C_VS), f"hT{k}"], [pk])
                    for k in range(8):
                        MM(ps[:, 256:512], hT[:, k, s * 128:(s + 1) * 128], Wi[:, k, C_VW:C_VW + 256], k == 0, k == 7,
                           [wkey(C_VS), f"hT{k}"], [pk])
                    e_ = ev[evi[0] % 3]
                    ek = f"ev{evi[0] % 3}"
                    evi[0] += 1
                    CP(e_[:], ps[:], [pk], [ek])
                    DMA("sp", vsw_d[t0 + s * 128:t0 + (s + 1) * 128, :], e_[:], [ek], [])
                    ps, pk = next_ps()
                    for k in range(8):
                        MM(ps[:, 0:48], hT[:, k, s * 128:(s + 1) * 128], Wi[:, k, C_GN:C_GN + 48], k == 0, k == 7,
                           [wkey(C_VS), f"hT{k}"], [pk])
                    ACT(gsb[:], ps[:, 0:48], AF.Sigmoid, [pk], ["gsb"])
                    DMA("sp", gsg_d[t0 + s * 128:t0 + (s + 1) * 128, :], gsb[:], ["gsb"], [])
            SC.emit()

        with ExitStack() as es:
            def sb(name, shape, dt=F32):
                return es.enter_context(nc.sbuf_tensor("sb_" + name, list(shape), dt))
            w1 = [sb(f"cw1_{i}", [128, 16, 256], BF16) for i in range(2)]
            w2 = [sb(f"cw2_{i}", [128, 2, 64], BF16) for i in range(2)]
            posT = [sb(f"posT{i}", [128, 16], BF16) for i in range(2)]
            cb = [sb(f"cbias{i}", [128, 2]) for i in range(2)]
            kin = [sb(f"kin{i}", [128, S], BF16) for i in range(3)]
            kinr = [sb(f"kinr{i}", [128, 8, S // 16], BF16) for i in range(2)]
            u = sb("c_u", [128, 512])
            t1 = sb("c_t1", [128, 512])
            t2 = sb("c_t2", [128, 512])
            Hh = [sb(f"c_H{i}", [128, 512], BF16) for i in range(2)]
            okc = sb("c_okc", [64, 512], BF16)
            ovc = sb("c_ovc", [128, 64], BF16)
            psH = [es.enter_context(nc.psum_tensor(f"psH{i}", [128, 512], F32)) for i in range(2)]
            psO = [es.enter_context(nc.psum_tensor(f"psO{i}", [128, 512], F32)) for i in range(2)]
            psB = es.enter_context(nc.psum_tensor("psB", [128, 512], F32))
            for kv in range(2):
                DMA("pool", w1[kv][:], cw1_d[kv].rearrange("(lc p) h -> p lc h", p=128), (), [f"w1{kv}"])
                DMA("pool", w2[kv][:], cw2_d[kv].rearrange("(c p) d -> p c d", p=128), (), [f"w2{kv}"])
                DMA("pool", posT[kv][:], posT_d[kv], (), [f"posT{kv}"])
            MEMSET(okc[:], 0.0, ["okc"])
            MEMSET(ovc[:], 0.0, ["ovc"])
            for i3 in range(3):
                MEMSET(kin[i3][64:128, S - 16:S], 0.0, [f"kin{i3}"])
            ki = 0
            for kv in range(2):
                src_d = kcT_d if kv == 0 else vcT_d
                for hc in range(2):
                    for lc in range(16):
                        MM(psB[:, hc:hc + 1], w1[kv][:, lc, hc * 128:(hc + 1) * 128], posT[kv][:, lc:lc + 1], lc == 0, lc == 15,
                           [f"w1{kv}", f"posT{kv}"], ["psB"])
                CP(cb[kv][:], psB[:, 0:2], ["psB"], [f"cb{kv}"])
                for g in range(4):
                    kt_ = kin[ki % 3]
                    kk_ = f"kin{ki % 3}"
                    kr_ = kinr[ki % 2]
                    krk = f"kinr{ki % 2}"
                    ki += 1
                    DMA("sp", kt_[0:64, :], src_d[g * 64:(g + 1) * 64, :], (), [kk_])
                    DMA("sp", kt_[64:128, 0:S - 1], src_d[g * 64:(g + 1) * 64, 1:S], (), [kk_])
                    kv4 = kt_[:].rearrange("p (i j t) -> p j t i", j=8, t=2)
                    CP(kr_[:, 0:4, :], kv4[:, 0:4, 0, :], [kk_], [krk + "a"])
                    CP(kr_[:, 4:8, :], kv4[:, 4:8, 0, :], [kk_], [krk + "b"], eng="pool")
                    for hc in range(2):
                        ph = psH[hc]
                        for lc in range(16):
                            j = lc % 8
                            rhs_ = kr_[:, j, 0:NC] if lc < 8 else kr_[:, j, 1:NC + 1]
                            MM(ph[:, 0:NC], w1[kv][:, lc, hc * 128:(hc + 1) * 128], rhs_,
                               lc == 0, lc == 15, [f"w1{kv}", krk + ("a" if j < 4 else "b")], [f"psH{hc}"])
                        ACT(u[:, 0:NC], ph[:, 0:NC], AF.Identity, [f"psH{hc}", f"cb{kv}"], ["c_u"], bias=cb[kv][:, hc:hc + 1])
                        gelu_tanh((t1[:, 0:NC], t2[:, 0:NC]), Hh[hc][:, 0:NC], u[:, 0:NC], 128, NC, "p2", ["c_u"], [f"H{hc}"])
                    if kv == 0:
                        po = psO[0]
                        for hc in range(2):
                            MM(po[0:64, 0:NC], w2[kv][:, hc, :], Hh[hc][:, 0:NC], hc == 0, hc == 1,
                               [f"w2{kv}", f"H{hc}"], ["psO0"])
                        CP(okc[:, 0:NC], po[0:64, 0:NC], ["psO0"], ["okc"])
                        DMA("sp", kcc_d[g * 64:(g + 1) * 64, :], okc[:], ["okc"], [])
                    else:
                        for ct in range(NCT):
                            n = min(128, NC - ct * 128)
                            po = psO[ct % 2]
                            for hc in range(2):
                                MM(po[0:n, 0:64], Hh[hc][:, ct * 128:ct * 128 + n], w2[kv][:, hc, :], hc == 0, hc == 1,
                                   [f"w2{kv}", f"H{hc}"], [f"psO{ct % 2}"])
                            CP(ovc[0:n, :], po[0:n, 0:64], [f"psO{ct % 2}"], ["ovc"])
                            DMA("sp", vcc_d[ct * 128:ct * 128 + 128, g * 64:(g + 1) * 64], ovc[:, :], ["ovc"], [])
            SC.emit()

        with ExitStack() as es:
            def sb(name, shape, dt=F32):
                return es.enter_context(nc.sbuf_tensor("sb_" + name, list(shape), dt))
            DEPTH = 3
            NPB = 6
            KsA = sb("KsA", [128, S], BF16)
            KwA = sb("KwA", [128, S], BF16)
            KcA = sb("KcA", [128, 512], BF16)
            Vs1 = sb("Vs1", [128, NKT, 65], BF16)
            Vw1 = sb("Vw1", [128, NKT, 65], BF16)
            CA1 = sb("CA1", [128, NCT, 193], BF16)
            Gm = sb("Gm", [128, S], BF16)
            tri = sb("tri", [128, 2, 128], BF16)
            cmask = sb("cmask", [128, 5, 512], BF16)
            mvalid = sb("mvalid", [128, 256])
            madd = sb("madd", [128, 256])
            maskS = sb("maskS", [128, NKT, 512], BF16)
            QA = [sb(f"QA{i}", [128, 4, 512], BF16) for i in range(2)]
            gsig = [sb(f"gsig{i}", [128, 4, 48]) for i in range(2)]
            Pb = [sb(f"Pb{i}", [128, 512], BF16) for i in range(NPB)]
            P2 = [sb(f"P2{i}", [128, 512], BF16) for i in range(NPB)]
            IMP = sb("IMP", [128, 4, 128])
            impt = sb("impt", [128, 2, 128])
            sc1 = sb("sc1", [128, 128])
            sc2 = sb("sc2", [128, 128])
            sc3 = sb("sc3", [128, 128])
            m8 = sb("m8", [128, 8])
            selq = [sb(f"selq{i}", [128, 128]) for i in range(4)]
            selT = sb("selT", [128, 512], BF16)
            Y = [sb(f"Y{i}", [128, 4, 256]) for i in range(2)]
            Yb = [sb(f"Yb{i}", [128, 4, 256], BF16) for i in range(2)]
            ytmp = [sb(f"ytmp{i}", [128, 4, 64]) for i in range(2)]
            rcs = [sb(f"rcs{i}", [128, 8]) for i in range(4)]
            psS = [es.enter_context(nc.psum_tensor(f"psS{i}", [128, 512], F32)) for i in range(4)]
            psA = [es.enter_context(nc.psum_tensor(f"psA{i}", [128, 512], F32)) for i in range(4)]

            DMA("sp", Gm[:], G_d, (), ["Gm"])
            DMA("sp", tri[:], tri_d, (), ["tri"])
            DMA("sp", cmask[:], cmask_d, (), ["cmask"])
            DMA("sp", mvalid[:], mvalid_d, (), ["mvalid"])
            DMA("sp", madd[:], madd_d, (), ["madd"])
            MEMSET(KsA[64:128, :], 0.0, ["KsA"])
            MEMSET(KwA[64:128, :], 0.0, ["KwA"], eng="pool")
            MEMSET(KcA[64:128, :], 0.0, ["KcA"])
            MEMSET(QA[0][64:128, :, :], 0.0, ["QA0"])
            MEMSET(QA[1][64:128, :, :], 0.0, ["QA1"], eng="pool")
            DMA("sp", KsA[64:70, :], kaug_d, (), ["KsA"])
            DMA("sp", KwA[64:70, :], kaug_d, (), ["KwA"])
            DMA("sp", KcA[64:70, :], kaugc_d, (), ["KcA"])
            DMA("sp", CA1[:, :, 64:193], A1_d, (), ["CA1"])
            MEMSET(Vs1[:, :, 64:65], 1.0, ["Vs1"])
            MEMSET(Vw1[:, :, 64:65], 1.0, ["Vw1"])

            cnt = {"s": 0, "p": 0, "qa": 0, "m": 0}
            negb = sb("negb", [128, 1])
            MEMSET(negb[:], -30000.0, ["negb"])

            def next_s():
                si = cnt["s"] % 4
                cnt["s"] += 1
                return psS[si], f"psS{si}"

            def run_pipeline(steps):
                def stageA(st):
                    if st.get("before") is not None:
                        st["before"]()
                    ps, pk = next_s()
                    c0, c1 = st["cols"]
                    n = st["n"]
                    addmask = (st.get("mk") is not None) and (st["mk"] % 3 != 2)
                    has_add = (st.get("cm") is not None) or (st.get("tri") is not None) or addmask
                    MM(ps[0:n, c0:c1], st["lhsT"], st["rhsQ"][:, c0:c1], True, not has_add, st["rk"], [pk])
                    if st.get("cm") is not None:
                        MM(ps[0:n, c0:c1], identb[0:n, 0:n], cmask[0:n, st["cm"], c0:c1], False, True, ["identb", "cmask"], [pk])
                    if st.get("tri") is not None:
                        tt, sub = st["tri"]
                        MM(ps[0:n, sub * 128:(sub + 1) * 128], identb[0:n, 0:n], tri[0:n, tt, :], False, True,
                           ["identb", "tri"], [pk])
                    if addmask:
                        MM(ps[0:n, c0:c1], identb[0:n, 0:n], maskS[0:n, st["mk"], c0:c1], False, True,
                           ["identb", f"mS{st['mk']}"], [pk])
                    pi = cnt["p"] % NPB
                    cnt["p"] += 1
                    pb = Pb[pi]
                    ACT(pb[0:n, c0:c1], ps[0:n, c0:c1], AF.Exp, [pk], [f"Pb{pi}"], bias=st["bias"])
                    src, srck = pb, f"Pb{pi}"
                    if st.get("mk") is not None and not addmask:
                        kt = st["mk"]
                        p2 = P2[pi]
                        eng = "dve" if cnt["m"] % 2 == 0 else "pool"
                        cnt["m"] += 1
                        TT(p2[0:n, c0:c1], pb[0:n, c0:c1], maskS[0:n, kt, c0:c1], ALU.mult, [srck, f"mS{kt}"], [f"P2{pi}"], eng=eng)
                        src, srck = p2, f"P2{pi}"
                    st["src"] = (src, srck)

                def stageB(st):
                    src, srck = st["src"]
                    n = st["n"]
                    for sub in st["subs"]:
                        acc, ak = st["acc"](sub)
                        MM(acc, src[0:n, sub * 128:(sub + 1) * 128], st["rhsV"], st["first"](sub), st["last"], [srck] + st["vk"], [ak])
                    if st.get("after") is not None:
                        st["after"]()

                for i in range(len(steps) + DEPTH):
                    if i < len(steps):
                        stageA(steps[i])
                    if i >= DEPTH:
                        stageB(steps[i - DEPTH])

            def bc(ap2, shape):
                return ap2.unsqueeze(2).to_broadcast(shape)

            CUT = 150.0

            def far(h, q_first, k_last):
                return sl[h] * float(q_first - k_last) > CUT

            all_steps = []
            bufsel = {"qa": 0}

            def loads_gb(g, b, qi):
                q0 = b * 512
                qa = QA[qi]
                qk = f"QA{qi}"
                gs = gsig[qi]
                gk = f"gsig{qi}"
                DMA("sp", qa[0:64, :, :], qT_d[g * 256:(g + 1) * 256, q0:q0 + 512].rearrange("(r d) s -> d r s", d=64),
                    (), [qk])
                DMA("sp", qa[64:70, :, :], qaug_d[:, 4 * g:4 * g + 4, :], (), [qk])
                DMA("sp", gs[:], gsg_d[q0:q0 + 512, :].rearrange("(s p) c -> p s c", p=128), (), [gk])

            def loads_group(g):
                DMA("sp", KsA[0:64, :], ksT_d[g * 64:(g + 1) * 64, :], (), ["KsA"])
                DMA("sp", KwA[0:64, :], kwT_d[g * 64:(g + 1) * 64, :], (), ["KwA"])
                DMA("sp", KcA[0:64, :], kcc_d[g * 64:(g + 1) * 64, :], (), ["KcA"])
                DMA("act", Vs1[:, :, 0:64], vsw_d[:, g * 64:(g + 1) * 64].rearrange("(t p) d -> p t d", p=128), (), ["Vs1"])
                DMA("act", Vw1[:, :, 0:64], vsw_d[:, 256 + g * 64:256 + (g + 1) * 64].rearrange("(t p) d -> p t d", p=128),
                    (), ["Vw1"])
                DMA("act", CA1[:, :, 0:64], vcc_d[:, g * 64:(g + 1) * 64].rearrange("(t p) d -> p t d", p=128), (), ["CA1"])

            def build_gb(g, b, qi, nxt):
                q0 = b * 512
                qa = QA[qi]
                qk = f"QA{qi}"
                gs = gsig[qi]
                gk = f"gsig{qi}"
                Yt = Y[qi]
                yk = f"Y{qi}"
                gb_steps = []
                n_ct = min(NCT, (32 * b + 30) // 128 + 1)

                def selection_chain():
                    for sub in range(4):
                        i_ = 4 * b + sub
                        c_lo = 126 - 2 * i_
                        TT(sc1[:], IMP[:, sub, :], mvalid[:, c_lo:c_lo + 128], ALU.mult, [f"IMP{sub // 2}", "mvalid"], ["sc1"])
                        TT(sc1[:], sc1[:], madd[:, c_lo:c_lo + 128], ALU.add, ["sc1", "madd"], ["sc1"])
                        MEMSET(sc1[:, 0:1], 1e4, ["sc1"])
                        SC.op("dve", lambda e: e.max(out=m8[:], in_=sc1[:]), ["sc1"], ["m8"])
                        SC.op("dve", lambda e: e.match_replace(out=sc2[:], in_to_replace=m8[:], in_values=sc1[:], imm_value=-5.0),
                              ["sc1", "m8"], ["sc2"])
                        SC.op("dve", lambda e: e.max(out=m8[:], in_=sc2[:]), ["sc2"], ["m8"])
                        SC.op("dve", lambda e: e.match_replace(out=sc3[:], in_to_replace=m8[:], in_values=sc2[:], imm_value=-5.0),
                              ["sc2", "m8"], ["sc3"])
                        TT(selq[sub][:], sc3[:], sc1[:], ALU.not_equal, ["sc3", "sc1"], [f"selq{sub}"])

                for r in range(4):
                    h = 4 * g + r
                    accs = [psA[0], psA[1]] if r % 2 == 0 else [psA[2], psA[3]]
                    acck = ["psA0", "psA1"] if r % 2 == 0 else ["psA2", "psA3"]

                    def post_cmp(r=r, h=h, accs=accs, acck=acck):
                        rc = rcs[r]
                        rk_ = f"rcs{r}"
                        for j in range(2):
                            a2 = accs[j][:, 0:386].rearrange("p (s c) -> p s c", c=193)
                            ak = acck[j]
                            TS(rc[:, 0:2], a2[:, :, 192], 1e-30, None, ALU.max, None, [ak], [rk_])
                            RECIP(rc[:, 0:2], rc[:, 0:2], [rk_], [rk_])
                            TT(rc[:, 2:4], rc[:, 0:2], gs[:, 2 * j:2 * j + 2, 3 * h], ALU.mult, [rk_, gk], [rk_])
                            if r == 0:
                                TT(IMP[:, 2 * j:2 * j + 2, :], a2[:, :, 64:192], bc(rc[:, 0:2], [128, 2, 128]), ALU.mult,
                                   [ak, rk_], [f"IMP{j}"])
                            else:
                                TT(impt[:], a2[:, :, 64:192], bc(rc[:, 0:2], [128, 2, 128]), ALU.mult, [ak, rk_], ["impt"])
                                TT(IMP[:, 2 * j:2 * j + 2, :], IMP[:, 2 * j:2 * j + 2, :], impt[:], ALU.add,
                                   ["impt", f"IMP{j}"], [f"IMP{j}"], eng="pool")
                            TT(Yt[:, 2 * j:2 * j + 2, r * 64:(r + 1) * 64], a2[:, :, 0:64], bc(rc[:, 2:4], [128, 2, 64]), ALU.mult,
                               [ak, rk_], [yk + f"_{r}"])
                        if r == 3:
                            selection_chain()
                    cts = [ct for ct in range(n_ct) if not far(h, q0, 16 * (min(NC, ct * 128 + 128) - 1) + 31)]
                    if not cts:
                        cts = [n_ct - 1]
                    for ct in cts:
                        n = min(128, NC - ct * 128)
                        m = b - 4 * ct
                        st = dict(kt=ct, n=n, lhsT=KcA[0:128, ct * 128:ct * 128 + n], rhsQ=qa[0:128, r, :], cols=(0, 512),
                                  rk=["KcA", qk], bias=sl[h] * (16.0 * 128 * ct + 31.0 - q0),
                                  cm=(m if m <= 4 else None), subs=[0, 1, 2, 3], rhsV=CA1[0:n, ct, :], vk=["CA1"],
                                  last=(ct == cts[-1]))
                        st["acc"] = (lambda sub, accs=accs, acck=acck:
                                     (accs[sub // 2][:, (sub % 2) * 193:(sub % 2) * 193 + 193], acck[sub // 2]))
                        st["first"] = (lambda sub, ct=ct, c0_=cts[0]: ct == c0_ and sub % 2 == 0)
                        if ct == cts[-1]:
                            st["after"] = post_cmp
                        gb_steps.append(st)

                def post_sw(r, bi):
                    h = 4 * g + r
                    rc = rcs[r]
                    rk_ = f"rcs{r}"
                    a = psA[r]
                    ak = f"psA{r}"
                    av = a[:, 0:260].rearrange("p (s c) -> p s c", c=65)
                    TS(rc[:, 0:4], av[:, :, 64], 1e-30, None, ALU.max, None, [ak], [rk_])
                    RECIP(rc[:, 0:4], rc[:, 0:4], [rk_], [rk_])
                    TT(rc[:, 4:8], rc[:, 0:4], gs[:, :, 3 * h + bi], ALU.mult, [rk_, gk], [rk_])
                    yt_ = ytmp[r % 2]
                    TT(yt_[:], av[:, :, 0:64], bc(rc[:, 4:8], [128, 4, 64]), ALU.mult, [ak, rk_], [f"ytmp{r % 2}"])
                    TT(Yt[:, :, r * 64:(r + 1) * 64], Yt[:, :, r * 64:(r + 1) * 64], yt_[:], ALU.add,
                       [f"ytmp{r % 2}", yk + f"_{r}"], [yk + f"_{r}"], eng="pool")

                def mask_one(kt):
                    d = kt - 4 * b
                    c0 = 128 * d if d > 0 else 0
                    ps, pk = next_s()
                    MM(ps[:, c0:512], Gm[:, kt * 128:(kt + 1) * 128], selT[:, c0:512], True, True, ["Gm", "selT"], [pk])
                    if kt % 3 != 2:
                        TS(maskS[:, kt, c0:512], ps[:, c0:512], 30000.0, -30000.0, ALU.mult, ALU.add, [pk], [f"mS{kt}"])
                    else:
                        CP(maskS[:, kt, c0:512], ps[:, c0:512], [pk], [f"mS{kt}"])

                MLOOK = 2

                def mask_build(kts):
                    ps, pk = next_s()
                    for sub in range(4):
                        TR(ps[:, sub * 128:(sub + 1) * 128], selq[sub][:], ident[:], [f"selq{sub}", "ident"], [pk])
                    CP(selT[:], ps[:], [pk], ["selT"])
                    for kt in kts[:MLOOK]:
                        mask_one(kt)

                for kind in ("win", "sel"):
                    bi = 1 if kind == "sel" else 2
                    KA, KAk = (KsA, "KsA") if kind == "sel" else (KwA, "KwA")
                    VA, VAk = (Vs1, "Vs1") if kind == "sel" else (Vw1, "Vw1")
                    kts = list(range(0, 4 * b + 4)) if kind == "sel" else [kt for kt in range(4 * b - 4, 4 * b + 4) if kt >= 0]
                    started = set()
                    seen_kt = set()
                    first_of_kind = True
                    used_kts = [kt for kt in kts if any(not far(4 * g + r, q0, 128 * kt + 127) for r in range(4))]
                    for kt in kts:
                        d = kt - 4 * b
                        if kind == "sel":
                            subs = [0, 1, 2, 3] if d < 0 else list(range(d, 4))
                            trim = (0, d) if d >= 0 else None
                        else:
                            subs = list(range(max(d, 0), min(d + 4, 3) + 1))
                            trim = None
                            if d >= 0:
                                trim = (0, d)
                            elif d + 4 <= 3:
                                trim = (1, d + 4)
                        c0, c1 = subs[0] * 128, (subs[-1] + 1) * 128
                        for r in range(4):
                            h = 4 * g + r
                            if far(h, q0, 128 * kt + 127):
                                continue
                            st = dict(kt=kt, n=128, lhsT=KA[0:128, kt * 128:(kt + 1) * 128], rhsQ=qa[0:128, r, :],
                                      cols=(c0, c1), rk=[KAk, qk], bias=sl[h] * (128.0 * kt - q0), subs=subs,
                                      rhsV=VA[:, kt, :], vk=[VAk], last=(kt == kts[-1]), tri=trim,
                                      mk=(kt if kind == "sel" else None))
                            st["acc"] = (lambda sub, r=r: (psA[r][:, sub * 65:sub * 65 + 65], f"psA{r}"))

                            def first(sub, r=r, started=started):
                                key = (r, sub)
                                if key in started:
                                    return False
                                isf = not any(k_[0] == r for k_ in started)
                                started.add(key)
                                return isf
                            st["first"] = first
                            if first_of_kind:
                                first_of_kind = False
                                if kind == "win":
                                    if nxt is not None:
                                        st["before"] = (lambda nxt=nxt: loads_gb(*nxt))
                                else:
                                    def bf0(used_kts=used_kts):
                                        mask_build(used_kts)
                                        if len(used_kts) > MLOOK:
                                            mask_one(used_kts[MLOOK])
                                    st["before"] = bf0
                                    seen_kt.add(kt)
                            elif kind == "sel" and kt not in seen_kt:
                                seen_kt.add(kt)
                                j_ = used_kts.index(kt)
                                if j_ + MLOOK < len(used_kts):
                                    st["before"] = (lambda ktn=used_kts[j_ + MLOOK]: mask_one(ktn))
                            if kt == kts[-1]:
                                if kind == "sel" and r == 3:
                                    def fin(r=r, bi=bi):
                                        post_sw(r, bi)
                                        CP(Yb[qi][:], Yt[:], [yk + f"_{rr}" for rr in range(4)], [f"Yb{qi}"], eng="pool")
                                        DMA("sp", ya_d[q0:q0 + 512, g * 256:(g + 1) * 256].rearrange("(s p) c -> p s c", p=128),
                                            Yb[qi][:], [f"Yb{qi}"], [])
                                    st["after"] = fin
                                else:
                                    st["after"] = (lambda r=r, bi=bi: post_sw(r, bi))
                            gb_steps.append(st)
                return gb_steps

            idx = 0
            for g in range(4):
                loads_group(g)
                loads_gb(g, 0, idx % 2)
                g_steps = []
                for b in range(NT):
                    qi = idx % 2
                    nxt = (g, b + 1, 1 - qi) if b + 1 < NT else None
                    g_steps.extend(build_gb(g, b, qi, nxt))
                    idx += 1
                run_pipeline(g_steps)
            SC.emit()

        def load_bf16_w(dst, src, nk, keyp):
            for k in range(nk):
                DMA("pool", dst[:, k, :], src[k * 128:(k + 1) * 128, :], (), [f"{keyp}{k}"], max_dma_last_dim=4096)

        def post_norm_residual(ps_pair, pk_pair, res, resk, gtile, gk, outt, outk, tmp, ssq2, sfx):
            sfx = sfx or ""
            kt_, k0, k1, kr = "pn_tmp" + sfx, "pn_ssq0" + sfx, "pn_ssq1" + sfx, "pn_rstd" + sfx
            for hf in range(2):
                ACT(tmp[:, hf * 512:(hf + 1) * 512], ps_pair[hf][:], AF.Square, [pk_pair[hf]], [kt_, (k0, k1)[hf]],
                    accum=ssq2[:, hf:hf + 1])
            TT(ssq2[:, 2:3], ssq2[:, 0:1], ssq2[:, 1:2], ALU.add, [k0, k1], [kr])
            rms_rstd(ssq2[:, 2:3], ssq2[:, 2:3], [kr], [kr])
            for hf in range(2):
                STT(tmp[:, hf * 512:(hf + 1) * 512], ps_pair[hf][:], ssq2[:, 2:3], gtile[:, hf * 512:(hf + 1) * 512],
                    ALU.mult, ALU.mult, [pk_pair[hf], kr, gk], [kt_])
            TT(outt, tmp[:], res, ALU.add, [kt_, resk], [outk], eng="pool")

        ffn_es = ExitStack()
        Wgu = ffn_es.enter_context(nc.sbuf_tensor("sb_Wgu", [128, 8, 2 * DFF], BF16))
        Wd = ffn_es.enter_context(nc.sbuf_tensor("sb_Wd", [128, 22, D], BF16))
        with ExitStack() as es:
            def sb(name, shape, dt=F32):
                return es.enter_context(nc.sbuf_tensor("sb_" + name, list(shape), dt))
            Wo = sb("Wo", [128, 8, D], BF16)
            gpost = sb("gpost", [128, D])
            gat = [sb(f"gat{i}", [128, 512], BF16) for i in range(4)]
            yrt = [sb(f"yrt{i}", [128, 512], BF16) for i in range(4)]
            yTs = [sb(f"yT{i}", [128, 8, 512], BF16) for i in range(2)]
            tmpfs = [sb(f"tmpf{i}", [128, 512]) for i in range(2)]
            xres = [sb(f"xres{i}", [128, D]) for i in range(3)]
            tmps = [sb(f"pn_tmp{i}", [128, D]) for i in range(2)]
            ssq2s = [sb(f"pn_ssq{i}", [128, 3]) for i in range(2)]
            psT4 = [es.enter_context(nc.psum_tensor(f"psT4{i}", [128, 512], F32)) for i in range(4)]
            psU = [es.enter_context(nc.psum_tensor(f"psU{i}", [128, 1024], BF16)) for i in range(4)]
            yab = [sb(f"yab{i}", [128, 512], BF16) for i in range(4)]
            load_bf16_w(Wo, w_out_d, 8, "Wo")
            DMA("sp", gpost[:], gains_d[1], (), ["gpost"])
            load_bf16_w(Wgu, wgu_d, 8, "Wgu")
            load_bf16_w(Wd, wd_d, 22, "Wd")
            xcnt = [0]

            def merge4(b):
                t0 = b * 512
                yT = yTs[b % 2]
                ytk_ = f"yT{b % 2}_"
                for half in range(2):
                    for s in range(4):
                        r0 = t0 + s * 128
                        yb_ = yab[s % 4]
                        ybk = f"yab{s % 4}"
                        DMA("sp", yb_[:], ya_d[r0:r0 + 128, half * 512:(half + 1) * 512], (), [ybk])
                        for cc in range(4):
                            TR(psU[cc][:, s * 128:(s + 1) * 128], yb_[:, cc * 128:(cc + 1) * 128], identb[:], [ybk, "identb"],
                               [f"psU{cc}"])
                    for cc in range(4):
                        c = half * 4 + cc
                        ga_ = gat[c % 4]
                        yr_ = yrt[c % 4]
                        tf = tmpfs[c % 2]
                        DMA("sp", ga_[:], ga_d[c * 128:(c + 1) * 128, t0:t0 + 512], (), [f"gat{c % 4}"])
                        DMA("sp", yr_[:], yr_d[c * 128:(c + 1) * 128, t0:t0 + 512], (), [f"yrt{c % 4}"])
                        TT(tf[:], ga_[:], psU[cc][:, 0:512], ALU.mult, [f"gat{c % 4}", f"psU{cc}"], [f"tmpf{c % 2}"])
                        TT(yT[:, c, :], tf[:], yr_[:], ALU.add, [f"tmpf{c % 2}", f"yrt{c % 4}"], [ytk_ + str(c)],
                           eng=("pool" if c % 2 == 0 else "dve"))

            def proj4(b):
                t0 = b * 512
                yT = yTs[b % 2]
                ytk_ = f"yT{b % 2}_"
                for s in range(4):
                    r0 = t0 + s * 128
                    xi = xcnt[0] % 3
                    xcnt[0] += 1
                    xr_ = xres[xi]
                    xrk = f"xres{xi}"
                    DMA("sp", xr_[:], x_d[r0:r0 + 128, :], (), [xrk])
                    pp = [psT4[(s % 2) * 2], psT4[(s % 2) * 2 + 1]]
                    ppk = [f"psT4{(s % 2) * 2}", f"psT4{(s % 2) * 2 + 1}"]
                    for hf in range(2):
                        for k in range(8):
                            MM(pp[hf][:], yT[:, k, s * 128:(s + 1) * 128], Wo[:, k, hf * 512:(hf + 1) * 512], k == 0, k == 7,
                               [ytk_ + str(k), f"Wo{k}"], [ppk[hf]])
                    post_norm_residual(pp, ppk, xr_[:], xrk, gpost, "gpost", xr_[:], xrk, tmps[s % 2], ssq2s[s % 2], f"a{s % 2}")
                    DMA("pool", x1_d[r0:r0 + 128, :], xr_[:], [xrk], [])

            merge4(0)
            for b in range(NT):
                if b + 1 < NT:
                    merge4(b + 1)
                proj4(b)
            SC.emit()

        with ExitStack() as es:
            def sb(name, shape, dt=F32):
                return es.enter_context(nc.sbuf_tensor("sb_" + name, list(shape), dt))
            TW = 256
            gfpre = sb("gfpre", [128, D])
            gfpost = sb("gfpost", [128, D])
            x1t = [sb(f"x1t{i}", [128, D]) for i in range(4)]
            h2 = sb("h2", [128, D], BF16)
            h2Ts = [sb(f"h2T{i}", [128, 8, TW], BF16) for i in range(2)]
            aT = sb("aT", [128, 22, TW], BF16)
            sgt = [sb(f"sgt{i}", [128, TW]) for i in range(2)]
            junk = sb("junk4", [128, D])
            ssq = sb("ssq4", [128, 2])
            tmp = sb("pn_tmp4", [128, D])
            ssq2 = sb("pn_ssq4", [128, 3])
            xo = [sb(f"xo4{i}", [128, D]) for i in range(2)]
            psTt = [es.enter_context(nc.psum_tensor(f"psTt{i}", [128, 1024], BF16)) for i in range(2)]
            psGU = [es.enter_context(nc.psum_tensor(f"psGU{i}", [128, 512], F32)) for i in range(4)]
            psD = [es.enter_context(nc.psum_tensor(f"psD{i}", [128, 512], F32)) for i in range(2)]
            DMA("sp", gfpre[:], gains_d[2], (), ["gfpre"])
            DMA("sp", gfpost[:], gains_d[3], (), ["gfpost"])
            NS = TW // 128
            gi = [0]
            NB4 = S // TW

            def prep4(b):
                t0 = b * TW
                hT_ = h2Ts[b % 2]
                for s in range(NS):
                    r0 = t0 + s * 128
                    xi = (b % 2) * NS + s
                    xt = x1t[xi]
                    xk = f"x1t{xi}"
                    DMA("sp", xt[:], x1_d[r0:r0 + 128, :], (), [xk])
                    ACT(junk[:], xt[:], AF.Square, [xk], ["junk4", "ssq4"], accum=ssq[:, 0:1])
                    rms_rstd(ssq[:, 0:1], ssq[:, 0:1], ["ssq4"], ["ssq4"])
                    STT(h2[:], xt[:], ssq[:, 0:1], gfpre[:], ALU.mult, ALU.mult, [xk, "ssq4", "gfpre"], ["h2"])
                    for c in range(8):
                        TR(psTt[c // 4][:, (c % 4) * 128:(c % 4 + 1) * 128], h2[:, c * 128:(c + 1) * 128], identb[:], ["h2", "identb"],
                           [f"psTt{c // 4}"])
                    for hh in range(2):
                        src = psTt[hh][:, 0:512].rearrange("p (c t) -> p c t", c=4)
                        if hh == 0:
                            CP(hT_[:, hh * 4:(hh + 1) * 4, s * 128:(s + 1) * 128], src, [f"psTt{hh}"], [f"h2T{b % 2}_{s}"])
                        else:
                            ACT(hT_[:, hh * 4:(hh + 1) * 4, s * 128:(s + 1) * 128], src, AF.Identity, [f"psTt{hh}"],
                                [f"h2T{b % 2}_{s}"])

            def gu4(b):
                hT_ = h2Ts[b % 2]
                hk = [f"h2T{b % 2}_{s}" for s in range(NS)]
                for f in range(22):
                    pg = psGU[gi[0] % 4]
                    pgk = f"psGU{gi[0] % 4}"
                    gi[0] += 1
                    for k in range(8):
                        MM(pg[:, 0:TW], Wgu[:, k, f * 128:(f + 1) * 128], hT_[:, k, :], k == 0, k == 7, [f"Wgu{k}"] + hk, [pgk])
                    for k in range(8):
                        MM(pg[:, TW:2 * TW], Wgu[:, k, DFF + f * 128:DFF + (f + 1) * 128], hT_[:, k, :], k == 0, k == 7,
                           [f"Wgu{k}"] + hk, [pgk])
                    sg = sgt[f % 2]
                    ACT(sg[:], pg[:, 0:TW], AF.Silu, [pgk], [f"sgt{f % 2}"])
                    TT(aT[:, f, :], sg[:], pg[:, TW:2 * TW], ALU.mult, [f"sgt{f % 2}", pgk], [f"aT{f}"])

            def down4(b):
                t0 = b * TW
                for s in range(NS):
                    r0 = t0 + s * 128
                    xi = (b % 2) * NS + s
                    xt = x1t[xi]
                    xk = f"x1t{xi}"
                    for hf in range(2):
                        for f in range(22):
                            MM(psD[hf][:], aT[:, f, s * 128:(s + 1) * 128], Wd[:, f, hf * 512:(hf + 1) * 512], f == 0, f == 21,
                               [f"aT{f}", f"Wd{f}"], [f"psD{hf}"])
                    xo_ = xo[s % 2]
                    xok = f"xo4{s % 2}"
                    post_norm_residual(psD, ["psD0", "psD1"], xt[:], xk, gfpost, "gfpost", xo_[:], xok, tmp, ssq2, None)
                    DMA("pool", x2_d[r0:r0 + 128, :], xo_[:], [xok], [])

            prep4(0)
            for b in range(NB4):
                gu4(b)
                if b + 1 < NB4:
                    prep4(b + 1)
                down4(b)
            SC.emit()

        ffn_es.close()

        with ExitStack() as es:
            def sb(name, shape, dt=F32):
                return es.enter_context(nc.sbuf_tensor("sb_" + name, list(shape), dt))
            Wpg = sb("Wpg", [128, 8, D], BF16)
            Wpp = sb("Wpp", [128, 2, D], BF16)
            bple = sb("bple", [128, D])
            x2t = [sb(f"x2t{i}", [128, D]) for i in range(3)]
            pt_ = [sb(f"ptl{i}", [128, 256]) for i in range(3)]
            x2Ts = [sb(f"x2T{i}", [128, 8, 128], BF16) for i in range(2)]
            pTs = [sb(f"pT{i}", [128, 2, 128], BF16) for i in range(2)]
            tgs = [sb(f"tg{i}", [128, D]) for i in range(2)]
            og = [sb(f"og{i}", [128, D]) for i in range(2)]
            psTt = [es.enter_context(nc.psum_tensor(f"psTc{i}", [128, 512], F32)) for i in range(3)]
            psGp = [es.enter_context(nc.psum_tensor(f"psGp{i}", [128, 512], F32)) for i in range(2)]
            psPp = [es.enter_context(nc.psum_tensor(f"psPp{i}", [128, 512], F32)) for i in range(2)]
            load_bf16_w(Wpg, wpg_d, 8, "Wpg")
            load_bf16_w(Wpp, wpp_d, 2, "Wpp")
            DMA("sp", bple[:], gains_d[4], (), ["bple"])
            def prep_c(i_):
                r0 = i_ * 128
                xt = x2t[i_ % 3]
                xk = f"x2t{i_ % 3}"
                pl = pt_[i_ % 3]
                plk = f"ptl{i_ % 3}"
                x2T, x2Tk = x2Ts[i_ % 2], f"x2T{i_ % 2}"
                pT, pTk = pTs[i_ % 2], f"pT{i_ % 2}"
                DMA("sp", xt[:], x2_d[r0:r0 + 128, :], (), [xk])
                DMA("sp", pl[:], p_d[r0:r0 + 128, :], (), [plk])
                for c in range(8):
                    TR(psTt[c // 4][:, (c % 4) * 128:(c % 4 + 1) * 128], xt[:, c * 128:(c + 1) * 128], ident[:], [xk, "ident"],
                       [f"psTc{c // 4}"])
                for c in range(2):
                    TR(psTt[2][:, c * 128:(c + 1) * 128], pl[:, c * 128:(c + 1) * 128], ident[:], [plk, "ident"], ["psTc2"])
                for hh in range(2):
                    CP(x2T[:, hh * 4:(hh + 1) * 4, :], psTt[hh][:].rearrange("p (c t) -> p c t", c=4), [f"psTc{hh}"], [x2Tk])
                ACT(pT[:, :, :], psTt[2][:, 0:256].rearrange("p (c t) -> p c t", c=2), AF.Identity, ["psTc2"], [pTk])

            def main_c(i_):
                r0 = i_ * 128
                xt = x2t[i_ % 3]
                xk = f"x2t{i_ % 3}"
                x2T, x2Tk = x2Ts[i_ % 2], f"x2T{i_ % 2}"
                pT, pTk = pTs[i_ % 2], f"pT{i_ % 2}"
                tg, tgk = tgs[i_ % 2], f"tg{i_ % 2}_"
                o_ = og[i_ % 2]
                ok_ = f"og{i_ % 2}"
                for hf in range(2):
                    for k in range(8):
                        MM(psGp[hf][:], x2T[:, k, :], Wpg[:, k, hf * 512:(hf + 1) * 512], k == 0, k == 7, [x2Tk, f"Wpg{k}"],
                           [f"psGp{hf}"])
                    for k in range(2):
                        MM(psPp[hf][:], pT[:, k, :], Wpp[:, k, hf * 512:(hf + 1) * 512], k == 0, k == 1, [pTk, f"Wpp{k}"],
                           [f"psPp{hf}"])
                    cs = slice(hf * 512, (hf + 1) * 512)
                    TT(tg[:, cs], psGp[hf][:], bple[:, cs], ALU.add, [f"psGp{hf}", "bple"], [tgk + str(hf)])
                    ACT(tg[:, cs], tg[:, cs], AF.Sigmoid, [tgk + str(hf)], [tgk + str(hf)])
                    TT(tg[:, cs], tg[:, cs], psPp[hf][:], ALU.mult, [tgk + str(hf), f"psPp{hf}"], [tgk + str(hf)])
                    TT(o_[:, cs], tg[:, cs], xt[:, cs], ALU.add, [tgk + str(hf), xk], [ok_ + str(hf)], eng="pool")
                DMA("pool", out_d[r0:r0 + 128, :], o_[:], [ok_ + "0", ok_ + "1"], [])

            NI = S // 128
            prep_c(0)
            for i_ in range(NI):
                if i_ + 1 < NI:
                    prep_c(i_ + 1)
                main_c(i_)
            SC.emit()
    nc._n_ops = SC.total
    return nc


def prep_shared(inp, S):
    f = lambda a: np.ascontiguousarray(np.asarray(a, dtype=np.float32))
    m = {}
    m["w_in"] = f(inp["w_in"][0])
    m["w_out"] = f(inp["w_out"][0])
    m["wgu"] = f(inp["ffn_w_gate_up"][0])
    m["wd"] = f(inp["ffn_w_down"][0])
    m["wpp"] = f(inp["ple_w_proj"][0])
    m["wpg"] = f(inp["ple_w_gate"][0])
    gains = np.stack([np.broadcast_to(np.asarray(inp[k][0], np.float32)[None, :], (128, D)) for k in
                      ("norm_mix_pre", "norm_mix_post", "norm_ffn_pre", "norm_ffn_post", "ple_b_gate")])
    m["gains"] = f(gains)
    cols = []
    cw = np.asarray(inp["conv_w"][0], np.float32)
    for k in range(4):
        cols.append(cw[k].reshape(8, 128).T)
    for key in ("conv_b", "lru_ba", "lru_bx", "lru_lambda"):
        cols.append(np.asarray(inp[key][0], np.float32).reshape(8, 128).T)
    m["pv"] = f(np.concatenate(cols, axis=1))
    for nm, key in (("bda", "lru_wa"), ("bdx", "lru_wx")):
        wsrc = np.asarray(inp[key][0], np.float32)
        bd = np.zeros((128, 8, 128), np.float32)
        for c in range(8):
            bd[0:64, c, 0:64] = wsrc[2 * c]
            bd[64:128, c, 64:128] = wsrc[2 * c + 1]
        m[nm] = bd
    m["ckw1"] = f(inp["cmp_k_w1"][0])
    m["cvw1"] = f(inp["cmp_v_w1"][0])
    m["ckw2"] = f(inp["cmp_k_w2"][0])
    m["cvw2"] = f(inp["cmp_v_w2"][0])
    m["posTk"] = f(np.asarray(inp["cmp_pos_k"][0], np.float32).reshape(16, 128).T)
    m["posTv"] = f(np.asarray(inp["cmp_pos_v"][0], np.float32).reshape(16, 128).T)
    m.update(make_consts(S))
    return m


_NC_CACHE = {}


def kernel(**inputs):
    S = 8192
    x = np.asarray(inputs["x"], np.float32)
    p = np.asarray(inputs["p"], np.float32)
    B = x.shape[0]
    if S not in _NC_CACHE:
        _NC_CACHE[S] = build(S)
    nc = _NC_CACHE[S]
    shared = prep_shared(inputs, S)
    in_maps = []
    for b in range(B):
        m = dict(shared)
        m["x"] = np.ascontiguousarray(x[b])
        m["p"] = np.ascontiguousarray(p[0, b])
        in_maps.append(m)
    res = run_bass_kernel_spmd(nc, in_maps, core_ids=list(range(B)))
    return np.stack([np.asarray(r["out"], np.float32) for r in res.results], axis=0)
```

```python
import numpy as np
import ml_dtypes
from contextlib import ExitStack
import concourse.bass as bass
import concourse.mybir as mybir
from concourse.bass_utils import run_bass_kernel_spmd

F32 = mybir.dt.float32
BF16 = mybir.dt.bfloat16
AF = mybir.ActivationFunctionType
ALU = mybir.AluOpType

D = 1024
DIN = 6704
DFF = 2816
EPS = 1e-6
ENGS = ("pe", "act", "dve", "pool", "sp")
C_XR, C_GR, C_Q, C_KC, C_VC, C_KS, C_VS, C_KW, C_VW, C_GN, C_GMR, C_GMA = (
    0, 1024, 2048, 3072, 3328, 3584, 3840, 4096, 4352, 4608, 4656, 5680)


class Sched:
    def __init__(self, nc, n_dma_sems=8):
        self.nc = nc
        self.nd = n_dma_sems
        self.semkey = {}
        for e in ENGS:
            self.semkey[("E", e)] = nc.alloc_semaphore(name=f"S_{e}")
        self.dmaq = ("sp", "act", "pool")
        for e in self.dmaq:
            for i in range(n_dma_sems):
                self.semkey[("D", e, i)] = nc.alloc_semaphore(name=f"D_{e}{i}")
        self.eng_cnt = {e: 0 for e in ENGS}
        self.dma_cnt = {e: 0 for e in self.dmaq}
        self.known = {e: {} for e in ENGS}
        self.dma_sem_last = {}
        self.ops = []
        self.total = 0

    def op(self, eng, fn, r=(), w=(), dma=False):
        self.ops.append((eng, fn, tuple(r), tuple(w), dma))

    def full_clock(self):
        c = {}
        for e in ENGS:
            if self.eng_cnt[e] > 0:
                c[("E", e)] = self.eng_cnt[e]
        for e in self.dmaq:
            for i in range(self.nd):
                n = (self.dma_cnt[e] - i + self.nd - 1) // self.nd
                if n > 0:
                    c[("D", e, i)] = 16 * n
        return c

    def emit(self):
        nc = self.nc
        ops = self.ops
        nops = len(ops)
        self.total += nops
        last_w = {}
        readers = {}
        deps = [None] * nops
        for k, (eng, fn, rd, wr, dma) in enumerate(ops):
            d = set()
            for r in rd:
                if r in last_w:
                    d.add(last_w[r])
            for w in wr:
                if w in last_w:
                    d.add(last_w[w])
                for x in readers.get(w, ()):
                    d.add(x)
            d.discard(k)
            deps[k] = d
            for w in wr:
                last_w[w] = k
                readers[w] = []
            for r in rd:
                if r not in wr:
                    readers.setdefault(r, []).append(k)
        sig = [None] * nops
        clock = [None] * nops
        per_eng = {e: [] for e in ENGS}
        dma_sem_last = {}
        for k, (eng, fn, rd, wr, dma) in enumerate(ops):
            waits = {}
            kn = self.known[eng]
            dl = set(deps[k])
            if dma:
                i = self.dma_cnt[eng] % self.nd
                sk = ("D", eng, i)
                self.dma_cnt[eng] += 1
                if sk in dma_sem_last:
                    dl.add(dma_sem_last[sk])
                dma_sem_last[sk] = k
            for d in dl:
                if (not ops[d][4]) and ops[d][0] == eng and eng == "pe":
                    continue
                s, v = sig[d]
                if kn.get(s, 0) >= v:
                    continue
                if waits.get(s, 0) < v:
                    waits[s] = v
            for d in dl:
                if (not ops[d][4]) and ops[d][0] == eng and eng == "pe":
                    continue
                for cs, cv in clock[d].items():
                    if kn.get(cs, 0) < cv:
                        kn[cs] = cv
            if dma:
                val = 16 * ((self.dma_cnt[eng] - 1) // self.nd) + 16
                sig[k] = (sk, val)
                c = dict(kn)
                c[sk] = val
                clock[k] = c
                per_eng[eng].append((waits, fn, sk, 16))
            else:
                self.eng_cnt[eng] += 1
                sk = ("E", eng)
                sig[k] = (sk, self.eng_cnt[eng])
                c = dict(kn)
                c[sk] = self.eng_cnt[eng]
                clock[k] = c
                per_eng[eng].append((waits, fn, sk, 1))
        final = self.full_clock()
        engobj = {"pe": nc.tensor, "act": nc.scalar, "dve": nc.vector, "pool": nc.gpsimd, "sp": nc.sync}
        semkey = self.semkey

        def run(ename):
            E = engobj[ename]
            for waits, fn, sk, inc in per_eng[ename]:
                for s, v in waits.items():
                    E.wait_ge(semkey[s], v)
                fn(E).then_inc(semkey[sk], inc)
            kn = self.known[ename]
            for s, v in final.items():
                if s == ("E", ename):
                    continue
                if kn.get(s, 0) < v:
                    E.wait_ge(semkey[s], v)

        with nc.Block() as block:
            @block.tensor
            def _(e):
                run("pe")

            @block.scalar
            def _(e):
                run("act")

            @block.vector
            def _(e):
                run("dve")

            @block.gpsimd
            def _(e):
                run("pool")

            @block.sync
            def _(e):
                run("sp")
        for e in ENGS:
            self.known[e] = dict(final)
        self.ops = []


def _split3(v):
    v = np.asarray(v, np.float32)
    hi = v.astype(ml_dtypes.bfloat16)
    r1 = (v - hi.astype(np.float32)).astype(np.float32)
    mid = r1.astype(ml_dtypes.bfloat16)
    r2 = (r1 - mid.astype(np.float32)).astype(np.float32)
    lo = r2.astype(ml_dtypes.bfloat16)
    return hi, mid, lo


def _slopes():
    h = np.arange(1, 17, dtype=np.float32)
    return np.exp2(-8.0 * h / 16.0).astype(np.float32)


def make_consts(S):
    bf = ml_dtypes.bfloat16
    sl = _slopes()
    c = {}
    c["ident"] = np.eye(128, dtype=np.float32)
    qaug = np.zeros((6, 16, 512), dtype=bf)
    ql = np.arange(512, dtype=np.float32)
    for h in range(16):
        a, b, cc = _split3(np.full((512,), sl[h], np.float32))
        qaug[0, h], qaug[1, h], qaug[2, h] = a, b, cc
        a, b, cc = _split3(-sl[h] * ql)
        qaug[3, h], qaug[4, h], qaug[5, h] = a, b, cc
    c["qaug"] = qaug
    kl = np.arange(S, dtype=np.float32) % 128
    kaug = np.ones((6, S), np.float32)
    kaug[0:3] = kl
    c["kaug"] = kaug.astype(bf)
    kaugc = np.ones((6, 512), np.float32)
    kaugc[0:3] = 16.0 * (np.arange(512, dtype=np.float32) % 128)
    c["kaugc"] = kaugc.astype(bf)
    kk = np.arange(128)[:, None]
    qq = np.arange(128)[None, :]
    tri = (np.stack([(qq >= kk), (qq < kk)]).astype(np.float32) - 1.0) * 30000.0
    c["tri"] = np.ascontiguousarray(tri.transpose(1, 0, 2)).astype(bf)
    cl = np.arange(128)[:, None]
    qL = np.arange(512)[None, :]
    cm = (np.stack([(qL - 16 * cl - 31 + 512 * m >= 0) for m in range(5)]).astype(np.float32) - 1.0) * 30000.0
    c["cmask"] = np.ascontiguousarray(cm.transpose(1, 0, 2)).astype(bf)
    NC = S // 16 - 1
    NCT = (NC + 128) // 128
    A1 = np.zeros((NCT * 128, 129), np.float32)
    for cc_ in range(NC):
        j, rem = cc_ // 4, cc_ % 4
        if rem < 3:
            if j < 128:
                A1[cc_, j] = 2.0
        else:
            if j < 128:
                A1[cc_, j] = 1.0
            if j + 1 < 128:
                A1[cc_, j + 1] = 1.0
        A1[cc_, 128] = 1.0
    c["A1"] = np.ascontiguousarray(A1.reshape(NCT, 128, 129).transpose(1, 0, 2)).astype(bf)
    G = np.zeros((128, S), np.float32)
    y = np.arange(S)
    G[y // 64, y] = 1.0
    c["G"] = G.astype(bf)
    mv = np.zeros((128, 256), np.float32)
    ma = np.zeros((128, 256), np.float32)
    for p in range(128):
        cur = 0 if p < 64 else 1
        for col in range(256):
            jr = col - 126
            valid = jr <= cur
            forced = (jr == cur) or (jr == cur - 1)
            if not valid:
                ma[p, col] = -1.0
            elif forced:
                ma[p, col] = 1e4
            else:
                mv[p, col] = 1.0
    c["mvalid"] = mv
    c["madd"] = ma
    return c


CONST_SPECS = None


def build(S, debug=False):
    assert S % 2048 == 0
    NT = S // 512
    NKT = S // 128
    NC = S // 16 - 1
    NCT = (NC + 128) // 128
    sl = [float(v) for v in _slopes()]
    nc = bass.Bass("TRN2", target_bir_lowering=False)
    skind = "ExternalOutput" if debug else "Internal"

    def din(name, shape, dt=F32):
        return nc.dram_tensor(name, list(shape), dt, kind="ExternalInput").ap()

    def dscr(name, shape, dt=F32):
        return nc.dram_tensor(name, list(shape), dt, kind=skind).ap()

    x_d = din("x", [S, D])
    p_d = din("p", [S, 256])
    w_in_d = din("w_in", [D, DIN])
    w_out_d = din("w_out", [D, D])
    wgu_d = din("wgu", [D, 2 * DFF])
    wd_d = din("wd", [DFF, D])
    wpp_d = din("wpp", [256, D])
    wpg_d = din("wpg", [D, D])
    gains_d = din("gains", [5, 128, D])
    pv_d = din("pv", [128, 64])
    bda_d = din("bda", [128, 8, 128])
    bdx_d = din("bdx", [128, 8, 128])
    cw1_d = [din("ckw1", [2048, 256]), din("cvw1", [2048, 256])]
    cw2_d = [din("ckw2", [256, 64]), din("cvw2", [256, 64])]
    posT_d = [din("posTk", [128, 16]), din("posTv", [128, 16])]
    ident_d = din("ident", [128, 128])
    qaug_d = din("qaug", [6, 16, 512], BF16)
    kaug_d = din("kaug", [6, S], BF16)
    kaugc_d = din("kaugc", [6, 512], BF16)
    tri_d = din("tri", [128, 2, 128], BF16)
    cmask_d = din("cmask", [128, 5, 512], BF16)
    A1_d = din("A1", [128, NCT, 129], BF16)
    G_d = din("G", [128, S], BF16)
    mvalid_d = din("mvalid", [128, 256])
    madd_d = din("madd", [128, 256])
    out_d = nc.dram_tensor("out", [S, D], F32, kind="ExternalOutput").ap()

    qT_d = dscr("s_qT", [D, S], BF16)
    kcT_d = dscr("s_kcT", [256, S], BF16)
    vcT_d = dscr("s_vcT", [256, S], BF16)
    ksT_d = dscr("s_ksT", [256, S], BF16)
    kwT_d = dscr("s_kwT", [256, S], BF16)
    vsw_d = dscr("s_vsw", [S, 512], BF16)
    gsg_d = dscr("s_gsg", [S, 48])
    yr_d = dscr("s_yr", [D, S], BF16)
    ga_d = dscr("s_ga", [D, S], BF16)
    kcc_d = dscr("s_kcc", [256, 512], BF16)
    vcc_d = dscr("s_vcc", [NCT * 128, 256], BF16)
    ya_d = dscr("s_ya", [S, D])
    x1_d = dscr("s_x1", [S, D])
    x2_d = dscr("s_x2", [S, D])

    SC = Sched(nc)

    def MM(out, lhsT, rhs, start, stop, r, w):
        SC.op("pe", lambda e: e.matmul(out, lhsT=lhsT, rhs=rhs, start=start, stop=stop, skip_group_check=True), r, w)

    def TR(out, in_, ident, r, w):
        SC.op("pe", lambda e: e.transpose(out, in_, ident), r, w)

    def ACT(out, in_, func, r, w, bias=0.0, scale=1.0, accum=None):
        SC.op("act", lambda e: e.activation(out=out, in_=in_, func=func, bias=bias, scale=scale, accum_out=accum), r, w)

    def TS(out, in0, s1, s2, op0, op1, r, w, eng="dve", accum=None):
        if op1 is None:
            SC.op(eng, lambda e: e.tensor_scalar(out=out, in0=in0, scalar1=s1, scalar2=None, op0=op0), r, w)
        else:
            SC.op(eng, lambda e: e.tensor_scalar(out=out, in0=in0, scalar1=s1, scalar2=s2, op0=op0, op1=op1), r, w)

    def TT(out, in0, in1, op, r, w, eng="dve"):
        SC.op(eng, lambda e: e.tensor_tensor(out=out, in0=in0, in1=in1, op=op), r, w)

    def STT(out, in0, scalar, in1, op0, op1, r, w):
        SC.op("dve", lambda e: e.scalar_tensor_tensor(out=out, in0=in0, scalar=scalar, in1=in1, op0=op0, op1=op1), r, w)

    def CP(out, in_, r, w, eng="dve"):
        SC.op(eng, lambda e: e.tensor_copy(out=out, in_=in_), r, w)

    def MEMSET(ap, val, w, eng="dve"):
        SC.op(eng, lambda e: e.memset(ap, val), (), w)

    def RECIP(out, in_, r, w):
        SC.op("dve", lambda e: e.reciprocal(out=out, in_=in_), r, w)

    def DMA(q, out, in_, r, w, **kw):
        SC.op(q, lambda e: e.dma_start(out=out, in_=in_, **kw), r, w, dma=True)

    def gelu_tanh(es_tmp, out, src, nparts, ncols, keyp, rkeys, wkeys):
        t1, t2 = es_tmp
        TT(t1, src, src, ALU.mult, rkeys, [keyp + "t1"])
        TS(t1, t1, 0.044715, 1.0, ALU.mult, ALU.add, [keyp + "t1"], [keyp + "t1"])
        TT(t1, t1, src, ALU.mult, [keyp + "t1"] + list(rkeys), [keyp + "t1"])
        ACT(t2, t1, AF.Sigmoid, [keyp + "t1"], [keyp + "t2"], scale=1.5957691216057308)
        TT(out, t2, src, ALU.mult, [keyp + "t2"] + list(rkeys), wkeys)

    def rms_rstd(rstd, ssq, r, w):
        ACT(rstd, ssq, AF.Sqrt, r, w, bias=epsb[:, 0:1], scale=1.0 / D)
        RECIP(rstd, rstd, w, w)

    with ExitStack() as g_es:
        ident = g_es.enter_context(nc.sbuf_tensor("sb_ident", [128, 128], F32))
        identb = g_es.enter_context(nc.sbuf_tensor("sb_identb", [128, 128], BF16))
        epsb = g_es.enter_context(nc.sbuf_tensor("sb_epsb", [128, 1], F32))
        DMA("sp", ident[:], ident_d, (), ["ident"])
        DMA("pool", identb[:], ident_d, (), ["identb"])
        MEMSET(epsb[:], EPS, ["epsb"])
        SC.emit()

        with ExitStack() as es:
            def sb(name, shape, dt=F32):
                return es.enter_context(nc.sbuf_tensor("sb_" + name, list(shape), dt))
            Wi = sb("Wi", [128, 8, DIN], BF16)
            bda = sb("bda", [128, 8, 128], BF16)
            bdx = sb("bdx", [128, 8, 128], BF16)
            pv = sb("pv", [128, 64])
            cneg = sb("cneg", [128, 8])
            cneg2 = sb("cneg2", [128, 8])
            ctmp = sb("ctmp", [128, 8])
            ctmp2 = sb("ctmp2", [128, 8])
            gpre = sb("gpre", [128, D])
            xs = [sb(f"xs{i}", [128, D]) for i in range(2)]
            junk = sb("junk", [128, D])
            hb = [sb("hb0", [128, 4, D], BF16)]
            hT = sb("hT", [128, 8, 512], BF16)
            ssq = sb("ssq", [128, 8])
            xbuf = sb("xbuf", [128, 8, 516])
            hlast = sb("hlast", [128, 8])
            T = {n: sb("t_" + n, [128, 512]) for n in
                 ("xc0", "xc1", "r", "i", "a", "a2", "bb", "hs", "g00", "g01", "t1", "t2", "ge", "sgm0", "sgm1")}
            xcb = [sb(f"xcb{i}", [128, 512], BF16) for i in range(2)]
            ev = [sb(f"ev{i}", [128, 512], BF16) for i in range(3)]
            evf = [sb(f"evf{i}", [128, 512], BF16) for i in range(2)]
            gsb = sb("gsb", [128, 48])
            psT = [es.enter_context(nc.psum_tensor(f"psT{i}", [128, 1024], BF16)) for i in range(2)]
            psM = [es.enter_context(nc.psum_tensor(f"psM{i}", [128, 512], F32)) for i in range(4)]
            psG = [es.enter_context(nc.psum_tensor(f"psG{i}", [128, 512], F32)) for i in range(2)]

            WBLK = [(0, 1024), (1024, 2048), (4656, 5680), (5680, 6704), (2048, 3072), (3072, 4656)]
            for j, (c0_, c1_) in enumerate(WBLK):
                for k in range(8):
                    DMA("pool", Wi[:, k, c0_:c1_], w_in_d[k * 128:(k + 1) * 128, c0_:c1_], (), [f"WiB{j}"],
                        max_dma_last_dim=4096)

            def wkey(col0):
                for j, (c0_, c1_) in enumerate(WBLK):
                    if c0_ <= col0 < c1_:
                        return f"WiB{j}"
                raise ValueError(col0)
            DMA("pool", bda[:], bda_d, (), ["bda"])
            DMA("pool", bdx[:], bdx_d, (), ["bdx"])
            DMA("sp", pv[:], pv_d, (), ["pv"])
            DMA("sp", gpre[:], gains_d[0], (), ["gpre"])
            MEMSET(xbuf[:], 0.0, ["xbuf"])
            MEMSET(hlast[:], 0.0, ["hlast"])
            onepb = sb("onepb", [128, 1])
            MEMSET(onepb[:], 1.0 + 2.0 ** -23, ["onepb"])
            lam = pv[:, 56:64]
            ACT(ctmp[:], lam, AF.Exp, ["pv"], ["ctmp"], scale=-1.0)
            TS(ctmp2[:], ctmp[:], -0.25, 1.0 / 3.0, ALU.mult, ALU.add, ["ctmp"], ["ctmp2"])
            TT(ctmp2[:], ctmp2[:], ctmp[:], ALU.mult, ["ctmp", "ctmp2"], ["ctmp2"])
            TS(ctmp2[:], ctmp2[:], -0.5, None, ALU.add, None, ["ctmp2"], ["ctmp2"])
            TT(ctmp2[:], ctmp2[:], ctmp[:], ALU.mult, ["ctmp", "ctmp2"], ["ctmp2"])
            TS(ctmp2[:], ctmp2[:], 1.0, None, ALU.add, None, ["ctmp2"], ["ctmp2"])
            TT(ctmp2[:], ctmp2[:], ctmp[:], ALU.mult, ["ctmp", "ctmp2"], ["ctmp2"])
            TS(cneg[:], ctmp2[:], -8.0, None, ALU.mult, None, ["ctmp2"], ["cneg"])
            TS(cneg2[:], ctmp2[:], -16.0, None, ALU.mult, None, ["ctmp2"], ["cneg"])

            def norm_tile(b):
                hbuf = hb[0]
                for s in range(4):
                    xt = xs[s % 2]
                    xk = f"xs{s % 2}"
                    r0 = b * 512 + s * 128
                    DMA("sp", xt[:], x_d[r0:r0 + 128, :], (), [xk])
                    col = (b % 2) * 4 + s
                    ACT(junk[:], xt[:], AF.Square, [xk], ["junk", f"ssq{col}"], accum=ssq[:, col:col + 1])
                    rms_rstd(ssq[:, col:col + 1], ssq[:, col:col + 1], [f"ssq{col}"], [f"ssq{col}"])
                    STT(hbuf[:, s, :], xt[:], ssq[:, col:col + 1], gpre[:], ALU.mult, ALU.mult,
                        [xk, f"ssq{col}", "gpre"], [f"hb_{s}"])

            norm_tile(0)
            evi = [0]
            mi = [0]

            def next_ps():
                i = mi[0] % 4
                mi[0] += 1
                return psM[i], f"psM{i}"

            def featmm(col0, ps, pk):
                for k in range(8):
                    MM(ps[:], Wi[:, k, col0:col0 + 128], hT[:, k, :], k == 0, k == 7, [wkey(col0), f"hT{k}"], [pk])

            OMB = 1.0 + 2.0 ** -23

            def conv_chunk(c):
                xk = f"xbuf{c}"
                xc = T[f"xc{c % 2}"]
                xck = f"xc{c % 2}"
                TS(xc[:], xbuf[:, c, 3:515], pv[:, 24 + c:25 + c], pv[:, 32 + c:33 + c], ALU.mult, ALU.add, [xk, "pv"], [xck])
                for kk in range(3):
                    STT(xc[:], xbuf[:, c, kk:kk + 512], pv[:, kk * 8 + c:kk * 8 + c + 1], xc[:], ALU.mult, ALU.add,
                        [xk, "pv", xck], [xck])
                CP(xbuf[:, c, 0:3], xbuf[:, c, 512:515], [xk], [xk], eng="pool")

            def conv_act(c):
                ACT(xcb[c % 2][:], T[f"xc{c % 2}"][:], AF.Identity, [f"xc{c % 2}"], [f"xcb{c % 2}"])

            def other_job(ji, job, t0):
                col0, dst, row0, scl, sig = job
                ps, pk = next_ps()
                featmm(col0, ps, pk)
                if sig:
                    ef = evf[ji % 2]
                    ACT(ef[:], ps[:], AF.Sigmoid, [pk], [f"evf{ji % 2}"])
                    DMA("sp", dst[row0:row0 + 128, t0:t0 + 512], ef[:], [f"evf{ji % 2}"], [])
                else:
                    e_ = ev[evi[0] % 3]
                    ek = f"ev{evi[0] % 3}"
                    evi[0] += 1
                    if ji % 2 == 0:
                        TS(e_[:], ps[:], scl, None, ALU.mult, None, [pk], [ek])
                    else:
                        ACT(e_[:], ps[:], AF.Identity, [pk], [ek], scale=scl)
                    DMA("sp", dst[row0:row0 + 128, t0:t0 + 512], e_[:], [ek], [])

            for b in range(NT):
                t0 = b * 512
                hbuf = hb[0]
                for c in range(8):
                    pt = psT[c % 2]
                    for s in range(4):
                        TR(pt[:, s * 128:(s + 1) * 128], hbuf[:, s, c * 128:(c + 1) * 128], identb[:],
                           [f"hb_{s}", "identb"], [f"psT{c % 2}"])
                    if c % 2 == 0:
                        CP(hT[:, c, :], pt[:, 0:512], [f"psT{c % 2}"], [f"hT{c}"])
                    else:
                        ACT(hT[:, c, :], pt[:, 0:512], AF.Identity, [f"psT{c % 2}"], [f"hT{c}"])
                if b + 1 < NT:
                    norm_tile(b + 1)
                for c in range(8):
                    ps, pk = next_ps()
                    featmm(C_XR + c * 128, ps, pk)
                    ACT(xbuf[:, c, 3:515], ps[:], AF.Identity, [pk], [f"xbuf{c}"])
                jobs = []
                for c in range(8):
                    jobs.append((C_GMA + c * 128, ga_d, c * 128, 1.0, True))
                    jobs.append((C_Q + c * 128, qT_d, c * 128, 0.125, False))
                    base, dst = ((C_KC, kcT_d), (C_VC, vcT_d), (C_KS, ksT_d), (C_KW, kwT_d))[c // 2]
                    jobs.append((base + (c % 2) * 128, dst, (c % 2) * 128, 1.0, False))
                conv_chunk(0)
                conv_act(0)
                for c in range(8):
                    if c + 1 < 8:
                        conv_chunk(c + 1)
                    xc = T[f"xc{c % 2}"]
                    xck = f"xc{c % 2}"
                    xb_, xbk = xcb[c % 2], f"xcb{c % 2}"
                    g0, g0k = T[f"g0{c % 2}"], f"g0{c % 2}"
                    sgm, sgmk = T[f"sgm{c % 2}"], f"sgm{c % 2}"
                    MM(psG[0][:], bda[:, c, :], xb_[:], True, True, ["bda", xbk], ["psG0"])
                    MM(psG[1][:], bdx[:, c, :], xb_[:], True, True, ["bdx", xbk], ["psG1"])
                    psg, pkg = next_ps()
                    featmm(C_GR + c * 128, psg, pkg)
                    psm, pkm = next_ps()
                    featmm(C_GMR + c * 128, psm, pkm)
                    ACT(T["r"][:], psG[0][:], AF.Sigmoid, ["psG0", "pv"], ["r"], bias=pv[:, 40 + c:41 + c])
                    ACT(T["i"][:], psG[1][:], AF.Sigmoid, ["psG1", "pv"], ["i"], bias=pv[:, 48 + c:49 + c])
                    ACT(T["a"][:], T["r"][:], AF.Exp, ["r", "cneg"], ["a"], scale=cneg[:, c:c + 1])
                    TT(T["a2"][:], T["a"][:], T["a"][:], ALU.mult, ["a"], ["a2"], eng="pool")
                    TS(T["a2"][:], T["a2"][:], -1.0, OMB, ALU.mult, ALU.add, ["a2"], ["a2"], eng="pool")
                    if c + 1 < 8:
                        conv_act(c + 1)
                    ACT(g0[:], psg[:], AF.Identity, [pkg], [g0k])
                    ACT(T["a2"][:], T["a2"][:], AF.Ln, ["a2"], ["a2"])
                    ACT(T["a2"][:], T["a2"][:], AF.Exp, ["a2"], ["a2"], scale=0.5)
                    ACT(sgm[:], psm[:], AF.Sigmoid, [pkm], [sgmk])
                    TT(T["bb"][:], T["i"][:], xc[:], ALU.mult, ["i", xck], ["bb"], eng="pool")
                    TT(T["bb"][:], T["bb"][:], T["a2"][:], ALU.mult, ["bb", "a2"], ["bb"])
                    SC.op("dve", lambda e, c=c: e.tensor_tensor_scan(out=T["hs"][:], data0=T["a"][:], data1=T["bb"][:],
                                                                   initial=hlast[:, c:c + 1], op0=ALU.mult, op1=ALU.add),
                          ["a", "bb", "hlast"], ["hs"])
                    CP(hlast[:, c:c + 1], T["hs"][:, 511:512], ["hs"], ["hlast"])
                    t1, t2 = T["t1"], T["t2"]
                    TT(t1[:], g0[:], g0[:], ALU.mult, [g0k], ["t1"], eng="pool")
                    TS(t1[:], t1[:], 0.044715, 1.0, ALU.mult, ALU.add, ["t1"], ["t1"], eng="pool")
                    TT(t1[:], t1[:], g0[:], ALU.mult, ["t1", g0k], ["t1"], eng="pool")
                    for jj in range(3):
                        other_job(3 * c + jj, jobs[3 * c + jj], t0)
                    ACT(t2[:], t1[:], AF.Sigmoid, ["t1"], ["t2"], scale=1.5957691216057308)
                    TT(T["ge"][:], t2[:], g0[:], ALU.mult, ["t2", g0k], ["ge"])
                    TT(T["ge"][:], T["ge"][:], T["hs"][:], ALU.mult, ["ge", "hs"], ["ge"])
                    ef = evf[c % 2]
                    TT(ef[:], T["ge"][:], sgm[:], ALU.mult, ["ge", sgmk], [f"evf{c % 2}"], eng="pool")
                    DMA("sp", yr_d[c * 128:(c + 1) * 128, t0:t0 + 512], ef[:], [f"evf{c % 2}"], [])
                for s in range(4):
                    ps, pk = next_ps()
                    for k in range(8):
                        MM(ps[:, 0:256], hT[:, k, s * 128:(s + 1) * 128], Wi[:, k, C_VS:C_VS + 256], k == 0, k == 7,
                           [wkey(C_VS), f"hT{k}"], [pk])
                    for k in range(8):
                        MM(ps[:, 256:512], hT[:, k, s * 128:(s + 1) * 128], Wi[:, k, C_VW:C_VW + 256], k == 0, k == 7,
                           [wkey(C_VS), f"hT{k}"], [pk])
                    e_ = ev[evi[0] % 3]
                    ek = f"ev{evi[0] % 3}"
                    evi[0] += 1
                    CP(e_[:], ps[:], [pk], [ek])
                    DMA("sp", vsw_d[t0 + s * 128:t0 + (s + 1) * 128, :], e_[:], [ek], [])
                    ps, pk = next_ps()
                    for k in range(8):
                        MM(ps[:, 0:48], hT[:, k, s * 128:(s + 1) * 128], Wi[:, k, C_GN:C_GN + 48], k == 0, k == 7,
                           [wkey(C_VS), f"hT{k}"], [pk])
                    ACT(gsb[:], ps[:, 0:48], AF.Sigmoid, [pk], ["gsb"])
                    DMA("sp", gsg_d[t0 + s * 128:t0 + (s + 1) * 128, :], gsb[:], ["gsb"], [])
            SC.emit()

        with ExitStack() as es:
            def sb(name, shape, dt=F32):
                return es.enter_context(nc.sbuf_tensor("sb_" + name, list(shape), dt))
            w1 = [sb(f"cw1_{i}", [128, 16, 256], BF16) for i in range(2)]
            w2 = [sb(f"cw2_{i}", [128, 2, 64], BF16) for i in range(2)]
            posT = [sb(f"posT{i}", [128, 16], BF16) for i in range(2)]
            cb = [sb(f"cbias{i}", [128, 2]) for i in range(2)]
            kin = [sb(f"kin{i}", [128, S], BF16) for i in range(3)]
            kinr = [sb(f"kinr{i}", [128, 8, S // 16], BF16) for i in range(2)]
            u = sb("c_u", [128, 512])
            t1 = sb("c_t1", [128, 512])
            t2 = sb("c_t2", [128, 512])
            Hh = [sb(f"c_H{i}", [128, 512], BF16) for i in range(2)]
            okc = sb("c_okc", [64, 512], BF16)
            ovc = sb("c_ovc", [128, 64], BF16)
            psH = [es.enter_context(nc.psum_tensor(f"psH{i}", [128, 512], F32)) for i in range(2)]
            psO = [es.enter_context(nc.psum_tensor(f"psO{i}", [128, 512], F32)) for i in range(2)]
            psB = es.enter_context(nc.psum_tensor("psB", [128, 512], F32))
            for kv in range(2):
                DMA("pool", w1[kv][:], cw1_d[kv].rearrange("(lc p) h -> p lc h", p=128), (), [f"w1{kv}"])
                DMA("pool", w2[kv][:], cw2_d[kv].rearrange("(c p) d -> p c d", p=128), (), [f"w2{kv}"])
                DMA("pool", posT[kv][:], posT_d[kv], (), [f"posT{kv}"])
            MEMSET(okc[:], 0.0, ["okc"])
            MEMSET(ovc[:], 0.0, ["ovc"])
            for i3 in range(3):
                MEMSET(kin[i3][64:128, S - 16:S], 0.0, [f"kin{i3}"])
            ki = 0
            for kv in range(2):
                src_d = kcT_d if kv == 0 else vcT_d
                for hc in range(2):
                    for lc in range(16):
                        MM(psB[:, hc:hc + 1], w1[kv][:, lc, hc * 128:(hc + 1) * 128], posT[kv][:, lc:lc + 1], lc == 0, lc == 15,
                           [f"w1{kv}", f"posT{kv}"], ["psB"])
                CP(cb[kv][:], psB[:, 0:2], ["psB"], [f"cb{kv}"])
                for g in range(4):
                    kt_ = kin[ki % 3]
                    kk_ = f"kin{ki % 3}"
                    kr_ = kinr[ki % 2]
                    krk = f"kinr{ki % 2}"
                    ki += 1
                    DMA("sp", kt_[0:64, :], src_d[g * 64:(g + 1) * 64, :], (), [kk_])
                    DMA("sp", kt_[64:128, 0:S - 1], src_d[g * 64:(g + 1) * 64, 1:S], (), [kk_])
                    kv4 = kt_[:].rearrange("p (i j t) -> p j t i", j=8, t=2)
                    CP(kr_[:, 0:4, :], kv4[:, 0:4, 0, :], [kk_], [krk + "a"])
                    CP(kr_[:, 4:8, :], kv4[:, 4:8, 0, :], [kk_], [krk + "b"], eng="pool")
                    for hc in range(2):
                        ph = psH[hc]
                        for lc in range(16):
                            j = lc % 8
                            rhs_ = kr_[:, j, 0:NC] if lc < 8 else kr_[:, j, 1:NC + 1]
                            MM(ph[:, 0:NC], w1[kv][:, lc, hc * 128:(hc + 1) * 128], rhs_,
                               lc == 0, lc == 15, [f"w1{kv}", krk + ("a" if j < 4 else "b")], [f"psH{hc}"])
                        ACT(u[:, 0:NC], ph[:, 0:NC], AF.Identity, [f"psH{hc}", f"cb{kv}"], ["c_u"], bias=cb[kv][:, hc:hc + 1])
                        gelu_tanh((t1[:, 0:NC], t2[:, 0:NC]), Hh[hc][:, 0:NC], u[:, 0:NC], 128, NC, "p2", ["c_u"], [f"H{hc}"])
                    if kv == 0:
                        po = psO[0]
                        for hc in range(2):
                            MM(po[0:64, 0:NC], w2[kv][:, hc, :], Hh[hc][:, 0:NC], hc == 0, hc == 1,
                               [f"w2{kv}", f"H{hc}"], ["psO0"])
                        CP(okc[:, 0:NC], po[0:64, 0:NC], ["psO0"], ["okc"])
                        DMA("sp", kcc_d[g * 64:(g + 1) * 64, :], okc[:], ["okc"], [])
                    else:
                        for ct in range(NCT):
                            n = min(128, NC - ct * 128)
                            po = psO[ct % 2]
                            for hc in range(2):
                                MM(po[0:n, 0:64], Hh[hc][:, ct * 128:ct * 128 + n], w2[kv][:, hc, :], hc == 0, hc == 1,
                                   [f"w2{kv}", f"H{hc}"], [f"psO{ct % 2}"])
                            CP(ovc[0:n, :], po[0:n, 0:64], [f"psO{ct % 2}"], ["ovc"])
                            DMA("sp", vcc_d[ct * 128:ct * 128 + 128, g * 64:(g + 1) * 64], ovc[:, :], ["ovc"], [])
            SC.emit()

        with ExitStack() as es:
            def sb(name, shape, dt=F32):
                return es.enter_context(nc.sbuf_tensor("sb_" + name, list(shape), dt))
            DEPTH = 3
            NPB = 6
            KsA = sb("KsA", [128, S], BF16)
            KwA = sb("KwA", [128, S], BF16)
            KcA = sb("KcA", [128, 512], BF16)
            Vs1 = sb("Vs1", [128, NKT, 65], BF16)
            Vw1 = sb("Vw1", [128, NKT, 65], BF16)
            CA1 = sb("CA1", [128, NCT, 193], BF16)
            Gm = sb("Gm", [128, S], BF16)
            tri = sb("tri", [128, 2, 128], BF16)
            cmask = sb("cmask", [128, 5, 512], BF16)
            mvalid = sb("mvalid", [128, 256])
            madd = sb("madd", [128, 256])
            maskS = sb("maskS", [128, NKT, 512], BF16)
            QA = [sb(f"QA{i}", [128, 4, 512], BF16) for i in range(2)]
            gsig = [sb(f"gsig{i}", [128, 4, 48]) for i in range(2)]
            Pb = [sb(f"Pb{i}", [128, 512], BF16) for i in range(NPB)]
            P2 = [sb(f"P2{i}", [128, 512], BF16) for i in range(NPB)]
            IMP = sb("IMP", [128, 4, 128])
            impt = sb("impt", [128, 2, 128])
            sc1 = sb("sc1", [128, 128])
            sc2 = sb("sc2", [128, 128])
            sc3 = sb("sc3", [128, 128])
            m8 = sb("m8", [128, 8])
            selq = [sb(f"selq{i}", [128, 128]) for i in range(4)]
            selT = sb("selT", [128, 512], BF16)
            Y = [sb(f"Y{i}", [128, 4, 256]) for i in range(2)]
            ytmp = [sb(f"ytmp{i}", [128, 4, 64]) for i in range(2)]
            rcs = [sb(f"rcs{i}", [128, 8]) for i in range(4)]
            psS = [es.enter_context(nc.psum_tensor(f"psS{i}", [128, 512], F32)) for i in range(4)]
            psA = [es.enter_context(nc.psum_tensor(f"psA{i}", [128, 512], F32)) for i in range(4)]

            DMA("sp", Gm[:], G_d, (), ["Gm"])
            DMA("sp", tri[:], tri_d, (), ["tri"])
            DMA("sp", cmask[:], cmask_d, (), ["cmask"])
            DMA("sp", mvalid[:], mvalid_d, (), ["mvalid"])
            DMA("sp", madd[:], madd_d, (), ["madd"])
            MEMSET(KsA[64:128, :], 0.0, ["KsA"])
            MEMSET(KwA[64:128, :], 0.0, ["KwA"], eng="pool")
            MEMSET(KcA[64:128, :], 0.0, ["KcA"])
            MEMSET(QA[0][64:128, :, :], 0.0, ["QA0"])
            MEMSET(QA[1][64:128, :, :], 0.0, ["QA1"], eng="pool")
            DMA("sp", KsA[64:70, :], kaug_d, (), ["KsA"])
            DMA("sp", KwA[64:70, :], kaug_d, (), ["KwA"])
            DMA("sp", KcA[64:70, :], kaugc_d, (), ["KcA"])
            DMA("sp", CA1[:, :, 64:193], A1_d, (), ["CA1"])
            MEMSET(Vs1[:, :, 64:65], 1.0, ["Vs1"])
            MEMSET(Vw1[:, :, 64:65], 1.0, ["Vw1"])

            cnt = {"s": 0, "p": 0, "qa": 0, "m": 0}
            negb = sb("negb", [128, 1])
            MEMSET(negb[:], -30000.0, ["negb"])

            def next_s():
                si = cnt["s"] % 4
                cnt["s"] += 1
                return psS[si], f"psS{si}"

            def run_pipeline(steps):
                def stageA(st):
                    if st.get("before") is not None:
                        st["before"]()
                    ps, pk = next_s()
                    c0, c1 = st["cols"]
                    n = st["n"]
                    addmask = (st.get("mk") is not None) and True
                    has_add = (st.get("cm") is not None) or (st.get("tri") is not None) or addmask
                    MM(ps[0:n, c0:c1], st["lhsT"], st["rhsQ"][:, c0:c1], True, not has_add, st["rk"], [pk])
                    if st.get("cm") is not None:
                        MM(ps[0:n, c0:c1], identb[0:n, 0:n], cmask[0:n, st["cm"], c0:c1], False, True, ["identb", "cmask"], [pk])
                    if st.get("tri") is not None:
                        tt, sub = st["tri"]
                        MM(ps[0:n, sub * 128:(sub + 1) * 128], identb[0:n, 0:n], tri[0:n, tt, :], False, True,
                           ["identb", "tri"], [pk])
                    if addmask:
                        MM(ps[0:n, c0:c1], identb[0:n, 0:n], maskS[0:n, st["mk"], c0:c1], False, True,
                           ["identb", f"mS{st['mk']}"], [pk])
                    pi = cnt["p"] % NPB
                    cnt["p"] += 1
                    pb = Pb[pi]
                    ACT(pb[0:n, c0:c1], ps[0:n, c0:c1], AF.Exp, [pk], [f"Pb{pi}"], bias=st["bias"])
                    src, srck = pb, f"Pb{pi}"
                    if st.get("mk") is not None and not addmask:
                        kt = st["mk"]
                        p2 = P2[pi]
                        eng = "dve" if cnt["m"] % 2 == 0 else "pool"
                        cnt["m"] += 1
                        TT(p2[0:n, c0:c1], pb[0:n, c0:c1], maskS[0:n, kt, c0:c1], ALU.mult, [srck, f"mS{kt}"], [f"P2{pi}"], eng=eng)
                        src, srck = p2, f"P2{pi}"
                    st["src"] = (src, srck)

                def stageB(st):
                    src, srck = st["src"]
                    n = st["n"]
                    for sub in st["subs"]:
                        acc, ak = st["acc"](sub)
                        MM(acc, src[0:n, sub * 128:(sub + 1) * 128], st["rhsV"], st["first"](sub), st["last"], [srck] + st["vk"], [ak])
                    if st.get("after") is not None:
                        st["after"]()

                for i in range(len(steps) + DEPTH):
                    if i < len(steps):
                        stageA(steps[i])
                    if i >= DEPTH:
                        stageB(steps[i - DEPTH])

            def bc(ap2, shape):
                return ap2.unsqueeze(2).to_broadcast(shape)

            CUT = 150.0

            def far(h, q_first, k_last):
                return sl[h] * float(q_first - k_last) > CUT

            all_steps = []
            bufsel = {"qa": 0}

            def loads_gb(g, b, qi):
                q0 = b * 512
                qa = QA[qi]
                qk = f"QA{qi}"
                gs = gsig[qi]
                gk = f"gsig{qi}"
                DMA("sp", qa[0:64, :, :], qT_d[g * 256:(g + 1) * 256, q0:q0 + 512].rearrange("(r d) s -> d r s", d=64),
                    (), [qk])
                DMA("sp", qa[64:70, :, :], qaug_d[:, 4 * g:4 * g + 4, :], (), [qk])
                DMA("sp", gs[:], gsg_d[q0:q0 + 512, :].rearrange("(s p) c -> p s c", p=128), (), [gk])

            def loads_group(g):
                DMA("sp", KsA[0:64, :], ksT_d[g * 64:(g + 1) * 64, :], (), ["KsA"])
                DMA("sp", KwA[0:64, :], kwT_d[g * 64:(g + 1) * 64, :], (), ["KwA"])
                DMA("sp", KcA[0:64, :], kcc_d[g * 64:(g + 1) * 64, :], (), ["KcA"])
                DMA("act", Vs1[:, :, 0:64], vsw_d[:, g * 64:(g + 1) * 64].rearrange("(t p) d -> p t d", p=128), (), ["Vs1"])
                DMA("act", Vw1[:, :, 0:64], vsw_d[:, 256 + g * 64:256 + (g + 1) * 64].rearrange("(t p) d -> p t d", p=128),
                    (), ["Vw1"])
                DMA("act", CA1[:, :, 0:64], vcc_d[:, g * 64:(g + 1) * 64].rearrange("(t p) d -> p t d", p=128), (), ["CA1"])

            def build_gb(g, b, qi, nxt):
                q0 = b * 512
                qa = QA[qi]
                qk = f"QA{qi}"
                gs = gsig[qi]
                gk = f"gsig{qi}"
                Yt = Y[qi]
                yk = f"Y{qi}"
                gb_steps = []
                n_ct = min(NCT, (32 * b + 30) // 128 + 1)

                def selection_chain():
                    for sub in range(4):
                        i_ = 4 * b + sub
                        c_lo = 126 - 2 * i_
                        TT(sc1[:], IMP[:, sub, :], mvalid[:, c_lo:c_lo + 128], ALU.mult, [f"IMP{sub // 2}", "mvalid"], ["sc1"])
                        TT(sc1[:], sc1[:], madd[:, c_lo:c_lo + 128], ALU.add, ["sc1", "madd"], ["sc1"])
                        MEMSET(sc1[:, 0:1], 1e4, ["sc1"])
                        SC.op("dve", lambda e: e.max(out=m8[:], in_=sc1[:]), ["sc1"], ["m8"])
                        SC.op("dve", lambda e: e.match_replace(out=sc2[:], in_to_replace=m8[:], in_values=sc1[:], imm_value=-5.0),
                              ["sc1", "m8"], ["sc2"])
                        SC.op("dve", lambda e: e.max(out=m8[:], in_=sc2[:]), ["sc2"], ["m8"])
                        SC.op("dve", lambda e: e.match_replace(out=sc3[:], in_to_replace=m8[:], in_values=sc2[:], imm_value=-5.0),
                              ["sc2", "m8"], ["sc3"])
                        TT(selq[sub][:], sc3[:], sc1[:], ALU.not_equal, ["sc3", "sc1"], [f"selq{sub}"])

                for r in range(4):
                    h = 4 * g + r
                    accs = [psA[0], psA[1]] if r % 2 == 0 else [psA[2], psA[3]]
                    acck = ["psA0", "psA1"] if r % 2 == 0 else ["psA2", "psA3"]

                    def post_cmp(r=r, h=h, accs=accs, acck=acck):
                        rc = rcs[r]
                        rk_ = f"rcs{r}"
                        for j in range(2):
                            a2 = accs[j][:, 0:386].rearrange("p (s c) -> p s c", c=193)
                            ak = acck[j]
                            TS(rc[:, 0:2], a2[:, :, 192], 1e-30, None, ALU.max, None, [ak], [rk_])
                            RECIP(rc[:, 0:2], rc[:, 0:2], [rk_], [rk_])
                            TT(rc[:, 2:4], rc[:, 0:2], gs[:, 2 * j:2 * j + 2, 3 * h], ALU.mult, [rk_, gk], [rk_])
                            if r == 0:
                                TT(IMP[:, 2 * j:2 * j + 2, :], a2[:, :, 64:192], bc(rc[:, 0:2], [128, 2, 128]), ALU.mult,
                                   [ak, rk_], [f"IMP{j}"])
                            else:
                                TT(impt[:], a2[:, :, 64:192], bc(rc[:, 0:2], [128, 2, 128]), ALU.mult, [ak, rk_], ["impt"])
                                TT(IMP[:, 2 * j:2 * j + 2, :], IMP[:, 2 * j:2 * j + 2, :], impt[:], ALU.add,
                                   ["impt", f"IMP{j}"], [f"IMP{j}"], eng="pool")
                            TT(Yt[:, 2 * j:2 * j + 2, r * 64:(r + 1) * 64], a2[:, :, 0:64], bc(rc[:, 2:4], [128, 2, 64]), ALU.mult,
                               [ak, rk_], [yk + f"_{r}"])
                        if r == 3:
                            selection_chain()
                    cts = [ct for ct in range(n_ct) if not far(h, q0, 16 * (min(NC, ct * 128 + 128) - 1) + 31)]
                    if not cts:
                        cts = [n_ct - 1]
                    for ct in cts:
                        n = min(128, NC - ct * 128)
                        m = b - 4 * ct
                        st = dict(kt=ct, n=n, lhsT=KcA[0:128, ct * 128:ct * 128 + n], rhsQ=qa[0:128, r, :], cols=(0, 512),
                                  rk=["KcA", qk], bias=sl[h] * (16.0 * 128 * ct + 31.0 - q0),
                                  cm=(m if m <= 4 else None), subs=[0, 1, 2, 3], rhsV=CA1[0:n, ct, :], vk=["CA1"],
                                  last=(ct == cts[-1]))
                        st["acc"] = (lambda sub, accs=accs, acck=acck:
                                     (accs[sub // 2][:, (sub % 2) * 193:(sub % 2) * 193 + 193], acck[sub // 2]))
                        st["first"] = (lambda sub, ct=ct, c0_=cts[0]: ct == c0_ and sub % 2 == 0)
                        if ct == cts[-1]:
                            st["after"] = post_cmp
                        gb_steps.append(st)

                def post_sw(r, bi):
                    h = 4 * g + r
                    rc = rcs[r]
                    rk_ = f"rcs{r}"
                    a = psA[r]
                    ak = f"psA{r}"
                    av = a[:, 0:260].rearrange("p (s c) -> p s c", c=65)
                    TS(rc[:, 0:4], av[:, :, 64], 1e-30, None, ALU.max, None, [ak], [rk_])
                    RECIP(rc[:, 0:4], rc[:, 0:4], [rk_], [rk_])
                    TT(rc[:, 4:8], rc[:, 0:4], gs[:, :, 3 * h + bi], ALU.mult, [rk_, gk], [rk_])
                    yt_ = ytmp[r % 2]
                    TT(yt_[:], av[:, :, 0:64], bc(rc[:, 4:8], [128, 4, 64]), ALU.mult, [ak, rk_], [f"ytmp{r % 2}"])
                    TT(Yt[:, :, r * 64:(r + 1) * 64], Yt[:, :, r * 64:(r + 1) * 64], yt_[:], ALU.add,
                       [f"ytmp{r % 2}", yk + f"_{r}"], [yk + f"_{r}"], eng="pool")

                def mask_one(kt):
                    d = kt - 4 * b
                    c0 = 128 * d if d > 0 else 0
                    ps, pk = next_s()
                    MM(ps[:, c0:512], Gm[:, kt * 128:(kt + 1) * 128], selT[:, c0:512], True, True, ["Gm", "selT"], [pk])
                    if True:
                        TS(maskS[:, kt, c0:512], ps[:, c0:512], 30000.0, -30000.0, ALU.mult, ALU.add, [pk], [f"mS{kt}"])
                    else:
                        CP(maskS[:, kt, c0:512], ps[:, c0:512], [pk], [f"mS{kt}"])

                MLOOK = 2

                def mask_build(kts):
                    ps, pk = next_s()
                    for sub in range(4):
                        TR(ps[:, sub * 128:(sub + 1) * 128], selq[sub][:], ident[:], [f"selq{sub}", "ident"], [pk])
                    CP(selT[:], ps[:], [pk], ["selT"])
                    for kt in kts[:MLOOK]:
                        mask_one(kt)

                for kind in ("win", "sel"):
                    bi = 1 if kind == "sel" else 2
                    KA, KAk = (KsA, "KsA") if kind == "sel" else (KwA, "KwA")
                    VA, VAk = (Vs1, "Vs1") if kind == "sel" else (Vw1, "Vw1")
                    kts = list(range(0, 4 * b + 4)) if kind == "sel" else [kt for kt in range(4 * b - 4, 4 * b + 4) if kt >= 0]
                    started = set()
                    seen_kt = set()
                    first_of_kind = True
                    used_kts = [kt for kt in kts if any(not far(4 * g + r, q0, 128 * kt + 127) for r in range(4))]
                    for kt in kts:
                        d = kt - 4 * b
                        if kind == "sel":
                            subs = [0, 1, 2, 3] if d < 0 else list(range(d, 4))
                            trim = (0, d) if d >= 0 else None
                        else:
                            subs = list(range(max(d, 0), min(d + 4, 3) + 1))
                            trim = None
                            if d >= 0:
                                trim = (0, d)
                            elif d + 4 <= 3:
                                trim = (1, d + 4)
                        c0, c1 = subs[0] * 128, (subs[-1] + 1) * 128
                        for r in range(4):
                            h = 4 * g + r
                            if far(h, q0, 128 * kt + 127):
                                continue
                            st = dict(kt=kt, n=128, lhsT=KA[0:128, kt * 128:(kt + 1) * 128], rhsQ=qa[0:128, r, :],
                                      cols=(c0, c1), rk=[KAk, qk], bias=sl[h] * (128.0 * kt - q0), subs=subs,
                                      rhsV=VA[:, kt, :], vk=[VAk], last=(kt == kts[-1]), tri=trim,
                                      mk=(kt if kind == "sel" else None))
                            st["acc"] = (lambda sub, r=r: (psA[r][:, sub * 65:sub * 65 + 65], f"psA{r}"))

                            def first(sub, r=r, started=started):
                                key = (r, sub)
                                if key in started:
                                    return False
                                isf = not any(k_[0] == r for k_ in started)
                                started.add(key)
                                return isf
                            st["first"] = first
                            if first_of_kind:
                                first_of_kind = False
                                if kind == "win":
                                    if nxt is not None:
                                        st["before"] = (lambda nxt=nxt: loads_gb(*nxt))
                                else:
                                    def bf0(used_kts=used_kts):
                                        mask_build(used_kts)
                                        if len(used_kts) > MLOOK:
                                            mask_one(used_kts[MLOOK])
                                    st["before"] = bf0
                                    seen_kt.add(kt)
                            elif kind == "sel" and kt not in seen_kt:
                                seen_kt.add(kt)
                                j_ = used_kts.index(kt)
                                if j_ + MLOOK < len(used_kts):
                                    st["before"] = (lambda ktn=used_kts[j_ + MLOOK]: mask_one(ktn))
                            if kt == kts[-1]:
                                if kind == "sel" and r == 3:
                                    def fin(r=r, bi=bi):
                                        post_sw(r, bi)
                                        DMA("sp", ya_d[q0:q0 + 512, g * 256:(g + 1) * 256].rearrange("(s p) c -> p s c", p=128),
                                            Yt[:], [yk + f"_{rr}" for rr in range(4)], [])
                                    st["after"] = fin
                                else:
                                    st["after"] = (lambda r=r, bi=bi: post_sw(r, bi))
                            gb_steps.append(st)
                return gb_steps

            idx = 0
            for g in range(4):
                loads_group(g)
                loads_gb(g, 0, idx % 2)
                g_steps = []
                for b in range(NT):
                    qi = idx % 2
                    nxt = (g, b + 1, 1 - qi) if b + 1 < NT else None
                    g_steps.extend(build_gb(g, b, qi, nxt))
                    idx += 1
                run_pipeline(g_steps)
            SC.emit()

        def load_bf16_w(dst, src, nk, keyp):
            for k in range(nk):
                DMA("pool", dst[:, k, :], src[k * 128:(k + 1) * 128, :], (), [f"{keyp}{k}"], max_dma_last_dim=4096)

        def post_norm_residual(ps_pair, pk_pair, res, resk, gtile, gk, outt, outk, tmp, ssq2, sfx):
            sfx = sfx or ""
            kt_, k0, k1, kr = "pn_tmp" + sfx, "pn_ssq0" + sfx, "pn_ssq1" + sfx, "pn_rstd" + sfx
            for hf in range(2):
                ACT(tmp[:, hf * 512:(hf + 1) * 512], ps_pair[hf][:], AF.Square, [pk_pair[hf]], [kt_, (k0, k1)[hf]],
                    accum=ssq2[:, hf:hf + 1])
            TT(ssq2[:, 2:3], ssq2[:, 0:1], ssq2[:, 1:2], ALU.add, [k0, k1], [kr])
            rms_rstd(ssq2[:, 2:3], ssq2[:, 2:3], [kr], [kr])
            for hf in range(2):
                STT(tmp[:, hf * 512:(hf + 1) * 512], ps_pair[hf][:], ssq2[:, 2:3], gtile[:, hf * 512:(hf + 1) * 512],
                    ALU.mult, ALU.mult, [pk_pair[hf], kr, gk], [kt_])
            TT(outt, tmp[:], res, ALU.add, [kt_, resk], [outk], eng="pool")

        ffn_es = ExitStack()
        Wgu = ffn_es.enter_context(nc.sbuf_tensor("sb_Wgu", [128, 8, 2 * DFF], BF16))
        Wd = ffn_es.enter_context(nc.sbuf_tensor("sb_Wd", [128, 22, D], BF16))
        with ExitStack() as es:
            def sb(name, shape, dt=F32):
                return es.enter_context(nc.sbuf_tensor("sb_" + name, list(shape), dt))
            Wo = sb("Wo", [128, 8, D], BF16)
            gpost = sb("gpost", [128, D])
            yat = [sb(f"yat{i}", [128, 512]) for i in range(2)]
            gat = [sb(f"gat{i}", [128, 512], BF16) for i in range(4)]
            yrt = [sb(f"yrt{i}", [128, 512], BF16) for i in range(4)]
            yTs = [sb(f"yT{i}", [128, 8, 512], BF16) for i in range(2)]
            tmpfs = [sb(f"tmpf{i}", [128, 512]) for i in range(2)]
            xres = [sb(f"xres{i}", [128, D]) for i in range(3)]
            tmps = [sb(f"pn_tmp{i}", [128, D]) for i in range(2)]
            ssq2s = [sb(f"pn_ssq{i}", [128, 3]) for i in range(2)]
            psT4 = [es.enter_context(nc.psum_tensor(f"psT4{i}", [128, 512], F32)) for i in range(4)]
            psU = [es.enter_context(nc.psum_tensor(f"psU{i}", [128, 1024], BF16)) for i in range(4)]
            yab = [sb(f"yab{i}", [128, 512], BF16) for i in range(2)]
            load_bf16_w(Wo, w_out_d, 8, "Wo")
            DMA("sp", gpost[:], gains_d[1], (), ["gpost"])
            load_bf16_w(Wgu, wgu_d, 8, "Wgu")
            load_bf16_w(Wd, wd_d, 22, "Wd")
            xcnt = [0]

            def merge4(b):
                t0 = b * 512
                yT = yTs[b % 2]
                ytk_ = f"yT{b % 2}_"
                for half in range(2):
                    for s in range(4):
                        yt_ = yat[s % 2]
                        ytk = f"yat{s % 2}"
                        r0 = t0 + s * 128
                        DMA("sp", yt_[:, 0:512], ya_d[r0:r0 + 128, half * 512:(half + 1) * 512], (), [ytk])
                        yb_ = yab[s % 2]
                        ybk = f"yab{s % 2}"
                        ACT(yb_[:], yt_[:, 0:512], AF.Identity, [ytk], [ybk])
                        for cc in range(4):
                            TR(psU[cc][:, s * 128:(s + 1) * 128], yb_[:, cc * 128:(cc + 1) * 128], identb[:], [ybk, "identb"],
                               [f"psU{cc}"])
                    for cc in range(4):
                        c = half * 4 + cc
                        ga_ = gat[c % 4]
                        yr_ = yrt[c % 4]
                        tf = tmpfs[c % 2]
                        DMA("sp", ga_[:], ga_d[c * 128:(c + 1) * 128, t0:t0 + 512], (), [f"gat{c % 4}"])
                        DMA("sp", yr_[:], yr_d[c * 128:(c + 1) * 128, t0:t0 + 512], (), [f"yrt{c % 4}"])
                        TT(tf[:], ga_[:], psU[cc][:, 0:512], ALU.mult, [f"gat{c % 4}", f"psU{cc}"], [f"tmpf{c % 2}"])
                        TT(yT[:, c, :], tf[:], yr_[:], ALU.add, [f"tmpf{c % 2}", f"yrt{c % 4}"], [ytk_ + str(c)],
                           eng=("pool" if c % 2 == 0 else "dve"))

            def proj4(b):
                t0 = b * 512
                yT = yTs[b % 2]
                ytk_ = f"yT{b % 2}_"
                for s in range(4):
                    r0 = t0 + s * 128
                    xi = xcnt[0] % 3
                    xcnt[0] += 1
                    xr_ = xres[xi]
                    xrk = f"xres{xi}"
                    DMA("sp", xr_[:], x_d[r0:r0 + 128, :], (), [xrk])
                    pp = [psT4[(s % 2) * 2], psT4[(s % 2) * 2 + 1]]
                    ppk = [f"psT4{(s % 2) * 2}", f"psT4{(s % 2) * 2 + 1}"]
                    for hf in range(2):
                        for k in range(8):
                            MM(pp[hf][:], yT[:, k, s * 128:(s + 1) * 128], Wo[:, k, hf * 512:(hf + 1) * 512], k == 0, k == 7,
                               [ytk_ + str(k), f"Wo{k}"], [ppk[hf]])
                    post_norm_residual(pp, ppk, xr_[:], xrk, gpost, "gpost", xr_[:], xrk, tmps[s % 2], ssq2s[s % 2], f"a{s % 2}")
                    DMA("pool", x1_d[r0:r0 + 128, :], xr_[:], [xrk], [])

            merge4(0)
            for b in range(NT):
                if b + 1 < NT:
                    merge4(b + 1)
                proj4(b)
            SC.emit()

        with ExitStack() as es:
            def sb(name, shape, dt=F32):
                return es.enter_context(nc.sbuf_tensor("sb_" + name, list(shape), dt))
            TW = 256
            gfpre = sb("gfpre", [128, D])
            gfpost = sb("gfpost", [128, D])
            x1t = [sb(f"x1t{i}", [128, D]) for i in range(4)]
            h2 = sb("h2", [128, D], BF16)
            h2Ts = [sb(f"h2T{i}", [128, 8, TW], BF16) for i in range(2)]
            aT = sb("aT", [128, 22, TW], BF16)
            sgt = [sb(f"sgt{i}", [128, TW]) for i in range(2)]
            junk = sb("junk4", [128, D])
            ssq = sb("ssq4", [128, 2])
            tmp = sb("pn_tmp4", [128, D])
            ssq2 = sb("pn_ssq4", [128, 3])
            xo = [sb(f"xo4{i}", [128, D]) for i in range(2)]
            psTt = [es.enter_context(nc.psum_tensor(f"psTt{i}", [128, 1024], BF16)) for i in range(2)]
            psGU = [es.enter_context(nc.psum_tensor(f"psGU{i}", [128, 512], F32)) for i in range(4)]
            psD = [es.enter_context(nc.psum_tensor(f"psD{i}", [128, 512], F32)) for i in range(2)]
            DMA("sp", gfpre[:], gains_d[2], (), ["gfpre"])
            DMA("sp", gfpost[:], gains_d[3], (), ["gfpost"])
            NS = TW // 128
            gi = [0]
            NB4 = S // TW

            def prep4(b):
                t0 = b * TW
                hT_ = h2Ts[b % 2]
                for s in range(NS):
                    r0 = t0 + s * 128
                    xi = (b % 2) * NS + s
                    xt = x1t[xi]
                    xk = f"x1t{xi}"
                    DMA("sp", xt[:], x1_d[r0:r0 + 128, :], (), [xk])
                    ACT(junk[:], xt[:], AF.Square, [xk], ["junk4", "ssq4"], accum=ssq[:, 0:1])
                    rms_rstd(ssq[:, 0:1], ssq[:, 0:1], ["ssq4"], ["ssq4"])
                    STT(h2[:], xt[:], ssq[:, 0:1], gfpre[:], ALU.mult, ALU.mult, [xk, "ssq4", "gfpre"], ["h2"])
                    for c in range(8):
                        TR(psTt[c // 4][:, (c % 4) * 128:(c % 4 + 1) * 128], h2[:, c * 128:(c + 1) * 128], identb[:], ["h2", "identb"],
                           [f"psTt{c // 4}"])
                    for hh in range(2):
                        src = psTt[hh][:, 0:512].rearrange("p (c t) -> p c t", c=4)
                        if hh == 0:
                            CP(hT_[:, hh * 4:(hh + 1) * 4, s * 128:(s + 1) * 128], src, [f"psTt{hh}"], [f"h2T{b % 2}_{s}"])
                        else:
                            ACT(hT_[:, hh * 4:(hh + 1) * 4, s * 128:(s + 1) * 128], src, AF.Identity, [f"psTt{hh}"],
                                [f"h2T{b % 2}_{s}"])

            def gu4(b):
                hT_ = h2Ts[b % 2]
                hk = [f"h2T{b % 2}_{s}" for s in range(NS)]
                for f in range(22):
                    pg = psGU[gi[0] % 4]
                    pgk = f"psGU{gi[0] % 4}"
                    gi[0] += 1
                    for k in range(8):
                        MM(pg[:, 0:TW], Wgu[:, k, f * 128:(f + 1) * 128], hT_[:, k, :], k == 0, k == 7, [f"Wgu{k}"] + hk, [pgk])
                    for k in range(8):
                        MM(pg[:, TW:2 * TW], Wgu[:, k, DFF + f * 128:DFF + (f + 1) * 128], hT_[:, k, :], k == 0, k == 7,
                           [f"Wgu{k}"] + hk, [pgk])
                    sg = sgt[f % 2]
                    ACT(sg[:], pg[:, 0:TW], AF.Silu, [pgk], [f"sgt{f % 2}"])
                    TT(aT[:, f, :], sg[:], pg[:, TW:2 * TW], ALU.mult, [f"sgt{f % 2}", pgk], [f"aT{f}"])

            def down4(b):
                t0 = b * TW
                for s in range(NS):
                    r0 = t0 + s * 128
                    xi = (b % 2) * NS + s
                    xt = x1t[xi]
                    xk = f"x1t{xi}"
                    for hf in range(2):
                        for f in range(22):
                            MM(psD[hf][:], aT[:, f, s * 128:(s + 1) * 128], Wd[:, f, hf * 512:(hf + 1) * 512], f == 0, f == 21,
                               [f"aT{f}", f"Wd{f}"], [f"psD{hf}"])
                    xo_ = xo[s % 2]
                    xok = f"xo4{s % 2}"
                    post_norm_residual(psD, ["psD0", "psD1"], xt[:], xk, gfpost, "gfpost", xo_[:], xok, tmp, ssq2, None)
                    DMA("pool", x2_d[r0:r0 + 128, :], xo_[:], [xok], [])

            prep4(0)
            for b in range(NB4):
                gu4(b)
                if b + 1 < NB4:
                    prep4(b + 1)
                down4(b)
            SC.emit()

        ffn_es.close()

        with ExitStack() as es:
            def sb(name, shape, dt=F32):
                return es.enter_context(nc.sbuf_tensor("sb_" + name, list(shape), dt))
            Wpg = sb("Wpg", [128, 8, D], BF16)
            Wpp = sb("Wpp", [128, 2, D], BF16)
            bple = sb("bple", [128, D])
            x2t = [sb(f"x2t{i}", [128, D]) for i in range(3)]
            pt_ = [sb(f"ptl{i}", [128, 256]) for i in range(3)]
            x2Ts = [sb(f"x2T{i}", [128, 8, 128], BF16) for i in range(2)]
            pTs = [sb(f"pT{i}", [128, 2, 128], BF16) for i in range(2)]
            tgs = [sb(f"tg{i}", [128, D]) for i in range(2)]
            og = [sb(f"og{i}", [128, D]) for i in range(2)]
            psTt = [es.enter_context(nc.psum_tensor(f"psTc{i}", [128, 512], F32)) for i in range(3)]
            psGp = [es.enter_context(nc.psum_tensor(f"psGp{i}", [128, 512], F32)) for i in range(2)]
            psPp = [es.enter_context(nc.psum_tensor(f"psPp{i}", [128, 512], F32)) for i in range(2)]
            load_bf16_w(Wpg, wpg_d, 8, "Wpg")
            load_bf16_w(Wpp, wpp_d, 2, "Wpp")
            DMA("sp", bple[:], gains_d[4], (), ["bple"])
            def prep_c(i_):
                r0 = i_ * 128
                xt = x2t[i_ % 3]
                xk = f"x2t{i_ % 3}"
                pl = pt_[i_ % 3]
                plk = f"ptl{i_ % 3}"
                x2T, x2Tk = x2Ts[i_ % 2], f"x2T{i_ % 2}"
                pT, pTk = pTs[i_ % 2], f"pT{i_ % 2}"
                DMA("sp", xt[:], x2_d[r0:r0 + 128, :], (), [xk])
                DMA("sp", pl[:], p_d[r0:r0 + 128, :], (), [plk])
                for c in range(8):
                    TR(psTt[c // 4][:, (c % 4) * 128:(c % 4 + 1) * 128], xt[:, c * 128:(c + 1) * 128], ident[:], [xk, "ident"],
                       [f"psTc{c // 4}"])
                for c in range(2):
                    TR(psTt[2][:, c * 128:(c + 1) * 128], pl[:, c * 128:(c + 1) * 128], ident[:], [plk, "ident"], ["psTc2"])
                for hh in range(2):
                    CP(x2T[:, hh * 4:(hh + 1) * 4, :], psTt[hh][:].rearrange("p (c t) -> p c t", c=4), [f"psTc{hh}"], [x2Tk])
                ACT(pT[:, :, :], psTt[2][:, 0:256].rearrange("p (c t) -> p c t", c=2), AF.Identity, ["psTc2"], [pTk])

            def main_c(i_):
                r0 = i_ * 128
                xt = x2t[i_ % 3]
                xk = f"x2t{i_ % 3}"
                x2T, x2Tk = x2Ts[i_ % 2], f"x2T{i_ % 2}"
                pT, pTk = pTs[i_ % 2], f"pT{i_ % 2}"
                tg, tgk = tgs[i_ % 2], f"tg{i_ % 2}_"
                o_ = og[i_ % 2]
                ok_ = f"og{i_ % 2}"
                for hf in range(2):
                    for k in range(8):
                        MM(psGp[hf][:], x2T[:, k, :], Wpg[:, k, hf * 512:(hf + 1) * 512], k == 0, k == 7, [x2Tk, f"Wpg{k}"],
                           [f"psGp{hf}"])
                    for k in range(2):
                        MM(psPp[hf][:], pT[:, k, :], Wpp[:, k, hf * 512:(hf + 1) * 512], k == 0, k == 1, [pTk, f"Wpp{k}"],
                           [f"psPp{hf}"])
                    cs = slice(hf * 512, (hf + 1) * 512)
                    TT(tg[:, cs], psGp[hf][:], bple[:, cs], ALU.add, [f"psGp{hf}", "bple"], [tgk + str(hf)])
                    ACT(tg[:, cs], tg[:, cs], AF.Sigmoid, [tgk + str(hf)], [tgk + str(hf)])
                    TT(tg[:, cs], tg[:, cs], psPp[hf][:], ALU.mult, [tgk + str(hf), f"psPp{hf}"], [tgk + str(hf)])
                    TT(o_[:, cs], tg[:, cs], xt[:, cs], ALU.add, [tgk + str(hf), xk], [ok_ + str(hf)], eng="pool")
                DMA("pool", out_d[r0:r0 + 128, :], o_[:], [ok_ + "0", ok_ + "1"], [])

            NI = S // 128
            prep_c(0)
            for i_ in range(NI):
                if i_ + 1 < NI:
                    prep_c(i_ + 1)
                main_c(i_)
            SC.emit()
    nc._n_ops = SC.total
    return nc


def prep_shared(inp, S):
    f = lambda a: np.ascontiguousarray(np.asarray(a, dtype=np.float32))
    m = {}
    m["w_in"] = f(inp["w_in"][0])
    m["w_out"] = f(inp["w_out"][0])
    m["wgu"] = f(inp["ffn_w_gate_up"][0])
    m["wd"] = f(inp["ffn_w_down"][0])
    m["wpp"] = f(inp["ple_w_proj"][0])
    m["wpg"] = f(inp["ple_w_gate"][0])
    gains = np.stack([np.broadcast_to(np.asarray(inp[k][0], np.float32)[None, :], (128, D)) for k in
                      ("norm_mix_pre", "norm_mix_post", "norm_ffn_pre", "norm_ffn_post", "ple_b_gate")])
    m["gains"] = f(gains)
    cols = []
    cw = np.asarray(inp["conv_w"][0], np.float32)
    for k in range(4):
        cols.append(cw[k].reshape(8, 128).T)
    for key in ("conv_b", "lru_ba", "lru_bx", "lru_lambda"):
        cols.append(np.asarray(inp[key][0], np.float32).reshape(8, 128).T)
    m["pv"] = f(np.concatenate(cols, axis=1))
    for nm, key in (("bda", "lru_wa"), ("bdx", "lru_wx")):
        wsrc = np.asarray(inp[key][0], np.float32)
        bd = np.zeros((128, 8, 128), np.float32)
        for c in range(8):
            bd[0:64, c, 0:64] = wsrc[2 * c]
            bd[64:128, c, 64:128] = wsrc[2 * c + 1]
        m[nm] = bd
    m["ckw1"] = f(inp["cmp_k_w1"][0])
    m["cvw1"] = f(inp["cmp_v_w1"][0])
    m["ckw2"] = f(inp["cmp_k_w2"][0])
    m["cvw2"] = f(inp["cmp_v_w2"][0])
    m["posTk"] = f(np.asarray(inp["cmp_pos_k"][0], np.float32).reshape(16, 128).T)
    m["posTv"] = f(np.asarray(inp["cmp_pos_v"][0], np.float32).reshape(16, 128).T)
    m.update(make_consts(S))
    return m


_NC_CACHE = {}


def kernel(**inputs):
    S = 8192
    x = np.asarray(inputs["x"], np.float32)
    p = np.asarray(inputs["p"], np.float32)
    B = x.shape[0]
    if S not in _NC_CACHE:
        _NC_CACHE[S] = build(S)
    nc = _NC_CACHE[S]
    shared = prep_shared(inputs, S)
    in_maps = []
    for b in range(B):
        m = dict(shared)
        m["x"] = np.ascontiguousarray(x[b])
        m["p"] = np.ascontiguousarray(p[0, b])
        in_maps.append(m)
    res = run_bass_kernel_spmd(nc, in_maps, core_ids=list(range(B)))
    return np.stack([np.asarray(r["out"], np.float32) for r in res.results], axis=0)
```
